# Optimizing a Trainium2 kernel written in Bass

```python
import math
import jax, jax.numpy as jnp
from jax import lax
import numpy as np

D_MODEL = 1024
BATCH = 2
SEQ = 8192
DEPTH = 1
DEC_BATCH = 128
DEC_SEQ = 1
PAST_LEN = 16384
PAGE_SIZE = 128

HEAD_DIM = 64
N_HEADS_ATTN = 8
N_KV_HEADS = 2
GQA_GROUP = N_HEADS_ATTN // N_KV_HEADS
WINDOW = 128
BLOCK = WINDOW
ATTN_WIDTH = N_HEADS_ATTN * HEAD_DIM
KV_WIDTH = N_KV_HEADS * HEAD_DIM
N_HEADS_GMLP = 8
GMLP_WIDTH = N_HEADS_GMLP * HEAD_DIM
CHUNK = 128
MIX_WIDTH = ATTN_WIDTH + GMLP_WIDTH
IN_WIDTH = ATTN_WIDTH + 2 * KV_WIDTH + 2 * GMLP_WIDTH
SPLITS = [ATTN_WIDTH, ATTN_WIDTH + KV_WIDTH, ATTN_WIDTH + 2 * KV_WIDTH, ATTN_WIDTH + 2 * KV_WIDTH + GMLP_WIDTH]
NUM_BUCKETS = 32
MAX_DISTANCE = 128
D_FF = 2816
CONV_WIDTH = 3
EPS = 1e-6
NEG_INF = -1e30

kernel_name = "hymba_gmlp_swa_sink_convffn_step"


def _rms_norm(x, g):
    xf = x.astype(jnp.float32)
    y = xf * lax.rsqrt(jnp.mean(xf * xf, axis=-1, keepdims=True) + EPS)
    return (y * g.astype(jnp.float32)).astype(x.dtype)


def _layer_norm(x, g, b):
    xf = x.astype(jnp.float32)
    mu = jnp.mean(xf, axis=-1, keepdims=True)
    xc = xf - mu
    y = xc * lax.rsqrt(jnp.mean(xc * xc, axis=-1, keepdims=True) + EPS)
    return (y * g.astype(jnp.float32) + b.astype(jnp.float32)).astype(x.dtype)


def _t5_bucket(dist):
    n = jnp.maximum(dist, 0)
    max_exact = NUM_BUCKETS // 2
    nf = jnp.maximum(n, 1).astype(jnp.float32)
    large = max_exact + (jnp.log(nf / max_exact) / math.log(MAX_DISTANCE / max_exact)
                         * (NUM_BUCKETS - max_exact)).astype(jnp.int32)
    large = jnp.minimum(large, NUM_BUCKETS - 1)
    return jnp.where(n < max_exact, n, large)


def _sink_window_attn(q, k, v, dist, key_valid, sinks, rel_bias):
    N, Lq = q.shape[0], q.shape[1]
    Lk = k.shape[1]
    qg = q.reshape(N, Lq, N_KV_HEADS, GQA_GROUP, HEAD_DIM)
    logits = jnp.einsum('nqkgd,nskd->nkgqs', qg, k).astype(jnp.float32) * (HEAD_DIM ** -0.5)
    bias = rel_bias[_t5_bucket(dist)].astype(jnp.float32)
    bias = bias.transpose(2, 0, 1).reshape(N_KV_HEADS, GQA_GROUP, Lq, Lk)
    allowed = ((dist >= 0) & (dist <= WINDOW))[None] & key_valid[:, None, :]
    logits = jnp.where(allowed[:, None, None], logits + bias, NEG_INF)
    sink = jnp.broadcast_to(sinks.astype(jnp.float32).reshape(N_KV_HEADS, GQA_GROUP, 1, 1),
                            (N, N_KV_HEADS, GQA_GROUP, Lq, 1))
    probs = jax.nn.softmax(jnp.concatenate([logits, sink], axis=-1), axis=-1)[..., :-1]
    out = jnp.einsum('nkgqs,nskd->nqkgd', probs.astype(v.dtype), v)
    return out.reshape(N, Lq, ATTN_WIDTH)


def _attn_prompt(q, k, v, sinks, rel_bias):
    B, L = q.shape[0], q.shape[1]
    nb = L // BLOCK
    qb = q.reshape(B * nb, BLOCK, N_HEADS_ATTN, HEAD_DIM)

    def band(t):
        tb = t.reshape(B, nb, BLOCK, N_KV_HEADS, HEAD_DIM)
        prev = jnp.concatenate([jnp.zeros_like(tb[:, :1]), tb[:, :-1]], axis=1)
        return jnp.concatenate([prev, tb], axis=2).reshape(B * nb, 2 * BLOCK, N_KV_HEADS, HEAD_DIM)

    r = jnp.arange(BLOCK)[:, None]
    c = jnp.arange(2 * BLOCK)[None, :]
    dist = r + BLOCK - c
    blk = jnp.tile(jnp.arange(nb), B)
    key_valid = (blk[:, None] > 0) | (c >= BLOCK)
    out = _sink_window_attn(qb, band(k), band(v), dist, key_valid, sinks, rel_bias)
    return out.reshape(B, L, ATTN_WIDTH)


def _attn_sample(q, k, v, win_k, win_v, sinks, rel_bias):
    N, Ld = q.shape[0], q.shape[1]
    wb = win_k.shape[1]
    kk = jnp.concatenate([win_k, k], axis=1)
    vv = jnp.concatenate([win_v, v], axis=1)
    dist = jnp.arange(Ld)[:, None] + wb - jnp.arange(wb + Ld)[None, :]
    key_valid = jnp.ones((N, wb + Ld), dtype=bool)
    out = _sink_window_attn(q, kk, vv, dist, key_valid, sinks, rel_bias)
    return out, kk[:, -wb:], vv[:, -wb:]


def _gmlp_spatial(u, vn, w_s, b_s):
    N, L = u.shape[0], u.shape[1]
    Lp = -(-L // CHUNK) * CHUNK
    vc = jnp.pad(vn, ((0, 0), (0, Lp - L), (0, 0))).reshape(N, Lp // CHUNK, CHUNK, N_HEADS_GMLP, HEAD_DIM)
    w = w_s * jnp.tril(jnp.ones((CHUNK, CHUNK), w_s.dtype))
    mixed = jnp.einsum('hts,ncshd->ncthd', w, vc) + b_s.T[:, :, None]
    mixed = mixed.reshape(N, Lp, GMLP_WIDTH)[:, :L]
    return u * mixed


def _conv_ffn(h, prev, w_up, conv_w, conv_b, w_down):
    up = h @ w_up
    L = up.shape[1]
    ext = jnp.concatenate([prev.astype(up.dtype), up], axis=1)
    conv = conv_b + sum(conv_w[i] * ext[:, i:i + L] for i in range(CONV_WIDTH))
    gate, val = jnp.split(conv, 2, axis=-1)
    y = (jax.nn.gelu(gate, approximate=True) * val) @ w_down
    return y, ext[:, -(CONV_WIDTH - 1):]


def _layer(x, win_k, win_v, conv_prev, rel_bias, w_in, b_in, sinks, ln_g, ln_b, w_s, b_s,
           g_attn_out, g_gmlp_out, w_out, g_pre_mix, g_post_mix, g_pre_ffn, g_post_ffn,
           w_up, conv_w, conv_b, w_down):
    N, L = x.shape[0], x.shape[1]
    xn = _rms_norm(x, g_pre_mix)
    proj = xn @ w_in + b_in
    q, k, v, u, vg = jnp.split(proj, SPLITS, axis=-1)
    q = q.reshape(N, L, N_HEADS_ATTN, HEAD_DIM)
    k = k.reshape(N, L, N_KV_HEADS, HEAD_DIM)
    v = v.reshape(N, L, N_KV_HEADS, HEAD_DIM)
    if win_k is None:
        attn = _attn_prompt(q, k, v, sinks, rel_bias)
        new_k, new_v = k[:, -WINDOW:], v[:, -WINDOW:]
        conv_prev = jnp.zeros((N, CONV_WIDTH - 1, 2 * D_FF), x.dtype)
    else:
        attn, new_k, new_v = _attn_sample(q, k, v, win_k, win_v, sinks, rel_bias)
    u = jax.nn.gelu(u, approximate=False)
    vn = _layer_norm(jax.nn.gelu(vg, approximate=False), ln_g, ln_b)
    gm = _gmlp_spatial(u, vn, w_s, b_s)
    mixed = jnp.concatenate([_rms_norm(attn, g_attn_out), _rms_norm(gm, g_gmlp_out)], axis=-1) @ w_out
    h = x + _rms_norm(mixed, g_post_mix)
    f, new_conv = _conv_ffn(_rms_norm(h, g_pre_ffn), conv_prev, w_up, conv_w, conv_b, w_down)
    y = h + _rms_norm(f, g_post_ffn)
    return y, new_k, new_v, vn, new_conv


def setup_inputs(seed: int = 0) -> dict:
    key = jax.random.key(seed)
    ks = jax.random.split(key, 32)

    def nrm(k, shape, scale):
        return jax.random.normal(k, shape, jnp.float32) * scale

    wb = min(WINDOW, PAST_LEN)
    f2 = 2 * D_FF
    return {
        "x_prompt": nrm(ks[0], (BATCH, SEQ, D_MODEL), 1.0),
        "x_sample": nrm(ks[1], (DEC_BATCH, DEC_SEQ, D_MODEL), 1.0),
        "cache_win_k": nrm(ks[2], (DEPTH, DEC_BATCH, wb, N_KV_HEADS, HEAD_DIM), 1.0),
        "cache_win_v": nrm(ks[3], (DEPTH, DEC_BATCH, wb, N_KV_HEADS, HEAD_DIM), 1.0),
        "state_ffn_conv": nrm(ks[4], (DEPTH, DEC_BATCH, CONV_WIDTH - 1, f2), 1.0),
        "rel_bias": nrm(ks[5], (NUM_BUCKETS, N_HEADS_ATTN), 0.5),
        "w_in": nrm(ks[6], (DEPTH, D_MODEL, IN_WIDTH), D_MODEL ** -0.5),
        "b_in": nrm(ks[7], (DEPTH, IN_WIDTH), 0.02),
        "attn_sinks": nrm(ks[8], (DEPTH, N_HEADS_ATTN), 1.0),
        "gmlp_ln_g": 1.0 + nrm(ks[9], (DEPTH, GMLP_WIDTH), 0.02),
        "gmlp_ln_b": nrm(ks[10], (DEPTH, GMLP_WIDTH), 0.02),
        "gmlp_w_s": nrm(ks[11], (DEPTH, N_HEADS_GMLP, CHUNK, CHUNK), CHUNK ** -0.5),
        "gmlp_b_s": 1.0 + nrm(ks[12], (DEPTH, N_HEADS_GMLP, CHUNK), 0.02),
        "g_attn_out": 1.0 + nrm(ks[13], (DEPTH, ATTN_WIDTH), 0.02),
        "g_gmlp_out": 1.0 + nrm(ks[14], (DEPTH, GMLP_WIDTH), 0.02),
        "w_out": nrm(ks[15], (DEPTH, MIX_WIDTH, D_MODEL), MIX_WIDTH ** -0.5),
        "g_pre_mix": 1.0 + nrm(ks[16], (DEPTH, D_MODEL), 0.02),
        "g_post_mix": 1.0 + nrm(ks[17], (DEPTH, D_MODEL), 0.02),
        "g_pre_ffn": 1.0 + nrm(ks[18], (DEPTH, D_MODEL), 0.02),
        "g_post_ffn": 1.0 + nrm(ks[19], (DEPTH, D_MODEL), 0.02),
        "w_up": nrm(ks[20], (DEPTH, D_MODEL, f2), D_MODEL ** -0.5),
        "ffn_conv_w": nrm(ks[21], (DEPTH, CONV_WIDTH, f2), CONV_WIDTH ** -0.5),
        "ffn_conv_b": nrm(ks[22], (DEPTH, f2), 0.02),
        "w_down": nrm(ks[23], (DEPTH, D_FF, D_MODEL), D_FF ** -0.5),
    }


def reference(x_prompt, x_sample, cache_win_k, cache_win_v, state_ffn_conv, rel_bias,
              w_in, b_in, attn_sinks, gmlp_ln_g, gmlp_ln_b, gmlp_w_s, gmlp_b_s,
              g_attn_out, g_gmlp_out, w_out, g_pre_mix, g_post_mix, g_pre_ffn, g_post_ffn,
              w_up, ffn_conv_w, ffn_conv_b, w_down):
    yp, ys = x_prompt, x_sample
    kp_l, vp_l, ks_l, vs_l, gv_l, cp_l, cs_l = [], [], [], [], [], [], []
    for l in range(DEPTH):
        lw = (w_in[l], b_in[l], attn_sinks[l], gmlp_ln_g[l], gmlp_ln_b[l], gmlp_w_s[l], gmlp_b_s[l],
              g_attn_out[l], g_gmlp_out[l], w_out[l], g_pre_mix[l], g_post_mix[l], g_pre_ffn[l],
              g_post_ffn[l], w_up[l], ffn_conv_w[l], ffn_conv_b[l], w_down[l])
        yp, kp, vp, _, cp = _layer(yp, None, None, None, rel_bias, *lw)
        ys, ksm, vsm, gvs, csm = _layer(ys, cache_win_k[l], cache_win_v[l], state_ffn_conv[l], rel_bias, *lw)
        kp_l.append(kp); vp_l.append(vp); cp_l.append(cp)
        ks_l.append(ksm); vs_l.append(vsm); gv_l.append(gvs); cs_l.append(csm)
    win_k_prompt = jnp.stack(kp_l)
    win_v_prompt = jnp.stack(vp_l)
    win_k_sample = jnp.stack(ks_l)
    win_v_sample = jnp.stack(vs_l)
    gmlp_v_sample = jnp.stack(gv_l)
    ffn_conv_prompt = jnp.stack(cp_l)
    ffn_conv_sample = jnp.stack(cs_l)
    return (yp, ys, win_k_prompt, win_v_prompt, win_k_sample, win_v_sample, gmlp_v_sample, ffn_conv_prompt, ffn_conv_sample)
```

```python
from contextlib import ExitStack
import numpy as np
import concourse.bass as bass
import concourse.mybir as mybir
from concourse.bass_utils import run_bass_kernel_spmd

F32 = mybir.dt.float32
BF16 = mybir.dt.bfloat16
AF = mybir.ActivationFunctionType
ALU = mybir.AluOpType
AX = mybir.AxisListType

NCORES = 8
D = 1024
NBLK = 18
TOK = 2048
NS = 16
DFF = 2816
F2 = 5632
NFT = 44
EPS = 1e-6
NEG = -30000.0
NCB = 5
NUX = 3
MAXPH = 3
MAXJ = 99
DBG = 0
NWS = 4


class Prog:
    def __init__(self, nc, es):
        self.nc, self.es = nc, es
        self.ops = []
        self.lastw, self.readers = {}, {}
        self.sems, self.count = {}, {}
        self.waited = {e: {} for e in ("pe", "act", "dve", "pool", "sp")}
        self.rec = None

    def record(self, f):
        self.rec = []
        f()
        ops, self.rec = self.rec, None
        return ops

    def merge(self, a, b):
        ia = ib = 0
        while ia < len(a) or ib < len(b):
            if ib >= len(b) or (ia < len(a) and ia * len(b) <= ib * len(a)):
                self.add(*a[ia]); ia += 1
            else:
                self.add(*b[ib]); ib += 1

    @staticmethod
    def _estimate(op):
        eng, fn = op[0], op[1]
        rec = []

        class _Stub:
            def __getattr__(self, name):
                def call(*a, **kw):
                    out = kw.get("out", a[0] if a else None)
                    n = 1
                    try:
                        for d in out.shape[1:]:
                            n *= int(d)
                    except Exception:
                        n = 512
                    rec.append((name, n))
                    return self
                return call
        try:
            fn(_Stub())
        except Exception:
            return 1.0
        t = 0.0
        for name, n in rec:
            if eng == "pe":
                t += 0.03 + n / 1200.0
            elif name == "dma_start":
                t += 2.0
            else:
                t += 0.12 + n / 1000.0
        return max(t, 0.05)

    def merge_sched(self, *streams):
        streams = [st for st in streams if st]
        isbank = lambda k: isinstance(k, str) and len(k) == 2 and k[0] == "B" and k[1].isdigit()
        deps, est = [], []
        for st in streams:
            lastw, readers, dl = {}, {}, []
            for i, (eng, fn, r, w, dma) in enumerate(st):
                w2 = list(w) + [k for k in r if isbank(k)]
                d = set()
                for k in r:
                    if k in lastw:
                        d.add(lastw[k])
                for k in w2:
                    if k in lastw:
                        d.add(lastw[k])
                    d.update(readers.get(k, ()))
                for k in r:
                    readers.setdefault(k, []).append(i)
                for k in w2:
                    lastw[k] = i
                    readers[k] = []
                dl.append(d)
            deps.append(dl)
            est.append([self._estimate(op) for op in st])
        free = {}
        fin = [dict() for _ in streams]
        ptr = [0] * len(streams)
        while True:
            best = None
            for s_, st in enumerate(streams):
                i = ptr[s_]
                if i >= len(st):
                    continue
                eng = st[i][0]
                ready = max([fin[s_][d] for d in deps[s_][i]] + [0.0])
                start = max(ready, free.get(eng, 0.0))
                key = (start, i / len(st))
                if best is None or key < best[0]:
                    best = (key, s_, i, eng, start)
            if best is None:
                break
            _, s_, i, eng, start = best
            end = start + est[s_][i] + 0.15
            fin[s_][i] = end
            free[eng] = end
            ptr[s_] += 1
            self.add(*streams[s_][i])

    def sem(self, name):
        if name not in self.sems:
            self.sems[name] = self.es.enter_context(self.nc.semaphore(name))
            self.count[name] = 0
        return self.sems[name]

    def add(self, eng, fn, r=(), w=(), dma=None):
        if self.rec is not None:
            self.rec.append((eng, fn, tuple(r), tuple(w), dma))
            return
        w = list(w) + [k for k in r if isinstance(k, str) and len(k) == 2 and k[0] == "B" and k[1].isdigit()]
        deps = set()
        for k in r:
            if k in self.lastw:
                deps.add(self.lastw[k])
        for k in w:
            if k in self.lastw:
                deps.add(self.lastw[k])
            for t in self.readers.get(k, ()):
                deps.add(t)
        deps = set((sn, self.count[sn]) if sn.startswith("d_") else (sn, v) for (sn, v) in deps)
        if dma is None:
            sname, inc = "c_" + eng, 1
        else:
            sname, inc = "d_" + dma, 16
        self.sem(sname)
        self.count[sname] += inc
        tok = (sname, self.count[sname])
        for k in r:
            self.readers.setdefault(k, []).append(tok)
        for k in w:
            self.lastw[k] = tok
            self.readers[k] = []
        self.ops.append((eng, fn, deps, tok, inc))

    def emit(self):
        with self.nc.Block() as block:
            for eng, deco in (("pe", block.tensor), ("act", block.scalar), ("dve", block.vector),
                              ("pool", block.gpsimd), ("sp", block.sync)):
                ops = [o for o in self.ops if o[0] == eng]

                def body(e, ops=ops, eng=eng):
                    wt = self.waited[eng]
                    for (_, fn, deps, tok, inc) in ops:
                        need = {}
                        for (s, v) in deps:
                            need[s] = max(need.get(s, 0), v)
                        for s, v in need.items():
                            if wt.get(s, 0) < v:
                                e.wait_ge(self.sems[s], v)
                                wt[s] = v
                        ins = fn(e)
                        ins.then_inc(self.sems[tok[0]], inc)
                    if eng == "sp":
                        for s, v in self.count.items():
                            if s.startswith("d_") and wt.get(s, 0) < v:
                                e.wait_ge(self.sems[s], v)
                                wt[s] = v

                deco(body)
        self.ops = []


def build_program():
    nc = bass.Bass("TRN2", target_bir_lowering=False)

    def din(name, shape):
        return nc.dram_tensor(name, list(shape), F32, kind="ExternalInput")

    def dout(name, shape):
        return nc.dram_tensor(name, list(shape), F32, kind="ExternalOutput")

    xh = din("xh", (NBLK * 128, D))
    xs = din("xs", (NS, D))
    ck = din("ck", (NS, 128, 128))
    cv = din("cv", (NS, 128, 128))
    stt = din("st", (NS, 2, F2))
    stp = din("stp", (NS, 2, F2))
    flag = din("flag", (128, 1))
    maskc = din("maskc", (128, 1))
    rel_bias = din("rel_bias", (32, 8))
    w_in = din("w_in", (D, 1792))
    b_in = din("b_in", (1, 1792))
    sinks = din("sinks", (1, 8))
    sinks_p = din("sinks_p", (1, 8))
    ln_g = din("ln_g", (1, 512))
    ln_b = din("ln_b", (1, 512))
    w_s = din("w_s", (8, 128, 128))
    b_s = din("b_s", (8, 128))
    g_ao = din("g_ao", (1, 1024))
    w_out = din("w_out", (D, D))
    g_pre_mix = din("g_pre_mix", (1, D))
    g_post_mix = din("g_post_mix", (1, D))
    g_pre_ffn = din("g_pre_ffn", (1, D))
    g_post_ffn = din("g_post_ffn", (1, D))
    w_up = din("w_up", (D, F2))
    cwl_d = din("cwl", (NFT, 4, 128))
    cwr_d = din("cwr", (4, F2))
    w_down = din("w_down", (DFF, D))
    ident = din("ident", (128, 128))
    oh = din("oh", (32, 384))
    oh2 = din("oh2", (32, 129))
    mv = din("mv", (8, 384))
    trilT = din("trilT", (128, 128))

    y = dout("y", (TOK, D))
    ys = dout("ys", (NS, D))
    wkv = dout("wkv", (128, 256))
    wks = dout("wks", (NS, 128, 128))
    wvs = dout("wvs", (NS, 128, 128))
    gv = dout("gv", (NS, 512))
    cp = dout("cp", (88, 128))
    cs = dout("cs", (NS, 2, F2))

    scr = nc.dram_tensor("scr", [8, 128, 384], F32)
    scr2 = nc.dram_tensor("scr2", [8, 129], F32)
    qscr = nc.dram_tensor("qscr", [NS, 512], F32)
    kvscr = nc.dram_tensor("kvscr", [NS, 256], F32)
    ascr = nc.dram_tensor("ascr", [128, 64], F32)

    es = ExitStack()
    with es:
        P = Prog(nc, es)

        def sb(stack, name, shape, dt=F32):
            return stack.enter_context(nc.sbuf_tensor(name, list(shape), dt))

        def dma(eng, out, in_, r, w, key, slow=False):
            if slow:
                P.add(eng, lambda e: e.dma_start(out=out, in_=in_, allow_slow_non_contiguous=True), r, w, dma=key)
            else:
                P.add(eng, lambda e: e.dma_start(out=out, in_=in_), r, w, dma=key)

        h_all = sb(es, "h_all", (128, 17, D))
        h_s = sb(es, "h_s", (128, D))
        identb = sb(es, "identb", (128, 128), BF16)
        banks = [es.enter_context(nc.psum_tensor("bank%d" % i, [128, 1024], BF16)) for i in range(2)]
        for i in range(2, 8):
            banks.append(es.enter_context(nc.psum_tensor("bank%d" % i, [128, 512], F32)))
        B = banks
        junks = [sb(es, "junk0", (128, D), BF16)]
        junk = junks[0]
        ssA = sb(es, "ssA", (128, 32))
        ssB = sb(es, "ssB", (128, 8))
        onesr = sb(es, "onesr", (1, 128), BF16)
        epsb = sb(es, "epsb", (128, 1))

        def rms_rstd(srcs, n, tag, np_=128, slot=0, jk=0):
            st_ = ssA
            o = 4 * slot
            kk = lambda nm: ("st", slot, nm)
            for i, (ap, rk) in enumerate(srcs):
                c = st_[0:np_, o + i:o + i + 1]
                jv = junks[jk][0:np_, 0:ap.shape[-1]] if len(ap.shape) == 2 else junks[jk][0:np_, 0:512]
                P.add("act", lambda e, ap=ap, c=c, jv=jv: e.activation(out=jv, in_=ap, func=AF.Square, accum_out=c),
                      r=rk, w=[kk("ss%d" % i), "junk%d" % jk])
            if len(srcs) == 2:
                P.add("dve", lambda e: e.tensor_tensor(out=st_[0:np_, o:o + 1], in0=st_[0:np_, o:o + 1], in1=st_[0:np_, o + 1:o + 2], op=ALU.add),
                      r=[kk("ss0"), kk("ss1")], w=[kk("ss0")])
            P.add("act", lambda e: e.activation(out=st_[0:np_, o + 2:o + 3], in_=st_[0:np_, o:o + 1], func=AF.Ln, scale=1.0 / n, bias=epsb[0:np_, :]),
                  r=[kk("ss0"), "epsb"], w=[kk("std")])
            P.add("act", lambda e: e.activation(out=st_[0:np_, o + 3:o + 4], in_=st_[0:np_, o + 2:o + 3], func=AF.Exp, scale=-0.5),
                  r=[kk("std")], w=[kk("rstd")])
            return st_[0:np_, o + 3:o + 4], kk("rstd")

        sAB = ExitStack()
        with sAB:
            W_in = sb(sAB, "W_in", (128, 8, 1792), BF16)
            W_out = sb(sAB, "W_out", (128, 8, D), BF16)
            junks.append(sb(sAB, "junk1", (128, D), BF16))
            gpo = sb(sAB, "gpo", (128, D))
            lng = sb(sAB, "lng", (128, 512))
            lnb = sb(sAB, "lnb", (128, 512))
            bqk = sb(sAB, "bqk", (128, 5))
            bq8 = sb(sAB, "bq8", (128, 4))
            gT = sb(sAB, "gT", (128, 16))
            brow = sb(sAB, "brow", (1, 1792), BF16)

            class Cx:
                pass
            CX = []
            for t in range(2):
                cx = Cx()
                cx.t = t
                cx.xn = sb(sAB, "xn%d" % t, (128, D), BF16)
                cx.xnT = sb(sAB, "xnT%d" % t, (128, 8, 128), BF16)
                cx.ug = sb(sAB, "ug%d" % t, (128, 512))
                cx.vgg = sb(sAB, "vgg%d" % t, (128, 512))
                cx.vnb = sb(sAB, "vnb%d" % t, (128, 512), BF16)
                cx.t1 = sb(sAB, "t1%d" % t, (128, 512))
                cx.gm = sb(sAB, "gm%d" % t, (128, 512))
                cx.attn = sb(sAB, "attn%d" % t, (128, 512))
                cx.amix = sb(sAB, "amix%d" % t, (128, D), BF16)
                cx.aT = sb(sAB, "aT%d" % t, (128, 8, 128), BF16)
                cx.lnst = sb(sAB, "lnst%d" % t, (128, 8))
                cx.den = sb(sAB, "den%d" % t, (128, 8))
                cx.T = B[t]
                cx.Tk = "B%d" % t
                cx.a, cx.b, cx.c = (B[2 + 3 * t], B[3 + 3 * t], B[4 + 3 * t])
                cx.ak, cx.bk, cx.ck = ("B%d" % (2 + 3 * t), "B%d" % (3 + 3 * t), "B%d" % (4 + 3 * t))
                cx.k = (lambda nm, t=t: (nm, "cx", t))
                CX.append(cx)
            vn = CX[1].vgg

            dma("pool", identb[:], ident.ap(), [], ["identb"], "cstp")
            w_in_v = w_in.ap().rearrange("(kc p) n -> p kc n", p=128)
            for i in range(4):
                dma("pool", W_in[:, 2 * i:2 * i + 2, :], w_in_v[:, 2 * i:2 * i + 2, :], [], ["W_in%d" % i], "W_in")
            WIN = ["W_in%d" % i for i in range(4)]
            w_out_v = w_out.ap().rearrange("(kc p) n -> p kc n", p=128)
            for i in range(2):
                dma("pool", W_out[:, 4 * i:4 * i + 4, :], w_out_v[:, 4 * i:4 * i + 4, :], [], ["W_out%d" % i], "W_out")
            WOUT = ["W_out0", "W_out1"]
            dma("pool", brow[:], b_in.ap(), [], ["brow"], "cstp")
            dma("sp", h_s[0:NS, :], xs.ap(), [], ["h_s"], "h_s")
            dma("sp", gT[:, 0:8], g_pre_mix.ap()[0, :].rearrange("(c p) -> p c", p=128), [], ["gT"], "cst", slow=True)
            dma("sp", gT[:, 8:16], g_ao.ap()[0, :].rearrange("(c p) -> p c", p=128), [], ["gT"], "cst", slow=True)
            dma("sp", bqk[:], b_in.ap()[0, 0:640].rearrange("(t p) -> p t", p=128), [], ["bqk"], "cst", slow=True)
            dma("sp", gpo[:], g_post_mix.ap().partition_broadcast(128), [], ["gpo"], "cst")
            dma("sp", lng[:], ln_g.ap().partition_broadcast(128), [], ["lng"], "cst")
            dma("sp", lnb[:], ln_b.ap().partition_broadcast(128), [], ["lnb"], "cst")
            rb = sb(sAB, "rb", (32, 8))
            wsT = sb(sAB, "wsT", (128, 8, 128), BF16)
            bs_t = sb(sAB, "bs_t", (128, 8))
            esink = sb(sAB, "esink", (128, 8))
            biasT = sb(sAB, "biasT", (128, 2, 1024), BF16)
            bias1 = sb(sAB, "bias1", (128, 1024), BF16)
            ohs = sb(sAB, "ohs", (32, 384))
            mkc = sb(sAB, "mkc", (128, 1))
            oh2s = sb(sAB, "oh2s", (32, 129))
            Gs2 = sb(sAB, "Gs2", (8, 129))
            dma("sp", rb[:], rel_bias.ap(), [], ["rb"], "cst")
            dma("sp", oh2s[:], oh2.ap(), [], ["oh2s"], "cst")
            P.add("pe", lambda e: e.matmul(B[6][0:8, 0:129], lhsT=rb[:], rhs=oh2s[:], start=True, stop=True), ["rb", "oh2s"], ["B6"])
            P.add("dve", lambda e: e.tensor_copy(out=Gs2[:], in_=B[6][0:8, 0:129]), ["B6"], ["Gs2"])
            dma("sp", scr2.ap(), Gs2[:], ["Gs2"], ["scr2"], "scr2")
            P.add("dve", lambda e: e.memset(onesr[:], 1.0), [], ["onesr"])
            P.add("dve", lambda e: e.memset(epsb[:], EPS), [], ["epsb"])
            P.add("dve", lambda e: e.tensor_scalar(out=bq8[:], in0=bqk[:, 0:4], scalar1=0.125, scalar2=None, op0=ALU.mult),
                  ["bqk"], ["bq8"])
            for kc in range(8):
                P.add("dve", lambda e, kc=kc: e.tensor_scalar(out=W_in[:, kc, :], in0=W_in[:, kc, :], scalar1=gT[:, kc:kc + 1], scalar2=None, op0=ALU.mult),
                      ["gT", "W_in%d" % (kc // 2)], ["W_in%d" % (kc // 2)])
                P.add("dve", lambda e, kc=kc: e.tensor_scalar(out=W_out[:, kc, :], in0=W_out[:, kc, :], scalar1=gT[:, 8 + kc:9 + kc], scalar2=None, op0=ALU.mult),
                      ["gT", "W_out%d" % (kc // 4)], ["W_out%d" % (kc // 4)])

            def norm_T(cx, xap, xkeys, np_=128):
                rstd, rk_ = rms_rstd([(xap, xkeys)], D, "x", np_, slot=4 * cx.t, jk=cx.t)
                P.add("dve", lambda e: e.tensor_scalar(out=cx.xn[0:np_, :], in0=xap, scalar1=rstd, scalar2=None, op0=ALU.mult),
                      r=list(xkeys) + [rk_], w=[cx.k("xn")])

                def tr(e):
                    for c in range(8):
                        ins = e.transpose(out=cx.T[:, c * 128:c * 128 + np_], in_=cx.xn[0:np_, c * 128:(c + 1) * 128],
                                          identity=identb[0:np_, 0:np_])
                    return ins
                P.add("pe", tr, r=[cx.k("xn"), "identb"], w=[cx.Tk])
                Tv = cx.T[:].rearrange("p (c t) -> p c t", c=8)
                if np_ == 128:
                    P.add("dve", lambda e: e.tensor_copy(out=cx.xnT[:].rearrange("p c t -> p (c t)"), in_=cx.T[:, 0:1024]),
                          r=[cx.Tk], w=[cx.k("xnT")])
                else:
                    P.add("act", lambda e: e.activation(out=cx.xnT[:, :, 0:np_], in_=Tv[:, :, 0:np_], func=AF.Copy),
                          r=[cx.Tk], w=[cx.k("xnT")])

            def mm_tok(cx, bank, bkey, c0, c1, np_=128, o0=0):
                n = c1 - c0

                def f(e):
                    for kc in range(8):
                        e.matmul(bank[0:np_, o0:o0 + n], lhsT=cx.xnT[:, kc, 0:np_], rhs=W_in[:, kc, c0:c1], start=(kc == 0), stop=False)
                    return e.matmul(bank[0:np_, o0:o0 + n], lhsT=onesr[0:1, 0:np_], rhs=brow[0:1, c0:c1], start=False, stop=True)
                P.add("pe", f, r=[cx.k("xnT"), "brow", "onesr"] + WIN, w=[bkey])

            def gelu_u(cx, bank, bkey, np_=128):
                P.add("act", lambda e: e.activation(out=cx.ug[0:np_, :], in_=bank[0:np_, :], func=AF.Gelu), r=[bkey], w=[cx.k("ug")])

            def gelu_ln(cx, bank, bkey, np_=128, want_f32=False):
                vgg, lnst = cx.vgg, cx.lnst
                kv_, kl = cx.k("vgg"), (lambda i: cx.k("ln%d" % i))
                P.add("act", lambda e: e.activation(out=vgg[0:np_, :], in_=bank[0:np_, :], func=AF.Gelu), r=[bkey], w=[kv_])
                P.add("dve", lambda e: e.reduce_sum(out=lnst[0:np_, 0:1], in_=vgg[0:np_, :], axis=AX.X), r=[kv_], w=[kl(0)])
                P.add("dve", lambda e: e.tensor_scalar(out=lnst[0:np_, 1:2], in0=lnst[0:np_, 0:1], scalar1=-1.0 / 512, scalar2=None, op0=ALU.mult),
                      r=[kl(0)], w=[kl(1)])
                P.add("act", lambda e: e.activation(out=junks[cx.t][0:np_, 0:512], in_=vgg[0:np_, :], func=AF.Square, bias=lnst[0:np_, 1:2],
                                                    accum_out=lnst[0:np_, 2:3]), r=[kv_, kl(1)], w=[kl(2), "junk%d" % cx.t])
                P.add("act", lambda e: e.activation(out=lnst[0:np_, 3:4], in_=lnst[0:np_, 2:3], func=AF.Ln, scale=1.0 / 512, bias=epsb[0:np_, :]),
                      r=[kl(2), "epsb"], w=[kl(3)])
                P.add("act", lambda e: e.activation(out=lnst[0:np_, 4:5], in_=lnst[0:np_, 3:4], func=AF.Exp, scale=-0.5), r=[kl(3)], w=[kl(4)])
                P.add("dve", lambda e: e.tensor_scalar(out=vgg[0:np_, :], in0=vgg[0:np_, :], scalar1=lnst[0:np_, 1:2], scalar2=lnst[0:np_, 4:5],
                                                       op0=ALU.add, op1=ALU.mult), r=[kv_, kl(1), kl(4)], w=[kv_])
                P.add("dve", lambda e: e.tensor_tensor(out=vgg[0:np_, :], in0=vgg[0:np_, :], in1=lng[0:np_, :], op=ALU.mult),
                      r=[kv_, "lng"], w=[kv_])
                if want_f32:
                    P.add("dve", lambda e: e.tensor_tensor(out=vn[0:np_, :], in0=vgg[0:np_, :], in1=lnb[0:np_, :], op=ALU.add),
                          r=[kv_, "lnb"], w=["vn"])
                else:
                    P.add("dve", lambda e: e.tensor_tensor(out=cx.vnb[0:np_, :], in0=vgg[0:np_, :], in1=lnb[0:np_, :], op=ALU.add),
                          r=[kv_, "lnb"], w=[cx.k("vnb")])

            def out_proj(cx, xap, xkeys, hap, hkeys, np_=128, akeys=None):
                akeys = [cx.k("attn")] if akeys is None else list(akeys)
                attn, gm, amix, aT, t1 = cx.attn, cx.gm, cx.amix, cx.aT, cx.t1
                rstd, rk_ = rms_rstd([(attn[0:np_, :], akeys)], 512, "a", np_, slot=4 * cx.t + 1, jk=cx.t)
                P.add("dve", lambda e, rstd=rstd: e.tensor_scalar(out=amix[0:np_, 0:512], in0=attn[0:np_, :], scalar1=rstd, scalar2=None, op0=ALU.mult),
                      r=akeys + [rk_], w=[cx.k("amixa")])
                rstd, rk_ = rms_rstd([(gm[0:np_, :], [cx.k("gm")])], 512, "g", np_, slot=4 * cx.t + 2, jk=cx.t)
                P.add("dve", lambda e, rstd=rstd: e.tensor_scalar(out=amix[0:np_, 512:1024], in0=gm[0:np_, :], scalar1=rstd, scalar2=None, op0=ALU.mult),
                      r=[cx.k("gm"), rk_], w=[cx.k("amixg")])

                def tr(e):
                    for c in range(8):
                        ins = e.transpose(out=cx.T[:, c * 128:c * 128 + np_], in_=amix[0:np_, c * 128:(c + 1) * 128],
                                          identity=identb[0:np_, 0:np_])
                    return ins
                P.add("pe", tr, r=[cx.k("amixa"), cx.k("amixg"), "identb"], w=[cx.Tk])
                Tv = cx.T[:].rearrange("p (c t) -> p c t", c=8)
                if np_ == 128:
                    P.add("dve", lambda e: e.tensor_copy(out=aT[:].rearrange("p c t -> p (c t)"), in_=cx.T[:, 0:1024]), r=[cx.Tk], w=[cx.k("aT")])
                else:
                    P.add("act", lambda e: e.activation(out=aT[:, :, 0:np_], in_=Tv[:, :, 0:np_], func=AF.Copy), r=[cx.Tk], w=[cx.k("aT")])
                ob = ((0, cx.a, cx.ak), (1, cx.b, cx.bk))
                for half, bank, bkey in ob:
                    def f(e, half=half, bank=bank):
                        for kc in range(8):
                            ins = e.matmul(bank[0:np_, :], lhsT=aT[:, kc, 0:np_], rhs=W_out[:, kc, half * 512:(half + 1) * 512],
                                           start=(kc == 0), stop=(kc == 7))
                        return ins
                    P.add("pe", f, r=[cx.k("aT")] + WOUT, w=[bkey])
                rstd, rk_ = rms_rstd([(cx.a[0:np_, :], [cx.ak]), (cx.b[0:np_, :], [cx.bk])], D, "o", np_, slot=4 * cx.t + 3, jk=cx.t)
                for half, bank, bkey in ob:
                    sl = slice(half * 512, (half + 1) * 512)
                    P.add("dve", lambda e, bank=bank, sl=sl, rstd=rstd: e.scalar_tensor_tensor(out=t1[0:np_, :], in0=bank[0:np_, :], scalar=rstd,
                                                                                in1=gpo[0:np_, sl], op0=ALU.mult, op1=ALU.mult),
                          r=[bkey, rk_, "gpo"], w=[cx.k("t1")])
                    P.add("dve", lambda e, sl=sl: e.tensor_tensor(out=hap[:, sl], in0=t1[0:np_, :], in1=xap[:, sl], op=ALU.add),
                          r=[cx.k("t1")] + list(xkeys), w=list(hkeys))

            sB = ExitStack()
            with sB:
                Ks = h_all[:, 0:8, :].rearrange("p b (c d) -> p (b c) d", d=64)
                Vs = h_all[:, 8:16, :].rearrange("p b (c d) -> p (b c) d", d=64)
                knew = sb(sB, "knew", (128, 64))
                wsl = sb(sB, "wsl", (128, 8, 128), BF16)
                trl = sb(sB, "trl", (128, 128))
                mvb = sb(sB, "mvb", (128, 384))
                gbt = [sb(sB, "gbt%d" % i, (128, 384)) for i in range(2)]
                vnew = sb(sB, "vnew", (128, 64))
                qn = sb(sB, "qn", (128, 64))
                prod = sb(sB, "prod", (128, 32, 64))
                lg = sb(sB, "lg", (128, 129))
                fsb = sb(sB, "fsb", (128, 129))
                esp = sb(sB, "esp", (128, 1))
                dsum = sb(sB, "dsum", (128, 4))
                osum = sb(sB, "osum", (128, 5, 64))
                onh = sb(sB, "onh", (128, 64))
                qs_t = sb(sB, "qs_t", (NS, 512))
                kvs = sb(sB, "kvs", (NS, 256))
                ws00 = sb(sB, "ws00", (NS, 8))
                bs0 = sb(sB, "bs0", (NS, 8))

                dma("sp", ws00[:], bass.AP(w_s, 0, [[0, NS], [128 * 128, 8]]), [], ["ws00"], "cst", slow=True)
                dma("sp", bs0[:], bass.AP(b_s, 0, [[0, NS], [128, 8]]), [], ["bs0"], "cst", slow=True)
                for T_, src_, nm in ((Ks, ck, "Ks"), (None, None, None), (Vs, cv, "Vs")):
                    for hp in range(8):
                        g = hp % 2
                        ps = slice(hp, 128, 8)
                        head = (hp % 2) * 4 + hp // 2
                        if T_ is None:
                            dma("sp", fsb[ps, :], scr2.ap()[head:head + 1, :].partition_broadcast(16), ["scr2"], [("fsb", hp)], "fe")
                            dma("sp", esp[ps, :], sinks_p.ap()[0:1, hp:hp + 1].partition_broadcast(16), [], [("esp", hp)], "fe")
                        elif hp < 2:
                            dma("sp", T_[ps, :, :], src_.ap()[:, :, g * 64:(g + 1) * 64], [], [(nm, hp)], nm + "L")
                        else:
                            gs_ = slice(g, 128, 8)
                            dma("sp", T_[ps, :, :], T_[gs_, :, :], [(nm, g)], [(nm, hp)], nm + "R")
                FSB = [("fsb", hp) for hp in range(8)]
                ESP = [("esp", hp) for hp in range(8)]

                c0 = CX[0]
                c1x = CX[1]
                norm_T(c0, h_s[0:NS, :], ["h_s"], NS)
                mm_tok(c0, B[2], "B2", 0, 512, NS)
                mm_tok(c0, B[3], "B3", 512, 768, NS)
                mm_tok(c0, B[4], "B4", 768, 1280, NS)
                mm_tok(c0, B[5], "B5", 1280, 1792, NS)
                P.add("act", lambda e: e.activation(out=qs_t[:], in_=B[2][0:NS, :], func=AF.Copy, scale=0.125), r=["B2"], w=["qs_t"])
                P.add("act", lambda e: e.activation(out=kvs[:], in_=B[3][0:NS, 0:256], func=AF.Copy), r=["B3"], w=["kvs"])
                dma("sp", qscr.ap(), qs_t[:], ["qs_t"], ["qscr"], "qscr")
                dma("sp", kvscr.ap(), kvs[:], ["kvs"], ["kvscr"], "kvscr")
                dma("sp", wks.ap()[:, 127, :], kvs[:, 0:128], ["kvs"], ["wks2"], "wkvs")
                dma("sp", wvs.ap()[:, 127, :], kvs[:, 128:256], ["kvs"], ["wvs2"], "wkvs")
                gelu_u(c0, B[4], "B4", NS)
                gelu_ln(c0, B[5], "B5", NS, want_f32=True)
                t1s, gms, ugs = c0.t1, c0.gm, c0.ug
                P.add("dve", lambda e: e.tensor_tensor(out=t1s[0:NS, :].rearrange("p (h d) -> p h d", h=8), in0=vn[0:NS, :].rearrange("p (h d) -> p h d", h=8),
                                                       in1=ws00[:].unsqueeze(2).to_broadcast([NS, 8, 64]), op=ALU.mult), r=["vn", "ws00"], w=[c0.k("t1")])
                P.add("dve", lambda e: e.tensor_tensor(out=t1s[0:NS, :].rearrange("p (h d) -> p h d", h=8), in0=t1s[0:NS, :].rearrange("p (h d) -> p h d", h=8),
                                                       in1=bs0[:].unsqueeze(2).to_broadcast([NS, 8, 64]), op=ALU.add), r=[c0.k("t1"), "bs0"], w=[c0.k("t1")])
                P.add("dve", lambda e: e.tensor_tensor(out=gms[0:NS, :], in0=t1s[0:NS, :], in1=ugs[0:NS, :], op=ALU.mult),
                      r=[c0.k("t1"), c0.k("ug")], w=[c0.k("gm")])
                for hp in range(8):
                    g = hp % 2
                    ps = slice(hp, 128, 8)
                    dma("sp", qn[ps, :], qscr.ap()[:, hp * 64:(hp + 1) * 64], ["qscr"], [("qn", hp)], "qkn")
                    dma("sp", knew[ps, :], kvscr.ap()[:, g * 64:(g + 1) * 64], ["kvscr"], [("knew", hp)], "qkn")
                    dma("sp", vnew[ps, :], kvscr.ap()[:, 128 + g * 64:128 + (g + 1) * 64], ["kvscr"], [("vnew", hp)], "qkn")
                for g in range(2):
                    gs_ = slice(g, 128, 8)
                    dma("sp", wks.ap()[:, 0:127, g * 64:(g + 1) * 64], Ks[gs_, 1:128, :], [("Ks", g)], [("wks", g)], "wkvs")
                    dma("sp", wvs.ap()[:, 0:127, g * 64:(g + 1) * 64], Vs[gs_, 1:128, :], [("Vs", g)], [("wvs", g)], "wkvs")
                P.add("act", lambda e: e.activation(out=esp[:], in_=esp[:], func=AF.Exp), ESP, ESP)
                dma("sp", gv.ap(), vn[0:NS, :], ["vn"], ["gv"], "gv")
                KS = [("Ks", hp) for hp in range(8)]
                VS = [("Vs", hp) for hp in range(8)]
                QN = [("qn", hp) for hp in range(8)]
                KN = [("knew", hp) for hp in range(8)]
                VN = [("vnew", hp) for hp in range(8)]
                for ch in range(4):
                    cs_ = slice(ch * 32, (ch + 1) * 32)
                    P.add("dve", lambda e, cs_=cs_: e.tensor_tensor(out=prod[:], in0=Ks[:, cs_, :], in1=qn[:].unsqueeze(1).to_broadcast([128, 32, 64]),
                                                                    op=ALU.mult), r=KS + QN, w=["prod"])
                    P.add("dve", lambda e, cs_=cs_: e.reduce_sum(out=lg[:, cs_], in_=prod[:], axis=AX.X), r=["prod"], w=["lg"])
                P.add("dve", lambda e: e.tensor_tensor(out=prod[:, 0, :], in0=knew[:], in1=qn[:], op=ALU.mult), r=KN + QN, w=["prod"])
                P.add("dve", lambda e: e.reduce_sum(out=lg[:, 128:129], in_=prod[:, 0, :], axis=AX.X), r=["prod"], w=["lg"])
                P.add("dve", lambda e: e.tensor_tensor(out=lg[:], in0=lg[:], in1=fsb[:], op=ALU.add), r=["lg"] + FSB, w=["lg"])
                P.add("act", lambda e: e.activation(out=lg[:], in_=lg[:], func=AF.Exp, accum_out=dsum[:, 0:1]), r=["lg"], w=["lg", "dsum"])
                P.add("dve", lambda e: e.tensor_tensor(out=dsum[:, 1:2], in0=dsum[:, 0:1], in1=esp[:], op=ALU.add), r=["dsum"] + ESP, w=["dsum1"])
                P.add("dve", lambda e: e.reciprocal(out=dsum[:, 2:3], in_=dsum[:, 1:2]), r=["dsum1"], w=["dsum2"])
                for ch in range(4):
                    cs_ = slice(ch * 32, (ch + 1) * 32)
                    P.add("dve", lambda e, cs_=cs_: e.tensor_tensor(out=prod[:], in0=Vs[:, cs_, :], in1=lg[:, cs_].unsqueeze(2).to_broadcast([128, 32, 64]),
                                                                    op=ALU.mult), r=VS + ["lg"], w=["prod"])
                    P.add("dve", lambda e, ch=ch: e.reduce_sum(out=osum[:, ch, :], in_=prod[:].rearrange("p c d -> p d c"), axis=AX.X),
                          r=["prod"], w=[("osum", ch)])
                P.add("dve", lambda e: e.tensor_scalar(out=osum[:, 4, :], in0=vnew[:], scalar1=lg[:, 128:129], scalar2=None, op0=ALU.mult),
                      r=VN + ["lg"], w=[("osum", 4)])
                P.add("dve", lambda e: e.reduce_sum(out=onh[:], in_=osum[:].rearrange("p k d -> p d k"), axis=AX.X),
                      r=[("osum", k) for k in range(5)], w=["onh"])
                P.add("dve", lambda e: e.tensor_scalar(out=onh[:], in0=onh[:], scalar1=dsum[:, 2:3], scalar2=None, op0=ALU.mult),
                      r=["onh", "dsum2"], w=["onh"])
                P.add("dve", lambda e: e.memset(dsum[:, 3:4], 0.0), r=KS + VS + ["onh"], w=[("h", j) for j in range(1, 17)])
                dma("sp", ascr.ap(), onh[:], ["onh"], ["ascr"], "ascr")
                AK = []
                for hp in range(8):
                    head = (hp % 2) * 4 + hp // 2
                    dma("sp", c0.attn[0:NS, head * 64:(head + 1) * 64], ascr.ap()[hp:128:8, :], ["ascr"], [("attn_s", hp)], "attn_s")
                    AK.append(("attn_s", hp))
                dma("sp", ohs[:], oh.ap(), [], ["ohs"], "cst2")
                dma("sp", mkc[:], maskc.ap(), [], ["mkc"], "cst2")
                dma("sp", mvb[:], mv.ap()[0:1, :].partition_broadcast(128), [], ["mvb"], "cst2")
                for h in range(8):
                    gt_ = gbt[h % 2]
                    gk_ = ("gbt", h % 2)
                    P.add("pe", lambda e, h=h: e.matmul(B[7][:, 0:384], lhsT=rb[:, h:h + 1].to_broadcast([32, 128]), rhs=ohs[:], start=True, stop=True),
                          ["rb", "ohs"], ["B7"])
                    P.add("dve", lambda e, gt_=gt_: e.tensor_tensor(out=gt_[:], in0=B[7][:, 0:384], in1=mvb[:], op=ALU.add), ["B7", "mvb"], [gk_])
                    dma("sp", scr.ap()[h], gt_[:], [gk_], [("scr", h)], "scr")
                for kb, off in ((0, 255), (1, 127)):
                    src = bass.AP(scr, off, [[383, 128], [128 * 384, 8], [1, 128]])
                    dma("pool", biasT[:, kb, :].rearrange("p (h q) -> p h q", h=8), src, [("scr", h) for h in range(8)], [("biasT", kb)], "biasT")
                P.add("dve", lambda e: e.tensor_scalar(out=bias1[:], in0=biasT[:, 0, :], scalar1=mkc[:, 0:1], scalar2=None, op0=ALU.add),
                      [("biasT", 0), "mkc"], ["bias1"])
                out_proj(c0, h_s[0:NS, :], ["h_s"], h_s[0:NS, :], ["h_s"], NS, akeys=AK)
                dma("sp", h_all[:, 16, :], xh.ap()[0:128, :], [], [("h", 17)], "x0")
                dma("sp", esink[:], sinks.ap().partition_broadcast(128), [], ["esink"], "cst2")
                dma("sp", trl[:], trilT.ap(), [], ["trl"], "cst2")
                dma("sp", bs_t[:], b_s.ap().rearrange("h t -> t h"), [], ["bs_t"], "cst2", slow=True)
                dma("pool", wsl[:], w_s.ap().rearrange("h t s -> t h s"), [], ["wsl"], "wsl")
                P.add("act", lambda e: e.activation(out=esink[:], in_=esink[:], func=AF.Exp), ["esink"], ["esink"])

                def trw(e):
                    for h in range(8):
                        ins = e.transpose(out=B[0][:, h * 128:(h + 1) * 128], in_=wsl[:, h, :], identity=identb[:])
                    return ins
                P.add("pe", trw, ["wsl", "identb"], ["B0"])
                for h in range(8):
                    P.add("dve", lambda e, h=h: e.tensor_tensor(out=wsT[:, h, :], in0=B[0][:, h * 128:(h + 1) * 128], in1=trl[:], op=ALU.mult),
                          ["B0", "trl"], ["wsT"])
                P.emit()
            if MAXPH < 2:
                return nc

            sA = ExitStack()
            with sA:
                kT_all = sb(sA, "kT_all", (128, NBLK, 128), BF16)
                v_all = sb(sA, "v_all", (128, NBLK, 2, 72), BF16)
                qT = [sb(sA, "qT%d" % t, (128, 512), BF16) for t in range(2)]
                PT = [sb(sA, "PT%d" % t, (128, 2, 1024), BF16) for t in range(2)]
                kvo = sb(sA, "kvo", (128, 256))

                for j in range(1, NBLK - 1):
                    dma("sp", h_all[:, j - 1, :], xh.ap()[j * 128:(j + 1) * 128, :], [], [("h", j)], "x%d" % j)
                XLAST = [True]
                P.add("dve", lambda e: e.memset(v_all[:], 1.0), [], [("v", j) for j in range(NBLK)])
                if DBG == 1:
                    P.emit()
                    return nc
                if DBG == 3:
                    P.emit()
                    return nc
                if DBG == 4:
                    P.emit()
                    return nc

                def stageA(j):
                    cx = CX[j % 2]
                    xap = h_all[:, 16, :] if j == 0 else h_all[:, j - 1, :]
                    xkeys = [("h", 17)] if j == 0 else [("h", j)]
                    norm_T(cx, xap, xkeys)
                    if j >= 1:
                        def fq(e):
                            for t in range(4):
                                for kc in range(8):
                                    ins = e.matmul(cx.a[:, t * 128:(t + 1) * 128], lhsT=W_in[:, kc, t * 128:(t + 1) * 128], rhs=cx.xnT[:, kc, :],
                                                   start=(kc == 0), stop=(kc == 7))
                            return ins
                        P.add("pe", fq, r=[cx.k("xnT")] + WIN, w=[cx.ak])
                        P.add("dve", lambda e: e.scalar_tensor_tensor(out=qT[cx.t][:].rearrange("p (t q) -> p t q", t=4),
                                                                      in0=cx.a[:].rearrange("p (t q) -> p t q", t=4), scalar=0.125,
                                                                      in1=bq8[:].unsqueeze(2).to_broadcast([128, 4, 128]),
                                                                      op0=ALU.mult, op1=ALU.add), r=[cx.ak, "bq8"], w=[cx.k("qT")])

                    def fk(e):
                        for kc in range(8):
                            ins = e.matmul(cx.b[:, 0:128], lhsT=W_in[:, kc, 512:640], rhs=cx.xnT[:, kc, :], start=(kc == 0), stop=(kc == 7))
                        return ins
                    P.add("pe", fk, r=[cx.k("xnT")] + WIN, w=[cx.bk])
                    mm_tok(cx, cx.b, cx.bk, 512, 768, o0=128)
                    P.add("dve", lambda e: e.tensor_scalar(out=kT_all[:, j, :], in0=cx.b[:, 0:128], scalar1=bqk[:, 4:5], scalar2=None, op0=ALU.add),
                          r=[cx.bk, "bqk"], w=[("k", j)])
                    for g in range(2):
                        P.add("act", lambda e, g=g: e.activation(out=v_all[:, j, g, 0:64], in_=cx.b[:, 256 + g * 64:320 + g * 64],
                                                                 func=AF.Identity), r=[cx.bk], w=[("v", j)])
                    if j == NBLK - 1:
                        P.add("act", lambda e: e.activation(out=kvo[:], in_=cx.b[:, 128:384], func=AF.Copy), r=[cx.bk], w=["kvo"])
                        dma("sp", wkv.ap(), kvo[:], ["kvo"], ["wkv"], "wkv")
                    if j == 0:
                        dma("sp", h_all[:, 16, :], xh.ap()[17 * 128:18 * 128, :], [], [("h", 17)], "x17")
                        return
                    mm_tok(cx, cx.c, cx.ck, 768, 1280)
                    gelu_u(cx, cx.c, cx.ck)
                    mm_tok(cx, cx.c, cx.ck, 1280, 1792)
                    gelu_ln(cx, cx.c, cx.ck)

                def stageB(j):
                    cx = CX[j % 2]
                    PTt = PT[cx.t]
                    for g in range(2):
                        for kb in range(2):
                            jb = j - 1 + kb
                            bank, bkey = (cx.b, cx.bk) if kb == 0 else (cx.c, cx.ck)
                            bt = (bias1[:, g * 512:(g + 1) * 512] if (kb == 0 and j <= 2) else biasT[:, kb, g * 512:(g + 1) * 512])
                            btk = "bias1" if (kb == 0 and j <= 2) else ("biasT", kb)

                            def fs(e, g=g, jb=jb, bank=bank, bt=bt):
                                e.matmul(bank[:], lhsT=kT_all[g * 64:(g + 1) * 64, jb, :], rhs=qT[cx.t][g * 64:(g + 1) * 64, :], start=True, stop=False)
                                return e.matmul(bank[:], lhsT=identb[:], rhs=bt, start=False, stop=True)
                            P.add("pe", fs, r=[("k", jb), cx.k("qT"), "identb", btk], w=[bkey])
                            P.add("act", lambda e, g=g, kb=kb, bank=bank: e.activation(out=PTt[:, kb, g * 512:(g + 1) * 512], in_=bank[:], func=AF.Exp),
                                  r=[bkey], w=[cx.k(("PT", kb, g))])
                    for half, bank, bkey in ((0, cx.a, cx.ak), (1, cx.b, cx.bk)):
                        def fpv(e, half=half, bank=bank):
                            for hh in range(4):
                                h = half * 4 + hh
                                for kb in range(2):
                                    ins = e.matmul(bank[:, hh * 128:hh * 128 + 65], lhsT=PTt[:, kb, h * 128:(h + 1) * 128],
                                                   rhs=v_all[:, j - 1 + kb, half, 0:65], start=(kb == 0), stop=(kb == 1))
                            return ins
                        P.add("pe", fpv, r=[cx.k(("PT", 0, half)), cx.k(("PT", 1, half)), ("v", j - 1), ("v", j)], w=[bkey])
                        bv = bank[:].rearrange("p (h c) -> p h c", h=4)
                        hs = slice(half * 4, half * 4 + 4)
                        den = cx.den
                        dk = cx.k(("den", half))
                        P.add("dve", lambda e, bv=bv, hs=hs, den=den: e.tensor_tensor(out=den[:, hs].unsqueeze(2), in0=bv[:, :, 64:65],
                                                                                      in1=esink[:, hs].unsqueeze(2), op=ALU.add),
                              r=[bkey, "esink"], w=[dk])
                        P.add("dve", lambda e, hs=hs, den=den: e.reciprocal(out=den[:, hs], in_=den[:, hs]), r=[dk], w=[dk])
                        P.add("dve", lambda e, bv=bv, hs=hs, half=half, den=den: e.tensor_tensor(
                            out=cx.attn[:, half * 256:(half + 1) * 256].rearrange("p (h d) -> p h d", h=4), in0=bv[:, :, 0:64],
                            in1=den[:, hs].unsqueeze(2).to_broadcast([128, 4, 64]), op=ALU.mult),
                            r=[bkey, dk], w=[cx.k("attn")])

                    def fg(e):
                        for h in range(8):
                            ins = e.matmul(cx.c[:, h * 64:(h + 1) * 64], lhsT=wsT[:, h, :], rhs=cx.vnb[:, h * 64:(h + 1) * 64], start=True, stop=True)
                        return ins
                    P.add("pe", fg, r=["wsT", cx.k("vnb")], w=[cx.ck])
                    P.add("dve", lambda e: e.tensor_tensor(out=cx.t1[:].rearrange("p (h d) -> p h d", h=8), in0=cx.c[:].rearrange("p (h d) -> p h d", h=8),
                                                           in1=bs_t[:].unsqueeze(2).to_broadcast([128, 8, 64]), op=ALU.add),
                          r=[cx.ck, "bs_t"], w=[cx.k("t1")])
                    P.add("dve", lambda e: e.tensor_tensor(out=cx.gm[:], in0=cx.t1[:], in1=cx.ug[:], op=ALU.mult),
                          r=[cx.k("t1"), cx.k("ug")], w=[cx.k("gm")])
                    out_proj(cx, h_all[:, j - 1, :], [("h", j)], h_all[:, j - 1, :], [("h", j)])

                NB = min(NBLK, MAXJ)
                stageA(0)
                if NB > 1:
                    stageA(1)
                for j in range(1, NB):
                    opsB = P.record(lambda: stageB(j))
                    opsA = P.record(lambda: stageA(j + 1)) if j + 1 < NB else []
                    P.merge_sched(opsB, opsA)
                P.emit()
        if MAXPH < 3:
            return nc

        sC = ExitStack()
        with sC:
            W_down = sb(sC, "W_down", (128, 22, D), BF16)
            Wup = [sb(sC, "Wup%d" % i, (128, 8, 256), BF16) for i in range(NWS)]
            gpf = sb(sC, "gpf", (128, D))
            gof = sb(sC, "gof", (128, D))
            identf = sb(sC, "identf", (128, 128))
            cwl = sb(sC, "cwl_s", (NFT, 4, 128))
            cwT = sb(sC, "cwT", (128, 4, NFT))
            flg = sb(sC, "flg", (128, 1))
            h2 = sb(sC, "h2", (128, D), BF16)
            h2T = sb(sC, "h2T", (128, 8, 512), BF16)
            h2Th = sb(sC, "h2Th", (128, 8, 2), BF16)
            h2Ts = sb(sC, "h2Ts", (128, 8, NS), BF16)
            cbuf = [sb(sC, "cbuf%d" % i, (128, 512)) for i in range(NCB)]
            actT = sb(sC, "actT", (128, 22, 512), BF16)
            tails = sb(sC, "tails", (128, NFT, 2))
            upx = [sb(sC, "upx%d" % i, (128, 514)) for i in range(NUX)]
            tpo = sb(sC, "tpo", (88, 128))
            cwr = sb(sC, "cwr_s", (NS, 4, 256))
            ups_b = [h_all[0:NS, 0, 0:256], h_all[0:NS, 0, 256:512]]
            sts = h_all[0:NS, 0, 512:1024].rearrange("p (r c) -> p r c", r=2)
            cs_t = sb(sC, "cs_t", (NS, 256))
            gs_t = sb(sC, "gs_t", (NS, 128))
            act_s = sb(sC, "act_s", (NS, DFF), BF16)
            actTs = sb(sC, "actTs", (128, 22, NS), BF16)

            w_down_v = w_down.ap().rearrange("(kc p) n -> p kc n", p=128)
            dma("sp", gpf[:], g_pre_ffn.ap().partition_broadcast(128), [], ["gpf"], "cst3")
            dma("sp", gof[:], g_post_ffn.ap().partition_broadcast(128), [], ["gof"], "cst3")
            dma("sp", identf[:], ident.ap(), [], ["identf"], "cst3")
            dma("sp", cwl[:], cwl_d.ap(), [], ["cwl"], "cst3")
            dma("sp", flg[:], flag.ap(), [], ["flg"], "cst3")
            dma("sp", cs.ap()[:, 0, :], stt.ap()[:, 1, :], [], ["cs0"], "cs0")
            w_up_v = w_up.ap().rearrange("(kc p) n -> p kc n", p=128)
            wup_seq = [(G, p) for G in range(4) for p in range(22)]

            def load_wup(i):
                G, p = wup_seq[i]
                s = i % NWS
                dma("pool", Wup[s][:], w_up_v[:, :, p * 256:(p + 1) * 256], [], [("Wup", s)], "Wup%d" % s)
            for i in range(NWS - 1):
                load_wup(i)
            for i in range(2):
                dma("pool", W_down[:, 11 * i:11 * i + 11, :], w_down_v[:, 11 * i:11 * i + 11, :], [], ["W_down%d" % i], "W_down")
            WDN = ["W_down0", "W_down1"]

            def trc(e):
                for r_ in range(4):
                    ins = e.transpose(out=B[7][:, r_ * NFT:(r_ + 1) * NFT], in_=cwl[:, r_, :], identity=identf[0:NFT, 0:NFT])
                return ins
            P.add("pe", trc, ["cwl", "identf"], ["B7"])
            P.add("dve", lambda e: e.tensor_copy(out=cwT[:].rearrange("p r f -> p (r f)"), in_=B[7][:, 0:4 * NFT]), ["B7"], ["cwT"])

            def ffn_norm_T(xap, xkeys, dst, dkey, np_=128, c0=0, slot=0, pre=None):
                rstd, rk_ = pre if pre is not None else rms_rstd([(xap, xkeys)], D, "f", np_, slot=slot)
                P.add("dve", lambda e: e.scalar_tensor_tensor(out=h2[0:np_, :], in0=xap, scalar=rstd, in1=gpf[0:np_, :],
                                                              op0=ALU.mult, op1=ALU.mult), r=list(xkeys) + [rk_, "gpf"], w=["h2"])

                def tr(e):
                    for c in range(8):
                        ins = e.transpose(out=B[0][:, c * 128:c * 128 + np_], in_=h2[0:np_, c * 128:(c + 1) * 128], identity=identb[0:np_, 0:np_])
                    return ins
                P.add("pe", tr, r=["h2", "identb"], w=["B0"])
                B0v = B[0][:].rearrange("p (c t) -> p c t", c=8)
                P.add("act", lambda e: e.activation(out=dst, in_=B0v[:, :, c0:np_], func=AF.Copy), r=["B0"], w=[dkey])

            ffn_norm_T(h_all[:, 0, :], [("h", 1)], h2Th[:], "h2Th", c0=126)
            ffn_norm_T(h_s[0:NS, :], ["h_s"], h2Ts[:], "h2Ts", NS)
            P.add("dve", lambda e: e.memset(ssB[:, 0:1], 0.0), r=[("h", 1)], w=[("ups", 0), ("ups", 1), "sts"])

            wi = 0
            cslot = 0
            for G in range(4):
                par, ppar = G % 2, (G + 1) % 2
                if G == 0:
                    pres = [rms_rstd([(h_all[:, 1 + tb, :], [("h", 2 + tb)])], D, "f", slot=tb) for tb in range(4)]
                else:
                    pres = pres_next
                if G == 0:
                    for tb in range(4):
                        ffn_norm_T(h_all[:, 1 + tb, :], [("h", 2 + tb)], h2T[:, :, tb * 128:(tb + 1) * 128], ("h2T", tb), pre=pres[tb])
                H2T = [("h2T", tb) for tb in range(4)]
                prev_side = []
                for p in range(22):
                    s = wi % NWS
                    if G == 0:
                        P.rec = []
                    if wi + NWS - 1 < len(wup_seq):
                        load_wup(wi + NWS - 1)
                    wi += 1
                    cts = []
                    ups = ups_b[p % 2]
                    upk = ("ups", p % 2)
                    if G == 0:
                        def fb7(e, s=s):
                            for t2 in range(2):
                                for kc in range(8):
                                    e.matmul(B[7][:, 2 * t2:2 * t2 + 2], lhsT=Wup[s][:, kc, t2 * 128:(t2 + 1) * 128], rhs=h2Th[:, kc, :],
                                             start=(kc == 0), stop=(kc == 7))
                            for kc in range(8):
                                ins = e.matmul(B[7][0:NS, 256:512], lhsT=h2Ts[:, kc, :], rhs=Wup[s][:, kc, :], start=(kc == 0), stop=(kc == 7))
                            return ins
                        P.add("pe", fb7, r=[("Wup", s), "h2Ts", "h2Th"], w=["B7"])
                        P.add("act", lambda e, ups=ups: e.activation(out=ups[:], in_=B[7][0:NS, 256:512], func=AF.Copy), r=["B7"], w=[upk])
                        P.add("act", lambda e, p=p: e.activation(out=tails[:, 2 * p:2 * p + 2, :], in_=B[7][:, 0:4].rearrange("q (a b) -> q a b", a=2),
                                                                 func=AF.Copy, scale=flg[:, 0:1]),
                              r=["B7", "flg"], w=[("tails", 2 * p), ("tails", 2 * p + 1)])
                    for t2 in range(2):
                        ft = p * 2 + t2
                        bi = 2 + (cslot % 4)

                        def fup(e, s=s, t2=t2, bi=bi):
                            for kc in range(8):
                                ins = e.matmul(B[bi][:], lhsT=Wup[s][:, kc, t2 * 128:(t2 + 1) * 128], rhs=h2T[:, kc, :], start=(kc == 0), stop=(kc == 7))
                            return ins
                        P.add("pe", fup, r=[("Wup", s)] + H2T, w=["B%d" % bi])
                        c = cbuf[cslot % NCB]
                        ck_ = ("cbuf", cslot % NCB)
                        ux = upx[cslot % NUX]
                        uk = ("upx", cslot % NUX)
                        uhk = ("upxh", cslot % NUX)
                        cslot += 1
                        cts.append((c, ck_))
                        w0, w1, w2, bb = (cwT[:, r_, ft:ft + 1] for r_ in range(4))
                        P.add("act", lambda e, c=c, bi=bi, w2=w2, bb=bb: e.activation(out=c[:], in_=B[bi][:], func=AF.Identity, scale=w2, bias=bb),
                              r=["B%d" % bi, "cwT"], w=[ck_])
                        P.add("act", lambda e, ux=ux, bi=bi: e.activation(out=ux[:, 2:514], in_=B[bi][:], func=AF.Copy),
                              r=["B%d" % bi], w=[uk])
                        P.add("pool", lambda e, ux=ux, ft=ft: e.tensor_copy(out=ux[:, 0:2], in_=tails[:, ft, :]), r=[("tails", ft)], w=[uhk])
                        P.add("act", lambda e, bi=bi, ft=ft: e.activation(out=tails[:, ft, :], in_=B[bi][:, 510:512], func=AF.Copy),
                              r=["B%d" % bi], w=[("tails", ft)])
                        P.add("dve", lambda e, c=c, ux=ux, w1=w1: e.scalar_tensor_tensor(out=c[:], in0=ux[:, 1:513], scalar=w1, in1=c[:],
                                                                                         op0=ALU.mult, op1=ALU.add), r=[uk, uhk, ck_, "cwT"], w=[ck_])
                        P.add("dve", lambda e, c=c, ux=ux, w0=w0: e.scalar_tensor_tensor(out=c[:], in0=ux[:, 0:512], scalar=w0, in1=c[:],
                                                                                         op0=ALU.mult, op1=ALU.add), r=[uk, uhk, ck_, "cwT"], w=[ck_])
                    (cg, cgk), (cv_, cvk) = cts
                    P.add("act", lambda e, cg=cg: e.activation(out=cg[:], in_=cg[:], func=AF.Gelu_apprx_tanh), r=[cgk], w=[cgk])
                    P.add("dve", lambda e, cg=cg, cv_=cv_, p=p: e.tensor_tensor(out=actT[:, p, :], in0=cg[:], in1=cv_[:], op=ALU.mult),
                          r=[cgk, cvk], w=[("actT", p)])
                    if G == 0:
                        main_ops, P.rec = P.rec, []
                        for t2 in range(2):
                            ft = p * 2 + t2
                            ocol = (ft % 2) * DFF + (ft // 2) * 128
                            dma("sp", cs.ap()[:, 1, ocol:ocol + 128], ups[:, t2 * 128:(t2 + 1) * 128], [upk], [("cs1", ft)], "cs1_%d" % t2)
                        c0_ = p * 256
                        dma("sp", cwr[:], bass.AP(cwr_d, c0_, [[0, NS], [F2, 4], [1, 256]]), [], ["cwr"], "cwr")
                        dma("sp", sts[:], stp.ap()[:, :, c0_:c0_ + 256], [], ["sts"], "sts")
                        P.add("dve", lambda e, ups=ups: e.tensor_tensor(out=cs_t[:], in0=ups[:], in1=cwr[:, 2, :], op=ALU.mult), r=[upk, "cwr"], w=["cs_t"])
                        P.add("dve", lambda e: e.tensor_tensor(out=cs_t[:], in0=cs_t[:], in1=cwr[:, 3, :], op=ALU.add), r=["cs_t", "cwr"], w=["cs_t"])
                        for r_ in range(2):
                            P.add("dve", lambda e, r_=r_: e.tensor_tensor(out=sts[:, r_, :], in0=sts[:, r_, :], in1=cwr[:, r_, :], op=ALU.mult),
                                  r=["sts", "cwr"], w=["sts"])
                            P.add("dve", lambda e, r_=r_: e.tensor_tensor(out=cs_t[:], in0=cs_t[:], in1=sts[:, r_, :], op=ALU.add),
                                  r=["cs_t", "sts"], w=["cs_t"])
                        P.add("act", lambda e: e.activation(out=gs_t[:], in_=cs_t[:, 0:128], func=AF.Gelu_apprx_tanh), r=["cs_t"], w=["gs_t"])
                        P.add("dve", lambda e, p=p: e.tensor_tensor(out=act_s[:, p * 128:(p + 1) * 128], in0=gs_t[:], in1=cs_t[:, 128:256], op=ALU.mult),
                              r=["gs_t", "cs_t"], w=["act_s"])
                        side_ops, P.rec = P.rec, None
                        P.merge_sched(main_ops, prev_side)
                        prev_side = side_ops
                if G == 0:
                    P.merge_sched(prev_side)
                if G < 3:
                    pres_next = [rms_rstd([(h_all[:, 1 + 4 * (G + 1) + tb, :], [("h", 2 + 4 * (G + 1) + tb)])], D, "f", slot=tb) for tb in range(4)]
                ACT_ALL = [("actT", p) for p in range(22)]
                P.rec = []
                for tb in range(4):
                    dbk = ((0, 6), (1, 7)) if tb % 2 == 0 else ((0, 2), (1, 3))
                    for half, bi in dbk:
                        def fdn(e, tb=tb, half=half, bi=bi):
                            for p in range(22):
                                ins = e.matmul(B[bi][:], lhsT=actT[:, p, tb * 128:(tb + 1) * 128], rhs=W_down[:, p, half * 512:(half + 1) * 512],
                                               start=(p == 0), stop=(p == 21))
                            return ins
                        P.add("pe", fdn, r=ACT_ALL + WDN, w=["B%d" % bi])
                    blk = 4 * G + tb
                    hblk = h_all[:, 1 + blk, :]
                    hk = ("h", 2 + blk)
                    rstd, rk_ = rms_rstd([(B[bi_][:], ["B%d" % bi_]) for (_, bi_) in dbk], D, "y", slot=4 + tb % 2)
                    for half, bi in dbk:
                        sl = slice(half * 512, (half + 1) * 512)
                        P.add("dve", lambda e, bi=bi, sl=sl, rstd=rstd: e.scalar_tensor_tensor(out=B[bi][:], in0=B[bi][:], scalar=rstd, in1=gof[:, sl],
                                                                                   op0=ALU.mult, op1=ALU.mult),
                              r=["B%d" % bi, rk_, "gof"], w=["B%d" % bi])
                        P.add("dve", lambda e, bi=bi, sl=sl, hblk=hblk: e.tensor_tensor(out=hblk[:, sl], in0=B[bi][:], in1=hblk[:, sl], op=ALU.add),
                              r=["B%d" % bi, hk], w=[hk])
                    dma("sp", y.ap()[blk * 128:(blk + 1) * 128, :], hblk, [hk], [("y", blk)], "y%d" % (blk % 4))
                s1_ops, P.rec = P.rec, []
                if G < 3:
                    for tb in range(4):
                        ffn_norm_T(h_all[:, 1 + 4 * (G + 1) + tb, :], [("h", 2 + 4 * (G + 1) + tb)], h2T[:, :, tb * 128:(tb + 1) * 128],
                                   ("h2T", tb), pre=pres_next[tb])
                s2_ops, P.rec = P.rec, None
                P.merge_sched(s1_ops, s2_ops)
                if G == 0:
                    def trs(e):
                        for p in range(22):
                            ins = e.transpose(out=B[0][:, p * NS:(p + 1) * NS], in_=act_s[:, p * 128:(p + 1) * 128], identity=identb[0:NS, 0:NS])
                        return ins
                    P.add("pe", trs, r=["act_s", "identb"], w=["B0"])
                    P.add("act", lambda e: e.activation(out=actTs[:].rearrange("p a n -> p (a n)"), in_=B[0][:, 0:22 * NS], func=AF.Copy),
                          r=["B0"], w=["actTs"])
                    for half, bi in ((0, 5), (1, 6)):
                        def fds(e, half=half, bi=bi):
                            for p in range(22):
                                ins = e.matmul(B[bi][0:NS, :], lhsT=actTs[:, p, :], rhs=W_down[:, p, half * 512:(half + 1) * 512],
                                               start=(p == 0), stop=(p == 21))
                            return ins
                        P.add("pe", fds, r=["actTs"] + WDN, w=["B%d" % bi])
                    rstd, rk_ = rms_rstd([(B[5][0:NS, :], ["B5"]), (B[6][0:NS, :], ["B6"])], D, "ys", NS, slot=6)
                    for half, bi in ((0, 5), (1, 6)):
                        sl = slice(half * 512, (half + 1) * 512)
                        P.add("dve", lambda e, bi=bi, sl=sl, rstd=rstd: e.scalar_tensor_tensor(out=B[bi][0:NS, :], in0=B[bi][0:NS, :], scalar=rstd, in1=gof[0:NS, sl],
                                                                                   op0=ALU.mult, op1=ALU.mult), r=["B%d" % bi, rk_, "gof"], w=["B%d" % bi])
                        P.add("dve", lambda e, bi=bi, sl=sl: e.tensor_tensor(out=h_s[0:NS, sl], in0=B[bi][0:NS, :], in1=h_s[0:NS, sl], op=ALU.add),
                              r=["B%d" % bi, "h_s"], w=["h_s"])
                    dma("sp", ys.ap(), h_s[0:NS, :], ["h_s"], ["ys"], "ys")
            TL = [("tails", ft) for ft in range(NFT)]
            P.add("pe", lambda e: e.transpose(out=B[7][0:88, 0:128], in_=tails[:].rearrange("p f t -> p (f t)"), identity=identf[:]),
                  r=TL + ["identf"], w=["B7"])
            P.add("dve", lambda e: e.tensor_copy(out=tpo[:], in_=B[7][0:88, 0:128]), ["B7"], ["tpo"])
            dma("sp", cp.ap(), tpo[:], ["tpo"], ["cp"], "cp")
            P.emit()
    return nc


def _t5_bucket_np(n):
    n = np.maximum(n, 0)
    max_exact = 16
    nf = np.maximum(n, 1).astype(np.float32)
    large = max_exact + (np.log(nf / max_exact) / np.log(128 / max_exact) * (32 - max_exact)).astype(np.int32)
    large = np.minimum(large, 31)
    return np.where(n < max_exact, n, large)


_NC_CACHE = {}


def _prep(x_prompt, x_sample, cache_win_k, cache_win_v, state_ffn_conv, rel_bias,
          w_in, b_in, attn_sinks, gmlp_ln_g, gmlp_ln_b, gmlp_w_s, gmlp_b_s,
          g_attn_out, g_gmlp_out, w_out, g_pre_mix, g_post_mix, g_pre_ffn, g_post_ffn,
          w_up, ffn_conv_w, ffn_conv_b, w_down):
    f32 = np.float32
    A = lambda a: np.ascontiguousarray(np.asarray(a, dtype=f32))
    x_prompt, x_sample = A(x_prompt), A(x_sample)
    qperm = np.array([(half * 4 + t) * 64 + d for t in range(4) for half in range(2) for d in range(64)])
    in_perm = np.concatenate([qperm, np.arange(512, 1792)])
    w_in_p = A(np.asarray(w_in)[0][:, in_perm])
    b_in_p = A(np.asarray(b_in)[0][in_perm][None, :])
    up_perm = np.array([(ft % 2) * DFF + (ft // 2) * 128 + i for ft in range(NFT) for i in range(128)])
    w_up_p = A(np.asarray(w_up)[0][:, up_perm])
    cw4 = np.concatenate([np.asarray(ffn_conv_w)[0], np.asarray(ffn_conv_b)[0][None, :]], axis=0)[:, up_perm]
    cwr = A(cw4)
    cwl = A(cw4.reshape(4, NFT, 128).transpose(1, 0, 2))
    hperm = np.array([(hp % 2) * 4 + hp // 2 for hp in range(8)])
    sinks = A(np.asarray(attn_sinks)[0][None, :])
    sinks_p = A(np.asarray(attn_sinks)[0][hperm][None, :])
    g_ao = A(np.concatenate([np.asarray(g_attn_out)[0], np.asarray(g_gmlp_out)[0]])[None, :])
    ident = np.eye(128, dtype=f32)
    dist = np.arange(383) - 127
    bk = _t5_bucket_np(dist)
    ok = (dist >= 0) & (dist <= 128)
    oh = np.zeros((32, 384), f32)
    oh[bk[ok], np.nonzero(ok)[0]] = 1.0
    mv = np.zeros((8, 384), f32)
    mv[:, :383][:, ~ok] = NEG
    mv[:, 383] = NEG
    oh2 = np.zeros((32, 129), f32)
    oh2[_t5_bucket_np(128 - np.arange(129)), np.arange(129)] = 1.0
    trilT = np.triu(np.ones((128, 128), f32))

    ck = A(cache_win_k)[0].reshape(128, 128, 128)
    cv = A(cache_win_v)[0].reshape(128, 128, 128)
    st = A(state_ffn_conv)[0]
    stp = np.ascontiguousarray(st[:, :, up_perm])

    shared = dict(rel_bias=A(rel_bias), w_in=w_in_p, b_in=b_in_p, sinks=sinks, sinks_p=sinks_p,
                  ln_g=A(gmlp_ln_g), ln_b=A(gmlp_ln_b), w_s=A(gmlp_w_s)[0], b_s=A(gmlp_b_s)[0], g_ao=g_ao,
                  w_out=A(w_out)[0], g_pre_mix=A(g_pre_mix), g_post_mix=A(g_post_mix), g_pre_ffn=A(g_pre_ffn),
                  g_post_ffn=A(g_post_ffn), w_up=w_up_p, cwl=cwl, cwr=cwr, w_down=A(w_down)[0],
                  ident=ident, oh=oh, oh2=oh2, mv=mv, trilT=trilT)
    in_maps = []
    for c in range(NCORES):
        b, part = c // 4, c % 4
        s0 = part * TOK
        xh = np.zeros((NBLK * 128, D), f32)
        lo = s0 - 256
        if lo >= 0:
            xh[:] = x_prompt[b, lo:s0 + TOK]
        else:
            xh[256:] = x_prompt[b, 0:TOK]
        m = dict(shared)
        m.update(xh=xh, xs=np.ascontiguousarray(x_sample[c * NS:(c + 1) * NS, 0, :]),
                 ck=np.ascontiguousarray(ck[c * NS:(c + 1) * NS]), cv=np.ascontiguousarray(cv[c * NS:(c + 1) * NS]),
                 st=np.ascontiguousarray(st[c * NS:(c + 1) * NS]), stp=np.ascontiguousarray(stp[c * NS:(c + 1) * NS]),
                 flag=np.full((128, 1), 0.0 if part == 0 else 1.0, f32),
                 maskc=np.full((128, 1), NEG if part == 0 else 0.0, f32))
        in_maps.append(m)

    return in_maps, up_perm


def _assemble(R, up_perm):
    f32 = np.float32

    y_prompt = np.stack([np.concatenate([R[b * 4 + p]["y"] for p in range(4)], axis=0) for b in range(2)])
    y_sample = np.concatenate([R[c]["ys"] for c in range(NCORES)], axis=0)[:, None, :]
    wk_p = np.stack([R[b * 4 + 3]["wkv"][:, 0:128].reshape(128, 2, 64) for b in range(2)])[None]
    wv_p = np.stack([R[b * 4 + 3]["wkv"][:, 128:256].reshape(128, 2, 64) for b in range(2)])[None]
    wk_s = np.concatenate([R[c]["wks"] for c in range(NCORES)], axis=0).reshape(1, 128, 128, 2, 64)
    wv_s = np.concatenate([R[c]["wvs"] for c in range(NCORES)], axis=0).reshape(1, 128, 128, 2, 64)
    gv_s = np.concatenate([R[c]["gv"] for c in range(NCORES)], axis=0)[None, :, None, :]
    inv = np.argsort(up_perm)
    cps = []
    for b in range(2):
        t = R[b * 4 + 3]["cp"].reshape(NFT, 2, 128).transpose(1, 0, 2).reshape(2, F2)
        cps.append(t[:, inv])
    c_p = np.stack(cps)[None]
    c_s = np.concatenate([R[c]["cs"] for c in range(NCORES)], axis=0)[None]
    outs = (y_prompt, y_sample, wk_p, wv_p, wk_s, wv_s, gv_s, c_p, c_s)
    return tuple(np.ascontiguousarray(o.astype(f32)) for o in outs)


def kernel(**inputs):
    in_maps, up_perm = _prep(**inputs)
    if "nc" not in _NC_CACHE:
        _NC_CACHE["nc"] = build_program()
    nc = _NC_CACHE["nc"]
    res = run_bass_kernel_spmd(nc, in_maps, core_ids=list(range(NCORES)))
    return _assemble(res.results, up_perm)
```

```python
from contextlib import ExitStack
import numpy as np
import concourse.bass as bass
import concourse.mybir as mybir
from concourse.bass_utils import run_bass_kernel_spmd

F32 = mybir.dt.float32
BF16 = mybir.dt.bfloat16
AF = mybir.ActivationFunctionType
ALU = mybir.AluOpType
AX = mybir.AxisListType

NCORES = 8
D = 1024
NBLK = 18
TOK = 2048
NS = 16
DFF = 2816
F2 = 5632
NFT = 44
EPS = 1e-6
NEG = -30000.0
NCB = 5
NUX = 3
MAXPH = 3
MAXJ = 99
DBG = 0
NWS = 4


class Prog:
    def __init__(self, nc, es):
        self.nc, self.es = nc, es
        self.ops = []
        self.lastw, self.readers = {}, {}
        self.sems, self.count = {}, {}
        self.waited = {e: {} for e in ("pe", "act", "dve", "pool", "sp")}
        self.rec = None

    def record(self, f):
        self.rec = []
        f()
        ops, self.rec = self.rec, None
        return ops

    def merge(self, a, b):
        ia = ib = 0
        while ia < len(a) or ib < len(b):
            if ib >= len(b) or (ia < len(a) and ia * len(b) <= ib * len(a)):
                self.add(*a[ia]); ia += 1
            else:
                self.add(*b[ib]); ib += 1

    @staticmethod
    def _estimate(op):
        eng, fn = op[0], op[1]
        rec = []

        class _Stub:
            def __getattr__(self, name):
                def call(*a, **kw):
                    out = kw.get("out", a[0] if a else None)
                    n = 1
                    try:
                        for d in out.shape[1:]:
                            n *= int(d)
                    except Exception:
                        n = 512
                    rec.append((name, n))
                    return self
                return call
        try:
            fn(_Stub())
        except Exception:
            return 1.0
        t = 0.0
        for name, n in rec:
            if eng == "pe":
                t += 0.03 + n / 1200.0
            elif name == "dma_start":
                t += 2.0
            else:
                t += 0.12 + n / 1000.0
        return max(t, 0.05)

    def merge_sched(self, *streams):
        streams = [st for st in streams if st]
        isbank = lambda k: isinstance(k, str) and len(k) == 2 and k[0] == "B" and k[1].isdigit()
        deps, est = [], []
        for st in streams:
            lastw, readers, dl = {}, {}, []
            for i, (eng, fn, r, w, dma) in enumerate(st):
                w2 = list(w) + [k for k in r if isbank(k)]
                d = set()
                for k in r:
                    if k in lastw:
                        d.add(lastw[k])
                for k in w2:
                    if k in lastw:
                        d.add(lastw[k])
                    d.update(readers.get(k, ()))
                for k in r:
                    readers.setdefault(k, []).append(i)
                for k in w2:
                    lastw[k] = i
                    readers[k] = []
                dl.append(d)
            deps.append(dl)
            est.append([self._estimate(op) for op in st])
        free = {}
        fin = [dict() for _ in streams]
        ptr = [0] * len(streams)
        while True:
            best = None
            for s_, st in enumerate(streams):
                i = ptr[s_]
                if i >= len(st):
                    continue
                eng = st[i][0]
                ready = max([fin[s_][d] for d in deps[s_][i]] + [0.0])
                start = max(ready, free.get(eng, 0.0))
                key = (start, i / len(st))
                if best is None or key < best[0]:
                    best = (key, s_, i, eng, start)
            if best is None:
                break
            _, s_, i, eng, start = best
            end = start + est[s_][i] + 0.15
            fin[s_][i] = end
            free[eng] = end
            ptr[s_] += 1
            self.add(*streams[s_][i])

    def sem(self, name):
        if name not in self.sems:
            self.sems[name] = self.es.enter_context(self.nc.semaphore(name))
            self.count[name] = 0
        return self.sems[name]

    def add(self, eng, fn, r=(), w=(), dma=None):
        if self.rec is not None:
            self.rec.append((eng, fn, tuple(r), tuple(w), dma))
            return
        w = list(w) + [k for k in r if isinstance(k, str) and len(k) == 2 and k[0] == "B" and k[1].isdigit()]
        deps = set()
        for k in r:
            if k in self.lastw:
                deps.add(self.lastw[k])
        for k in w:
            if k in self.lastw:
                deps.add(self.lastw[k])
            for t in self.readers.get(k, ()):
                deps.add(t)
        deps = set((sn, self.count[sn]) if sn.startswith("d_") else (sn, v) for (sn, v) in deps)
        if dma is None:
            sname, inc = "c_" + eng, 1
        else:
            sname, inc = "d_" + dma, 16
        self.sem(sname)
        self.count[sname] += inc
        tok = (sname, self.count[sname])
        for k in r:
            self.readers.setdefault(k, []).append(tok)
        for k in w:
            self.lastw[k] = tok
            self.readers[k] = []
        self.ops.append((eng, fn, deps, tok, inc))

    def emit(self):
        with self.nc.Block() as block:
            for eng, deco in (("pe", block.tensor), ("act", block.scalar), ("dve", block.vector),
                              ("pool", block.gpsimd), ("sp", block.sync)):
                ops = [o for o in self.ops if o[0] == eng]

                def body(e, ops=ops, eng=eng):
                    wt = self.waited[eng]
                    for (_, fn, deps, tok, inc) in ops:
                        need = {}
                        for (s, v) in deps:
                            need[s] = max(need.get(s, 0), v)
                        for s, v in need.items():
                            if wt.get(s, 0) < v:
                                e.wait_ge(self.sems[s], v)
                                wt[s] = v
                        ins = fn(e)
                        ins.then_inc(self.sems[tok[0]], inc)
                    if eng == "sp":
                        for s, v in self.count.items():
                            if s.startswith("d_") and wt.get(s, 0) < v:
                                e.wait_ge(self.sems[s], v)
                                wt[s] = v

                deco(body)
        self.ops = []


def build_program():
    nc = bass.Bass("TRN2", target_bir_lowering=False)

    def din(name, shape):
        return nc.dram_tensor(name, list(shape), F32, kind="ExternalInput")

    def dout(name, shape):
        return nc.dram_tensor(name, list(shape), F32, kind="ExternalOutput")

    xh = din("xh", (NBLK * 128, D))
    xs = din("xs", (NS, D))
    ck = din("ck", (NS, 128, 128))
    cv = din("cv", (NS, 128, 128))
    stt = din("st", (NS, 2, F2))
    stp = din("stp", (NS, 2, F2))
    flag = din("flag", (128, 1))
    maskc = din("maskc", (128, 1))
    rel_bias = din("rel_bias", (32, 8))
    w_in = din("w_in", (D, 1792))
    b_in = din("b_in", (1, 1792))
    sinks = din("sinks", (1, 8))
    sinks_p = din("sinks_p", (1, 8))
    ln_g = din("ln_g", (1, 512))
    ln_b = din("ln_b", (1, 512))
    w_s = din("w_s", (8, 128, 128))
    b_s = din("b_s", (8, 128))
    g_ao = din("g_ao", (1, 1024))
    w_out = din("w_out", (D, D))
    g_pre_mix = din("g_pre_mix", (1, D))
    g_post_mix = din("g_post_mix", (1, D))
    g_pre_ffn = din("g_pre_ffn", (1, D))
    g_post_ffn = din("g_post_ffn", (1, D))
    w_up = din("w_up", (D, F2))
    cwl_d = din("cwl", (NFT, 4, 128))
    cwr_d = din("cwr", (4, F2))
    w_down = din("w_down", (DFF, D))
    ident = din("ident", (128, 128))
    oh = din("oh", (32, 384))
    oh2 = din("oh2", (32, 129))
    mv = din("mv", (8, 384))
    trilT = din("trilT", (128, 128))

    y = dout("y", (TOK, D))
    ys = dout("ys", (NS, D))
    wkv = dout("wkv", (128, 256))
    wks = dout("wks", (NS, 128, 128))
    wvs = dout("wvs", (NS, 128, 128))
    gv = dout("gv", (NS, 512))
    cp = dout("cp", (88, 128))
    cs = dout("cs", (NS, 2, F2))

    scr = nc.dram_tensor("scr", [8, 128, 384], F32)
    scr2 = nc.dram_tensor("scr2", [8, 129], F32)
    qscr = nc.dram_tensor("qscr", [NS, 512], F32)
    kvscr = nc.dram_tensor("kvscr", [NS, 256], F32)
    ascr = nc.dram_tensor("ascr", [128, 64], F32)

    es = ExitStack()
    with es:
        P = Prog(nc, es)

        def sb(stack, name, shape, dt=F32):
            return stack.enter_context(nc.sbuf_tensor(name, list(shape), dt))

        def dma(eng, out, in_, r, w, key, slow=False):
            if slow:
                P.add(eng, lambda e: e.dma_start(out=out, in_=in_, allow_slow_non_contiguous=True), r, w, dma=key)
            else:
                P.add(eng, lambda e: e.dma_start(out=out, in_=in_), r, w, dma=key)

        h_all = sb(es, "h_all", (128, 17, D))
        h_s = sb(es, "h_s", (128, D))
        identb = sb(es, "identb", (128, 128), BF16)
        banks = [es.enter_context(nc.psum_tensor("bank%d" % i, [128, 1024], BF16)) for i in range(2)]
        for i in range(2, 8):
            banks.append(es.enter_context(nc.psum_tensor("bank%d" % i, [128, 512], F32)))
        B = banks
        junks = [sb(es, "junk0", (128, D), BF16)]
        junk = junks[0]
        ssA = sb(es, "ssA", (128, 32))
        ssB = sb(es, "ssB", (128, 8))
        onesr = sb(es, "onesr", (1, 128), BF16)
        epsb = sb(es, "epsb", (128, 1))

        def rms_rstd(srcs, n, tag, np_=128, slot=0, jk=0):
            st_ = ssA
            o = 4 * slot
            kk = lambda nm: ("st", slot, nm)
            for i, (ap, rk) in enumerate(srcs):
                c = st_[0:np_, o + i:o + i + 1]
                jv = junks[jk][0:np_, 0:ap.shape[-1]] if len(ap.shape) == 2 else junks[jk][0:np_, 0:512]
                P.add("act", lambda e, ap=ap, c=c, jv=jv: e.activation(out=jv, in_=ap, func=AF.Square, accum_out=c),
                      r=rk, w=[kk("ss%d" % i), "junk%d" % jk])
            if len(srcs) == 2:
                P.add("dve", lambda e: e.tensor_tensor(out=st_[0:np_, o:o + 1], in0=st_[0:np_, o:o + 1], in1=st_[0:np_, o + 1:o + 2], op=ALU.add),
                      r=[kk("ss0"), kk("ss1")], w=[kk("ss0")])
            P.add("act", lambda e: e.activation(out=st_[0:np_, o + 2:o + 3], in_=st_[0:np_, o:o + 1], func=AF.Ln, scale=1.0 / n, bias=epsb[0:np_, :]),
                  r=[kk("ss0"), "epsb"], w=[kk("std")])
            P.add("act", lambda e: e.activation(out=st_[0:np_, o + 3:o + 4], in_=st_[0:np_, o + 2:o + 3], func=AF.Exp, scale=-0.5),
                  r=[kk("std")], w=[kk("rstd")])
            return st_[0:np_, o + 3:o + 4], kk("rstd")

        sAB = ExitStack()
        with sAB:
            W_in = sb(sAB, "W_in", (128, 8, 1792), BF16)
            W_out = sb(sAB, "W_out", (128, 8, D), BF16)
            junks.append(sb(sAB, "junk1", (128, D), BF16))
            gpo = sb(sAB, "gpo", (128, D))
            lng = sb(sAB, "lng", (128, 512))
            lnb = sb(sAB, "lnb", (128, 512))
            bqk = sb(sAB, "bqk", (128, 5))
            bq8 = sb(sAB, "bq8", (128, 4))
            gT = sb(sAB, "gT", (128, 16))
            brow = sb(sAB, "brow", (1, 1792), BF16)

            class Cx:
                pass
            CX = []
            for t in range(2):
                cx = Cx()
                cx.t = t
                cx.xn = sb(sAB, "xn%d" % t, (128, D), BF16)
                cx.xnT = sb(sAB, "xnT%d" % t, (128, 8, 128), BF16)
                cx.ug = sb(sAB, "ug%d" % t, (128, 512))
                cx.vgg = sb(sAB, "vgg%d" % t, (128, 512))
                cx.vnb = sb(sAB, "vnb%d" % t, (128, 512), BF16)
                cx.t1 = sb(sAB, "t1%d" % t, (128, 512))
                cx.gm = sb(sAB, "gm%d" % t, (128, 512))
                cx.attn = sb(sAB, "attn%d" % t, (128, 512))
                cx.amix = sb(sAB, "amix%d" % t, (128, D), BF16)
                cx.aT = sb(sAB, "aT%d" % t, (128, 8, 128), BF16)
                cx.lnst = sb(sAB, "lnst%d" % t, (128, 8))
                cx.den = sb(sAB, "den%d" % t, (128, 8))
                cx.T = B[t]
                cx.Tk = "B%d" % t
                cx.a, cx.b, cx.c = (B[2 + 3 * t], B[3 + 3 * t], B[4 + 3 * t])
                cx.ak, cx.bk, cx.ck = ("B%d" % (2 + 3 * t), "B%d" % (3 + 3 * t), "B%d" % (4 + 3 * t))
                cx.k = (lambda nm, t=t: (nm, "cx", t))
                CX.append(cx)
            vn = CX[1].vgg

            dma("pool", identb[:], ident.ap(), [], ["identb"], "cstp")
            w_in_v = w_in.ap().rearrange("(kc p) n -> p kc n", p=128)
            for i in range(4):
                dma("pool", W_in[:, 2 * i:2 * i + 2, :], w_in_v[:, 2 * i:2 * i + 2, :], [], ["W_in%d" % i], "W_in")
            WIN = ["W_in%d" % i for i in range(4)]
            w_out_v = w_out.ap().rearrange("(kc p) n -> p kc n", p=128)
            for i in range(2):
                dma("pool", W_out[:, 4 * i:4 * i + 4, :], w_out_v[:, 4 * i:4 * i + 4, :], [], ["W_out%d" % i], "W_out")
            WOUT = ["W_out0", "W_out1"]
            dma("pool", brow[:], b_in.ap(), [], ["brow"], "cstp")
            dma("sp", h_s[0:NS, :], xs.ap(), [], ["h_s"], "h_s")
            dma("sp", gT[:, 0:8], g_pre_mix.ap()[0, :].rearrange("(c p) -> p c", p=128), [], ["gT"], "cst", slow=True)
            dma("sp", gT[:, 8:16], g_ao.ap()[0, :].rearrange("(c p) -> p c", p=128), [], ["gT"], "cst", slow=True)
            dma("sp", bqk[:], b_in.ap()[0, 0:640].rearrange("(t p) -> p t", p=128), [], ["bqk"], "cst", slow=True)
            dma("sp", gpo[:], g_post_mix.ap().partition_broadcast(128), [], ["gpo"], "cst")
            dma("sp", lng[:], ln_g.ap().partition_broadcast(128), [], ["lng"], "cst")
            dma("sp", lnb[:], ln_b.ap().partition_broadcast(128), [], ["lnb"], "cst")
            rb = sb(sAB, "rb", (32, 8))
            wsT = sb(sAB, "wsT", (128, 8, 128), BF16)
            bs_t = sb(sAB, "bs_t", (128, 8))
            esink = sb(sAB, "esink", (128, 8))
            biasT = sb(sAB, "biasT", (128, 2, 1024), BF16)
            bias1 = sb(sAB, "bias1", (128, 1024), BF16)
            ohs = sb(sAB, "ohs", (32, 384))
            mkc = sb(sAB, "mkc", (128, 1))
            oh2s = sb(sAB, "oh2s", (32, 129))
            Gs2 = sb(sAB, "Gs2", (8, 129))
            dma("sp", rb[:], rel_bias.ap(), [], ["rb"], "cst")
            dma("sp", oh2s[:], oh2.ap(), [], ["oh2s"], "cst")
            P.add("pe", lambda e: e.matmul(B[6][0:8, 0:129], lhsT=rb[:], rhs=oh2s[:], start=True, stop=True), ["rb", "oh2s"], ["B6"])
            P.add("dve", lambda e: e.tensor_copy(out=Gs2[:], in_=B[6][0:8, 0:129]), ["B6"], ["Gs2"])
            dma("sp", scr2.ap(), Gs2[:], ["Gs2"], ["scr2"], "scr2")
            P.add("dve", lambda e: e.memset(onesr[:], 1.0), [], ["onesr"])
            P.add("dve", lambda e: e.memset(epsb[:], EPS), [], ["epsb"])
            P.add("dve", lambda e: e.tensor_scalar(out=bq8[:], in0=bqk[:, 0:4], scalar1=0.125, scalar2=None, op0=ALU.mult),
                  ["bqk"], ["bq8"])
            for kc in range(8):
                P.add("dve", lambda e, kc=kc: e.tensor_scalar(out=W_in[:, kc, :], in0=W_in[:, kc, :], scalar1=gT[:, kc:kc + 1], scalar2=None, op0=ALU.mult),
                      ["gT", "W_in%d" % (kc // 2)], ["W_in%d" % (kc // 2)])
                P.add("dve", lambda e, kc=kc: e.tensor_scalar(out=W_out[:, kc, :], in0=W_out[:, kc, :], scalar1=gT[:, 8 + kc:9 + kc], scalar2=None, op0=ALU.mult),
                      ["gT", "W_out%d" % (kc // 4)], ["W_out%d" % (kc // 4)])

            def norm_T(cx, xap, xkeys, np_=128):
                rstd, rk_ = rms_rstd([(xap, xkeys)], D, "x", np_, slot=4 * cx.t, jk=cx.t)
                P.add("dve", lambda e: e.tensor_scalar(out=cx.xn[0:np_, :], in0=xap, scalar1=rstd, scalar2=None, op0=ALU.mult),
                      r=list(xkeys) + [rk_], w=[cx.k("xn")])

                def tr(e):
                    for c in range(8):
                        ins = e.transpose(out=cx.T[:, c * 128:c * 128 + np_], in_=cx.xn[0:np_, c * 128:(c + 1) * 128],
                                          identity=identb[0:np_, 0:np_])
                    return ins
                P.add("pe", tr, r=[cx.k("xn"), "identb"], w=[cx.Tk])
                Tv = cx.T[:].rearrange("p (c t) -> p c t", c=8)
                if np_ == 128:
                    P.add("dve", lambda e: e.tensor_copy(out=cx.xnT[:].rearrange("p c t -> p (c t)"), in_=cx.T[:, 0:1024]),
                          r=[cx.Tk], w=[cx.k("xnT")])
                else:
                    P.add("act", lambda e: e.activation(out=cx.xnT[:, :, 0:np_], in_=Tv[:, :, 0:np_], func=AF.Copy),
                          r=[cx.Tk], w=[cx.k("xnT")])

            def mm_tok(cx, bank, bkey, c0, c1, np_=128, o0=0):
                n = c1 - c0

                def f(e):
                    for kc in range(8):
                        e.matmul(bank[0:np_, o0:o0 + n], lhsT=cx.xnT[:, kc, 0:np_], rhs=W_in[:, kc, c0:c1], start=(kc == 0), stop=False)
                    return e.matmul(bank[0:np_, o0:o0 + n], lhsT=onesr[0:1, 0:np_], rhs=brow[0:1, c0:c1], start=False, stop=True)
                P.add("pe", f, r=[cx.k("xnT"), "brow", "onesr"] + WIN, w=[bkey])

            def gelu_u(cx, bank, bkey, np_=128):
                P.add("act", lambda e: e.activation(out=cx.ug[0:np_, :], in_=bank[0:np_, :], func=AF.Gelu), r=[bkey], w=[cx.k("ug")])

            def gelu_ln(cx, bank, bkey, np_=128, want_f32=False):
                vgg, lnst = cx.vgg, cx.lnst
                kv_, kl = cx.k("vgg"), (lambda i: cx.k("ln%d" % i))
                P.add("act", lambda e: e.activation(out=vgg[0:np_, :], in_=bank[0:np_, :], func=AF.Gelu), r=[bkey], w=[kv_])
                P.add("dve", lambda e: e.reduce_sum(out=lnst[0:np_, 0:1], in_=vgg[0:np_, :], axis=AX.X), r=[kv_], w=[kl(0)])
                P.add("dve", lambda e: e.tensor_scalar(out=lnst[0:np_, 1:2], in0=lnst[0:np_, 0:1], scalar1=-1.0 / 512, scalar2=None, op0=ALU.mult),
                      r=[kl(0)], w=[kl(1)])
                P.add("act", lambda e: e.activation(out=junks[cx.t][0:np_, 0:512], in_=vgg[0:np_, :], func=AF.Square, bias=lnst[0:np_, 1:2],
                                                    accum_out=lnst[0:np_, 2:3]), r=[kv_, kl(1)], w=[kl(2), "junk%d" % cx.t])
                P.add("act", lambda e: e.activation(out=lnst[0:np_, 3:4], in_=lnst[0:np_, 2:3], func=AF.Ln, scale=1.0 / 512, bias=epsb[0:np_, :]),
                      r=[kl(2), "epsb"], w=[kl(3)])
                P.add("act", lambda e: e.activation(out=lnst[0:np_, 4:5], in_=lnst[0:np_, 3:4], func=AF.Exp, scale=-0.5), r=[kl(3)], w=[kl(4)])
                P.add("dve", lambda e: e.tensor_scalar(out=vgg[0:np_, :], in0=vgg[0:np_, :], scalar1=lnst[0:np_, 1:2], scalar2=lnst[0:np_, 4:5],
                                                       op0=ALU.add, op1=ALU.mult), r=[kv_, kl(1), kl(4)], w=[kv_])
                P.add("dve", lambda e: e.tensor_tensor(out=vgg[0:np_, :], in0=vgg[0:np_, :], in1=lng[0:np_, :], op=ALU.mult),
                      r=[kv_, "lng"], w=[kv_])
                if want_f32:
                    P.add("dve", lambda e: e.tensor_tensor(out=vn[0:np_, :], in0=vgg[0:np_, :], in1=lnb[0:np_, :], op=ALU.add),
                          r=[kv_, "lnb"], w=["vn"])
                else:
                    P.add("dve", lambda e: e.tensor_tensor(out=cx.vnb[0:np_, :], in0=vgg[0:np_, :], in1=lnb[0:np_, :], op=ALU.add),
                          r=[kv_, "lnb"], w=[cx.k("vnb")])

            def out_proj(cx, xap, xkeys, hap, hkeys, np_=128, akeys=None):
                akeys = [cx.k("attn")] if akeys is None else list(akeys)
                attn, gm, amix, aT, t1 = cx.attn, cx.gm, cx.amix, cx.aT, cx.t1
                rstd, rk_ = rms_rstd([(attn[0:np_, :], akeys)], 512, "a", np_, slot=4 * cx.t + 1, jk=cx.t)
                P.add("dve", lambda e, rstd=rstd: e.tensor_scalar(out=amix[0:np_, 0:512], in0=attn[0:np_, :], scalar1=rstd, scalar2=None, op0=ALU.mult),
                      r=akeys + [rk_], w=[cx.k("amixa")])
                rstd, rk_ = rms_rstd([(gm[0:np_, :], [cx.k("gm")])], 512, "g", np_, slot=4 * cx.t + 2, jk=cx.t)
                P.add("dve", lambda e, rstd=rstd: e.tensor_scalar(out=amix[0:np_, 512:1024], in0=gm[0:np_, :], scalar1=rstd, scalar2=None, op0=ALU.mult),
                      r=[cx.k("gm"), rk_], w=[cx.k("amixg")])

                def tr(e):
                    for c in range(8):
                        ins = e.transpose(out=cx.T[:, c * 128:c * 128 + np_], in_=amix[0:np_, c * 128:(c + 1) * 128],
                                          identity=identb[0:np_, 0:np_])
                    return ins
                P.add("pe", tr, r=[cx.k("amixa"), cx.k("amixg"), "identb"], w=[cx.Tk])
                Tv = cx.T[:].rearrange("p (c t) -> p c t", c=8)
                if np_ == 128:
                    P.add("dve", lambda e: e.tensor_copy(out=aT[:].rearrange("p c t -> p (c t)"), in_=cx.T[:, 0:1024]), r=[cx.Tk], w=[cx.k("aT")])
                else:
                    P.add("act", lambda e: e.activation(out=aT[:, :, 0:np_], in_=Tv[:, :, 0:np_], func=AF.Copy), r=[cx.Tk], w=[cx.k("aT")])
                ob = ((0, cx.a, cx.ak), (1, cx.b, cx.bk))
                for half, bank, bkey in ob:
                    def f(e, half=half, bank=bank):
                        for kc in range(8):
                            ins = e.matmul(bank[0:np_, :], lhsT=aT[:, kc, 0:np_], rhs=W_out[:, kc, half * 512:(half + 1) * 512],
                                           start=(kc == 0), stop=(kc == 7))
                        return ins
                    P.add("pe", f, r=[cx.k("aT")] + WOUT, w=[bkey])
                rstd, rk_ = rms_rstd([(cx.a[0:np_, :], [cx.ak]), (cx.b[0:np_, :], [cx.bk])], D, "o", np_, slot=4 * cx.t + 3, jk=cx.t)
                for half, bank, bkey in ob:
                    sl = slice(half * 512, (half + 1) * 512)
                    P.add("dve", lambda e, bank=bank, sl=sl, rstd=rstd: e.scalar_tensor_tensor(out=t1[0:np_, :], in0=bank[0:np_, :], scalar=rstd,
                                                                                in1=gpo[0:np_, sl], op0=ALU.mult, op1=ALU.mult),
                          r=[bkey, rk_, "gpo"], w=[cx.k("t1")])
                    P.add("dve", lambda e, sl=sl: e.tensor_tensor(out=hap[:, sl], in0=t1[0:np_, :], in1=xap[:, sl], op=ALU.add),
                          r=[cx.k("t1")] + list(xkeys), w=list(hkeys))

            sB = ExitStack()
            with sB:
                Ks = h_all[:, 0:8, :].rearrange("p b (c d) -> p (b c) d", d=64)
                Vs = h_all[:, 8:16, :].rearrange("p b (c d) -> p (b c) d", d=64)
                knew = sb(sB, "knew", (128, 64))
                wsl = sb(sB, "wsl", (128, 8, 128), BF16)
                trl = sb(sB, "trl", (128, 128))
                mvb = sb(sB, "mvb", (128, 384))
                gbt = [sb(sB, "gbt%d" % i, (128, 384)) for i in range(2)]
                vnew = sb(sB, "vnew", (128, 64))
                qn = sb(sB, "qn", (128, 64))
                prod = sb(sB, "prod", (128, 32, 64))
                lg = sb(sB, "lg", (128, 129))
                fsb = sb(sB, "fsb", (128, 129))
                esp = sb(sB, "esp", (128, 1))
                dsum = sb(sB, "dsum", (128, 4))
                osum = sb(sB, "osum", (128, 5, 64))
                onh = sb(sB, "onh", (128, 64))
                qs_t = sb(sB, "qs_t", (NS, 512))
                kvs = sb(sB, "kvs", (NS, 256))
                ws00 = sb(sB, "ws00", (NS, 8))
                bs0 = sb(sB, "bs0", (NS, 8))

                dma("sp", ws00[:], bass.AP(w_s, 0, [[0, NS], [128 * 128, 8]]), [], ["ws00"], "cst", slow=True)
                dma("sp", bs0[:], bass.AP(b_s, 0, [[0, NS], [128, 8]]), [], ["bs0"], "cst", slow=True)
                def kv_dmas(phases):
                  for T_, src_, nm in phases:
                      for hp in range(8):
                          g = hp % 2
                          ps = slice(hp, 128, 8)
                          head = (hp % 2) * 4 + hp // 2
                          if T_ is None:
                              dma("sp", fsb[ps, :], scr2.ap()[head:head + 1, :].partition_broadcast(16), ["scr2"], [("fsb", hp)], "fe")
                              dma("sp", esp[ps, :], sinks_p.ap()[0:1, hp:hp + 1].partition_broadcast(16), [], [("esp", hp)], "fe")
                          elif hp < 2:
                              dma("sp", T_[ps, :, :], src_.ap()[:, :, g * 64:(g + 1) * 64], [], [(nm, hp)], nm + "L")
                          else:
                              gs_ = slice(g, 128, 8)
                              dma("sp", T_[ps, :, :], T_[gs_, :, :], [(nm, g)], [(nm, hp)], nm + "R")
                kv_dmas(((Ks, ck, "Ks"), (None, None, None)))
                FSB = [("fsb", hp) for hp in range(8)]
                ESP = [("esp", hp) for hp in range(8)]

                c0 = CX[0]
                c1x = CX[1]
                norm_T(c0, h_s[0:NS, :], ["h_s"], NS)
                mm_tok(c0, B[2], "B2", 0, 512, NS)
                mm_tok(c0, B[3], "B3", 512, 768, NS)
                mm_tok(c0, B[4], "B4", 768, 1280, NS)
                mm_tok(c0, B[5], "B5", 1280, 1792, NS)
                P.add("act", lambda e: e.activation(out=qs_t[:], in_=B[2][0:NS, :], func=AF.Copy, scale=0.125), r=["B2"], w=["qs_t"])
                P.add("act", lambda e: e.activation(out=kvs[:], in_=B[3][0:NS, 0:256], func=AF.Copy), r=["B3"], w=["kvs"])
                dma("sp", qscr.ap(), qs_t[:], ["qs_t"], ["qscr"], "qscr")
                dma("sp", kvscr.ap(), kvs[:], ["kvs"], ["kvscr"], "kvscr")
                dma("sp", wks.ap()[:, 127, :], kvs[:, 0:128], ["kvs"], ["wks2"], "wkvs")
                dma("sp", wvs.ap()[:, 127, :], kvs[:, 128:256], ["kvs"], ["wvs2"], "wkvs")
                gelu_u(c0, B[4], "B4", NS)
                gelu_ln(c0, B[5], "B5", NS, want_f32=True)
                t1s, gms, ugs = c0.t1, c0.gm, c0.ug
                P.add("dve", lambda e: e.tensor_tensor(out=t1s[0:NS, :].rearrange("p (h d) -> p h d", h=8), in0=vn[0:NS, :].rearrange("p (h d) -> p h d", h=8),
                                                       in1=ws00[:].unsqueeze(2).to_broadcast([NS, 8, 64]), op=ALU.mult), r=["vn", "ws00"], w=[c0.k("t1")])
                P.add("dve", lambda e: e.tensor_tensor(out=t1s[0:NS, :].rearrange("p (h d) -> p h d", h=8), in0=t1s[0:NS, :].rearrange("p (h d) -> p h d", h=8),
                                                       in1=bs0[:].unsqueeze(2).to_broadcast([NS, 8, 64]), op=ALU.add), r=[c0.k("t1"), "bs0"], w=[c0.k("t1")])
                P.add("dve", lambda e: e.tensor_tensor(out=gms[0:NS, :], in0=t1s[0:NS, :], in1=ugs[0:NS, :], op=ALU.mult),
                      r=[c0.k("t1"), c0.k("ug")], w=[c0.k("gm")])
                for hp in range(8):
                    g = hp % 2
                    ps = slice(hp, 128, 8)
                    dma("sp", qn[ps, :], qscr.ap()[:, hp * 64:(hp + 1) * 64], ["qscr"], [("qn", hp)], "qkn")
                    dma("sp", knew[ps, :], kvscr.ap()[:, g * 64:(g + 1) * 64], ["kvscr"], [("knew", hp)], "qkn")
                    dma("sp", vnew[ps, :], kvscr.ap()[:, 128 + g * 64:128 + (g + 1) * 64], ["kvscr"], [("vnew", hp)], "qkn")
                kv_dmas(((Vs, cv, "Vs"),))
                for g in range(2):
                    gs_ = slice(g, 128, 8)
                    dma("sp", wks.ap()[:, 0:127, g * 64:(g + 1) * 64], Ks[gs_, 1:128, :], [("Ks", g)], [("wks", g)], "wkvs")
                    dma("sp", wvs.ap()[:, 0:127, g * 64:(g + 1) * 64], Vs[gs_, 1:128, :], [("Vs", g)], [("wvs", g)], "wkvs")
                P.add("act", lambda e: e.activation(out=esp[:], in_=esp[:], func=AF.Exp), ESP, ESP)
                dma("sp", gv.ap(), vn[0:NS, :], ["vn"], ["gv"], "gv")
                KS = [("Ks", hp) for hp in range(8)]
                VS = [("Vs", hp) for hp in range(8)]
                QN = [("qn", hp) for hp in range(8)]
                KN = [("knew", hp) for hp in range(8)]
                VN = [("vnew", hp) for hp in range(8)]
                for ch in range(4):
                    cs_ = slice(ch * 32, (ch + 1) * 32)
                    P.add("dve", lambda e, cs_=cs_: e.tensor_tensor(out=prod[:], in0=Ks[:, cs_, :], in1=qn[:].unsqueeze(1).to_broadcast([128, 32, 64]),
                                                                    op=ALU.mult), r=KS + QN, w=["prod"])
                    P.add("dve", lambda e, cs_=cs_: e.reduce_sum(out=lg[:, cs_], in_=prod[:], axis=AX.X), r=["prod"], w=["lg"])
                P.add("dve", lambda e: e.tensor_tensor(out=prod[:, 0, :], in0=knew[:], in1=qn[:], op=ALU.mult), r=KN + QN, w=["prod"])
                P.add("dve", lambda e: e.reduce_sum(out=lg[:, 128:129], in_=prod[:, 0, :], axis=AX.X), r=["prod"], w=["lg"])
                P.add("dve", lambda e: e.tensor_tensor(out=lg[:], in0=lg[:], in1=fsb[:], op=ALU.add), r=["lg"] + FSB, w=["lg"])
                P.add("act", lambda e: e.activation(out=lg[:], in_=lg[:], func=AF.Exp, accum_out=dsum[:, 0:1]), r=["lg"], w=["lg", "dsum"])
                P.add("dve", lambda e: e.tensor_tensor(out=dsum[:, 1:2], in0=dsum[:, 0:1], in1=esp[:], op=ALU.add), r=["dsum"] + ESP, w=["dsum1"])
                P.add("dve", lambda e: e.reciprocal(out=dsum[:, 2:3], in_=dsum[:, 1:2]), r=["dsum1"], w=["dsum2"])
                for ch in range(4):
                    cs_ = slice(ch * 32, (ch + 1) * 32)
                    P.add("dve", lambda e, cs_=cs_: e.tensor_tensor(out=prod[:], in0=Vs[:, cs_, :], in1=lg[:, cs_].unsqueeze(2).to_broadcast([128, 32, 64]),
                                                                    op=ALU.mult), r=VS + ["lg"], w=["prod"])
                    P.add("dve", lambda e, ch=ch: e.reduce_sum(out=osum[:, ch, :], in_=prod[:].rearrange("p c d -> p d c"), axis=AX.X),
                          r=["prod"], w=[("osum", ch)])
                P.add("dve", lambda e: e.tensor_scalar(out=osum[:, 4, :], in0=vnew[:], scalar1=lg[:, 128:129], scalar2=None, op0=ALU.mult),
                      r=VN + ["lg"], w=[("osum", 4)])
                P.add("dve", lambda e: e.reduce_sum(out=onh[:], in_=osum[:].rearrange("p k d -> p d k"), axis=AX.X),
                      r=[("osum", k) for k in range(5)], w=["onh"])
                P.add("dve", lambda e: e.tensor_scalar(out=onh[:], in0=onh[:], scalar1=dsum[:, 2:3], scalar2=None, op0=ALU.mult),
                      r=["onh", "dsum2"], w=["onh"])
                P.add("dve", lambda e: e.memset(dsum[:, 3:4], 0.0), r=KS + VS + ["onh"], w=[("h", j) for j in range(1, 17)])
                dma("sp", ascr.ap(), onh[:], ["onh"], ["ascr"], "ascr")
                AK = []
                for hp in range(8):
                    head = (hp % 2) * 4 + hp // 2
                    dma("sp", c0.attn[0:NS, head * 64:(head + 1) * 64], ascr.ap()[hp:128:8, :], ["ascr"], [("attn_s", hp)], "attn_s")
                    AK.append(("attn_s", hp))
                dma("sp", ohs[:], oh.ap(), [], ["ohs"], "cst2")
                dma("sp", mkc[:], maskc.ap(), [], ["mkc"], "cst2")
                dma("sp", mvb[:], mv.ap()[0:1, :].partition_broadcast(128), [], ["mvb"], "cst2")
                for h in range(8):
                    gt_ = gbt[h % 2]
                    gk_ = ("gbt", h % 2)
                    P.add("pe", lambda e, h=h: e.matmul(B[7][:, 0:384], lhsT=rb[:, h:h + 1].to_broadcast([32, 128]), rhs=ohs[:], start=True, stop=True),
                          ["rb", "ohs"], ["B7"])
                    P.add("dve", lambda e, gt_=gt_: e.tensor_tensor(out=gt_[:], in0=B[7][:, 0:384], in1=mvb[:], op=ALU.add), ["B7", "mvb"], [gk_])
                    dma("sp", scr.ap()[h], gt_[:], [gk_], [("scr", h)], "scr")
                for kb, off in ((0, 255), (1, 127)):
                    src = bass.AP(scr, off, [[383, 128], [128 * 384, 8], [1, 128]])
                    dma("pool", biasT[:, kb, :].rearrange("p (h q) -> p h q", h=8), src, [("scr", h) for h in range(8)], [("biasT", kb)], "biasT")
                P.add("dve", lambda e: e.tensor_scalar(out=bias1[:], in0=biasT[:, 0, :], scalar1=mkc[:, 0:1], scalar2=None, op0=ALU.add),
                      [("biasT", 0), "mkc"], ["bias1"])
                out_proj(c0, h_s[0:NS, :], ["h_s"], h_s[0:NS, :], ["h_s"], NS, akeys=AK)
                dma("sp", h_all[:, 16, :], xh.ap()[0:128, :], [], [("h", 17)], "x0")
                dma("sp", esink[:], sinks.ap().partition_broadcast(128), [], ["esink"], "cst2")
                dma("sp", trl[:], trilT.ap(), [], ["trl"], "cst2")
                dma("sp", bs_t[:], b_s.ap().rearrange("h t -> t h"), [], ["bs_t"], "cst2", slow=True)
                dma("pool", wsl[:], w_s.ap().rearrange("h t s -> t h s"), [], ["wsl"], "wsl")
                P.add("act", lambda e: e.activation(out=esink[:], in_=esink[:], func=AF.Exp), ["esink"], ["esink"])

                def trw(e):
                    for h in range(8):
                        ins = e.transpose(out=B[0][:, h * 128:(h + 1) * 128], in_=wsl[:, h, :], identity=identb[:])
                    return ins
                P.add("pe", trw, ["wsl", "identb"], ["B0"])
                for h in range(8):
                    P.add("dve", lambda e, h=h: e.tensor_tensor(out=wsT[:, h, :], in0=B[0][:, h * 128:(h + 1) * 128], in1=trl[:], op=ALU.mult),
                          ["B0", "trl"], ["wsT"])
                P.emit()
            if MAXPH < 2:
                return nc

            sA = ExitStack()
            with sA:
                kT_all = sb(sA, "kT_all", (128, NBLK, 128), BF16)
                v_all = sb(sA, "v_all", (128, NBLK, 2, 72), BF16)
                qT = [sb(sA, "qT%d" % t, (128, 512), BF16) for t in range(2)]
                PT = [sb(sA, "PT%d" % t, (128, 2, 1024), BF16) for t in range(2)]
                kvo = sb(sA, "kvo", (128, 256))

                for j in range(1, NBLK - 1):
                    dma("sp", h_all[:, j - 1, :], xh.ap()[j * 128:(j + 1) * 128, :], [], [("h", j)], "x%d" % j)
                XLAST = [True]
                P.add("dve", lambda e: e.memset(v_all[:], 1.0), [], [("v", j) for j in range(NBLK)])
                if DBG == 1:
                    P.emit()
                    return nc
                if DBG == 3:
                    P.emit()
                    return nc
                if DBG == 4:
                    P.emit()
                    return nc

                def stageA(j):
                    cx = CX[j % 2]
                    xap = h_all[:, 16, :] if j == 0 else h_all[:, j - 1, :]
                    xkeys = [("h", 17)] if j == 0 else [("h", j)]
                    norm_T(cx, xap, xkeys)
                    if j >= 1:
                        def fq(e):
                            for t in range(4):
                                for kc in range(8):
                                    ins = e.matmul(cx.a[:, t * 128:(t + 1) * 128], lhsT=W_in[:, kc, t * 128:(t + 1) * 128], rhs=cx.xnT[:, kc, :],
                                                   start=(kc == 0), stop=(kc == 7))
                            return ins
                        P.add("pe", fq, r=[cx.k("xnT")] + WIN, w=[cx.ak])
                        P.add("dve", lambda e: e.scalar_tensor_tensor(out=qT[cx.t][:].rearrange("p (t q) -> p t q", t=4),
                                                                      in0=cx.a[:].rearrange("p (t q) -> p t q", t=4), scalar=0.125,
                                                                      in1=bq8[:].unsqueeze(2).to_broadcast([128, 4, 128]),
                                                                      op0=ALU.mult, op1=ALU.add), r=[cx.ak, "bq8"], w=[cx.k("qT")])

                    def fk(e):
                        for kc in range(8):
                            ins = e.matmul(cx.b[:, 0:128], lhsT=W_in[:, kc, 512:640], rhs=cx.xnT[:, kc, :], start=(kc == 0), stop=(kc == 7))
                        return ins
                    P.add("pe", fk, r=[cx.k("xnT")] + WIN, w=[cx.bk])
                    mm_tok(cx, cx.b, cx.bk, 512, 768, o0=128)
                    P.add("dve", lambda e: e.tensor_scalar(out=kT_all[:, j, :], in0=cx.b[:, 0:128], scalar1=bqk[:, 4:5], scalar2=None, op0=ALU.add),
                          r=[cx.bk, "bqk"], w=[("k", j)])
                    for g in range(2):
                        P.add("act", lambda e, g=g: e.activation(out=v_all[:, j, g, 0:64], in_=cx.b[:, 256 + g * 64:320 + g * 64],
                                                                 func=AF.Identity), r=[cx.bk], w=[("v", j)])
                    if j == NBLK - 1:
                        P.add("act", lambda e: e.activation(out=kvo[:], in_=cx.b[:, 128:384], func=AF.Copy), r=[cx.bk], w=["kvo"])
                        dma("sp", wkv.ap(), kvo[:], ["kvo"], ["wkv"], "wkv")
                    if j == 0:
                        dma("sp", h_all[:, 16, :], xh.ap()[17 * 128:18 * 128, :], [], [("h", 17)], "x17")
                        return
                    mm_tok(cx, cx.c, cx.ck, 768, 1280)
                    gelu_u(cx, cx.c, cx.ck)
                    mm_tok(cx, cx.c, cx.ck, 1280, 1792)
                    gelu_ln(cx, cx.c, cx.ck)

                def stageB(j):
                    cx = CX[j % 2]
                    PTt = PT[cx.t]
                    for g in range(2):
                        for kb in range(2):
                            jb = j - 1 + kb
                            bank, bkey = (cx.b, cx.bk) if kb == 0 else (cx.c, cx.ck)
                            bt = (bias1[:, g * 512:(g + 1) * 512] if (kb == 0 and j <= 2) else biasT[:, kb, g * 512:(g + 1) * 512])
                            btk = "bias1" if (kb == 0 and j <= 2) else ("biasT", kb)

                            def fs(e, g=g, jb=jb, bank=bank, bt=bt):
                                e.matmul(bank[:], lhsT=kT_all[g * 64:(g + 1) * 64, jb, :], rhs=qT[cx.t][g * 64:(g + 1) * 64, :], start=True, stop=False)
                                return e.matmul(bank[:], lhsT=identb[:], rhs=bt, start=False, stop=True)
                            P.add("pe", fs, r=[("k", jb), cx.k("qT"), "identb", btk], w=[bkey])
                            P.add("act", lambda e, g=g, kb=kb, bank=bank: e.activation(out=PTt[:, kb, g * 512:(g + 1) * 512], in_=bank[:], func=AF.Exp),
                                  r=[bkey], w=[cx.k(("PT", kb, g))])
                    for half, bank, bkey in ((0, cx.a, cx.ak), (1, cx.b, cx.bk)):
                        def fpv(e, half=half, bank=bank):
                            for hh in range(4):
                                h = half * 4 + hh
                                for kb in range(2):
                                    ins = e.matmul(bank[:, hh * 128:hh * 128 + 65], lhsT=PTt[:, kb, h * 128:(h + 1) * 128],
                                                   rhs=v_all[:, j - 1 + kb, half, 0:65], start=(kb == 0), stop=(kb == 1))
                            return ins
                        P.add("pe", fpv, r=[cx.k(("PT", 0, half)), cx.k(("PT", 1, half)), ("v", j - 1), ("v", j)], w=[bkey])
                        bv = bank[:].rearrange("p (h c) -> p h c", h=4)
                        hs = slice(half * 4, half * 4 + 4)
                        den = cx.den
                        dk = cx.k(("den", half))
                        P.add("dve", lambda e, bv=bv, hs=hs, den=den: e.tensor_tensor(out=den[:, hs].unsqueeze(2), in0=bv[:, :, 64:65],
                                                                                      in1=esink[:, hs].unsqueeze(2), op=ALU.add),
                              r=[bkey, "esink"], w=[dk])
                        P.add("dve", lambda e, hs=hs, den=den: e.reciprocal(out=den[:, hs], in_=den[:, hs]), r=[dk], w=[dk])
                        P.add("dve", lambda e, bv=bv, hs=hs, half=half, den=den: e.tensor_tensor(
                            out=cx.attn[:, half * 256:(half + 1) * 256].rearrange("p (h d) -> p h d", h=4), in0=bv[:, :, 0:64],
                            in1=den[:, hs].unsqueeze(2).to_broadcast([128, 4, 64]), op=ALU.mult),
                            r=[bkey, dk], w=[cx.k("attn")])

                    def fg(e):
                        for h in range(8):
                            ins = e.matmul(cx.c[:, h * 64:(h + 1) * 64], lhsT=wsT[:, h, :], rhs=cx.vnb[:, h * 64:(h + 1) * 64], start=True, stop=True)
                        return ins
                    P.add("pe", fg, r=["wsT", cx.k("vnb")], w=[cx.ck])
                    P.add("dve", lambda e: e.tensor_tensor(out=cx.t1[:].rearrange("p (h d) -> p h d", h=8), in0=cx.c[:].rearrange("p (h d) -> p h d", h=8),
                                                           in1=bs_t[:].unsqueeze(2).to_broadcast([128, 8, 64]), op=ALU.add),
                          r=[cx.ck, "bs_t"], w=[cx.k("t1")])
                    P.add("dve", lambda e: e.tensor_tensor(out=cx.gm[:], in0=cx.t1[:], in1=cx.ug[:], op=ALU.mult),
                          r=[cx.k("t1"), cx.k("ug")], w=[cx.k("gm")])
                    out_proj(cx, h_all[:, j - 1, :], [("h", j)], h_all[:, j - 1, :], [("h", j)])

                NB = min(NBLK, MAXJ)
                stageA(0)
                if NB > 1:
                    stageA(1)
                for j in range(1, NB):
                    opsB = P.record(lambda: stageB(j))
                    opsA = P.record(lambda: stageA(j + 1)) if j + 1 < NB else []
                    P.merge_sched(opsB, opsA)
                P.emit()
        if MAXPH < 3:
            return nc

        sC = ExitStack()
        with sC:
            W_down = sb(sC, "W_down", (128, 22, D), BF16)
            Wup = [sb(sC, "Wup%d" % i, (128, 8, 256), BF16) for i in range(NWS)]
            gpf = sb(sC, "gpf", (128, D))
            gof = sb(sC, "gof", (128, D))
            identf = sb(sC, "identf", (128, 128))
            cwl = sb(sC, "cwl_s", (NFT, 4, 128))
            cwT = sb(sC, "cwT", (128, 4, NFT))
            flg = sb(sC, "flg", (128, 1))
            h2 = sb(sC, "h2", (128, D), BF16)
            h2T = sb(sC, "h2T", (128, 8, 512), BF16)
            h2Th = sb(sC, "h2Th", (128, 8, 2), BF16)
            h2Ts = sb(sC, "h2Ts", (128, 8, NS), BF16)
            cbuf = [sb(sC, "cbuf%d" % i, (128, 512)) for i in range(NCB)]
            actT = sb(sC, "actT", (128, 22, 512), BF16)
            tails = sb(sC, "tails", (128, NFT, 2))
            upx = [sb(sC, "upx%d" % i, (128, 514)) for i in range(NUX)]
            tpo = sb(sC, "tpo", (88, 128))
            cwr = sb(sC, "cwr_s", (NS, 4, 256))
            ups_b = [h_all[0:NS, 0, 0:256], h_all[0:NS, 0, 256:512]]
            sts = h_all[0:NS, 0, 512:1024].rearrange("p (r c) -> p r c", r=2)
            cs_t = sb(sC, "cs_t", (NS, 256))
            gs_t = sb(sC, "gs_t", (NS, 128))
            act_s = sb(sC, "act_s", (NS, DFF), BF16)
            actTs = sb(sC, "actTs", (128, 22, NS), BF16)

            w_down_v = w_down.ap().rearrange("(kc p) n -> p kc n", p=128)
            dma("sp", gpf[:], g_pre_ffn.ap().partition_broadcast(128), [], ["gpf"], "cst3")
            dma("sp", gof[:], g_post_ffn.ap().partition_broadcast(128), [], ["gof"], "cst3")
            dma("sp", identf[:], ident.ap(), [], ["identf"], "cst3")
            dma("sp", cwl[:], cwl_d.ap(), [], ["cwl"], "cst3")
            dma("sp", flg[:], flag.ap(), [], ["flg"], "cst3")
            dma("sp", cs.ap()[:, 0, :], stt.ap()[:, 1, :], [], ["cs0"], "cs0")
            w_up_v = w_up.ap().rearrange("(kc p) n -> p kc n", p=128)
            wup_seq = [(G, p) for G in range(4) for p in range(22)]

            def load_wup(i):
                G, p = wup_seq[i]
                s = i % NWS
                dma("pool", Wup[s][:], w_up_v[:, :, p * 256:(p + 1) * 256], [], [("Wup", s)], "Wup%d" % s)
            for i in range(NWS - 1):
                load_wup(i)
            for i in range(2):
                dma("pool", W_down[:, 11 * i:11 * i + 11, :], w_down_v[:, 11 * i:11 * i + 11, :], [], ["W_down%d" % i], "W_down")
            WDN = ["W_down0", "W_down1"]

            def trc(e):
                for r_ in range(4):
                    ins = e.transpose(out=B[7][:, r_ * NFT:(r_ + 1) * NFT], in_=cwl[:, r_, :], identity=identf[0:NFT, 0:NFT])
                return ins
            P.add("pe", trc, ["cwl", "identf"], ["B7"])
            P.add("dve", lambda e: e.tensor_copy(out=cwT[:].rearrange("p r f -> p (r f)"), in_=B[7][:, 0:4 * NFT]), ["B7"], ["cwT"])

            def ffn_norm_T(xap, xkeys, dst, dkey, np_=128, c0=0, slot=0, pre=None):
                rstd, rk_ = pre if pre is not None else rms_rstd([(xap, xkeys)], D, "f", np_, slot=slot)
                P.add("dve", lambda e: e.scalar_tensor_tensor(out=h2[0:np_, :], in0=xap, scalar=rstd, in1=gpf[0:np_, :],
                                                              op0=ALU.mult, op1=ALU.mult), r=list(xkeys) + [rk_, "gpf"], w=["h2"])

                def tr(e):
                    for c in range(8):
                        ins = e.transpose(out=B[0][:, c * 128:c * 128 + np_], in_=h2[0:np_, c * 128:(c + 1) * 128], identity=identb[0:np_, 0:np_])
                    return ins
                P.add("pe", tr, r=["h2", "identb"], w=["B0"])
                B0v = B[0][:].rearrange("p (c t) -> p c t", c=8)
                P.add("act", lambda e: e.activation(out=dst, in_=B0v[:, :, c0:np_], func=AF.Copy), r=["B0"], w=[dkey])

            ffn_norm_T(h_all[:, 0, :], [("h", 1)], h2Th[:], "h2Th", c0=126)
            ffn_norm_T(h_s[0:NS, :], ["h_s"], h2Ts[:], "h2Ts", NS)
            P.add("dve", lambda e: e.memset(ssB[:, 0:1], 0.0), r=[("h", 1)], w=[("ups", 0), ("ups", 1), "sts"])

            wi = 0
            cslot = 0
            for G in range(4):
                par, ppar = G % 2, (G + 1) % 2
                if G == 0:
                    pres = [rms_rstd([(h_all[:, 1 + tb, :], [("h", 2 + tb)])], D, "f", slot=tb) for tb in range(4)]
                else:
                    pres = pres_next
                if G == 0:
                    for tb in range(4):
                        ffn_norm_T(h_all[:, 1 + tb, :], [("h", 2 + tb)], h2T[:, :, tb * 128:(tb + 1) * 128], ("h2T", tb), pre=pres[tb])
                H2T = [("h2T", tb) for tb in range(4)]
                prev_side = []
                for p in range(22):
                    s = wi % NWS
                    if G == 0:
                        P.rec = []
                    if wi + NWS - 1 < len(wup_seq):
                        load_wup(wi + NWS - 1)
                    wi += 1
                    cts = []
                    ups = ups_b[p % 2]
                    upk = ("ups", p % 2)
                    if G == 0:
                        def fb7(e, s=s):
                            for t2 in range(2):
                                for kc in range(8):
                                    e.matmul(B[7][:, 2 * t2:2 * t2 + 2], lhsT=Wup[s][:, kc, t2 * 128:(t2 + 1) * 128], rhs=h2Th[:, kc, :],
                                             start=(kc == 0), stop=(kc == 7))
                            for kc in range(8):
                                ins = e.matmul(B[7][0:NS, 256:512], lhsT=h2Ts[:, kc, :], rhs=Wup[s][:, kc, :], start=(kc == 0), stop=(kc == 7))
                            return ins
                        P.add("pe", fb7, r=[("Wup", s), "h2Ts", "h2Th"], w=["B7"])
                        P.add("act", lambda e, ups=ups: e.activation(out=ups[:], in_=B[7][0:NS, 256:512], func=AF.Copy), r=["B7"], w=[upk])
                        P.add("act", lambda e, p=p: e.activation(out=tails[:, 2 * p:2 * p + 2, :], in_=B[7][:, 0:4].rearrange("q (a b) -> q a b", a=2),
                                                                 func=AF.Copy, scale=flg[:, 0:1]),
                              r=["B7", "flg"], w=[("tails", 2 * p), ("tails", 2 * p + 1)])
                    for t2 in range(2):
                        ft = p * 2 + t2
                        bi = 2 + (cslot % 4)

                        def fup(e, s=s, t2=t2, bi=bi):
                            for kc in range(8):
                                ins = e.matmul(B[bi][:], lhsT=Wup[s][:, kc, t2 * 128:(t2 + 1) * 128], rhs=h2T[:, kc, :], start=(kc == 0), stop=(kc == 7))
                            return ins
                        P.add("pe", fup, r=[("Wup", s)] + H2T, w=["B%d" % bi])
                        c = cbuf[cslot % NCB]
                        ck_ = ("cbuf", cslot % NCB)
                        ux = upx[cslot % NUX]
                        uk = ("upx", cslot % NUX)
                        uhk = ("upxh", cslot % NUX)
                        cslot += 1
                        cts.append((c, ck_))
                        w0, w1, w2, bb = (cwT[:, r_, ft:ft + 1] for r_ in range(4))
                        P.add("act", lambda e, c=c, bi=bi, w2=w2, bb=bb: e.activation(out=c[:], in_=B[bi][:], func=AF.Identity, scale=w2, bias=bb),
                              r=["B%d" % bi, "cwT"], w=[ck_])
                        P.add("act", lambda e, ux=ux, bi=bi: e.activation(out=ux[:, 2:514], in_=B[bi][:], func=AF.Copy),
                              r=["B%d" % bi], w=[uk])
                        P.add("pool", lambda e, ux=ux, ft=ft: e.tensor_copy(out=ux[:, 0:2], in_=tails[:, ft, :]), r=[("tails", ft)], w=[uhk])
                        P.add("act", lambda e, bi=bi, ft=ft: e.activation(out=tails[:, ft, :], in_=B[bi][:, 510:512], func=AF.Copy),
                              r=["B%d" % bi], w=[("tails", ft)])
                        P.add("dve", lambda e, c=c, ux=ux, w1=w1: e.scalar_tensor_tensor(out=c[:], in0=ux[:, 1:513], scalar=w1, in1=c[:],
                                                                                         op0=ALU.mult, op1=ALU.add), r=[uk, uhk, ck_, "cwT"], w=[ck_])
                        P.add("dve", lambda e, c=c, ux=ux, w0=w0: e.scalar_tensor_tensor(out=c[:], in0=ux[:, 0:512], scalar=w0, in1=c[:],
                                                                                         op0=ALU.mult, op1=ALU.add), r=[uk, uhk, ck_, "cwT"], w=[ck_])
                    (cg, cgk), (cv_, cvk) = cts
                    P.add("act", lambda e, cg=cg: e.activation(out=cg[:], in_=cg[:], func=AF.Gelu_apprx_tanh), r=[cgk], w=[cgk])
                    P.add("dve", lambda e, cg=cg, cv_=cv_, p=p: e.tensor_tensor(out=actT[:, p, :], in0=cg[:], in1=cv_[:], op=ALU.mult),
                          r=[cgk, cvk], w=[("actT", p)])
                    if G == 0:
                        main_ops, P.rec = P.rec, []
                        for t2 in range(2):
                            ft = p * 2 + t2
                            ocol = (ft % 2) * DFF + (ft // 2) * 128
                            dma("sp", cs.ap()[:, 1, ocol:ocol + 128], ups[:, t2 * 128:(t2 + 1) * 128], [upk], [("cs1", ft)], "cs1_%d" % t2)
                        c0_ = p * 256
                        dma("sp", cwr[:], bass.AP(cwr_d, c0_, [[0, NS], [F2, 4], [1, 256]]), [], ["cwr"], "cwr")
                        dma("sp", sts[:], stp.ap()[:, :, c0_:c0_ + 256], [], ["sts"], "sts")
                        P.add("dve", lambda e, ups=ups: e.tensor_tensor(out=cs_t[:], in0=ups[:], in1=cwr[:, 2, :], op=ALU.mult), r=[upk, "cwr"], w=["cs_t"])
                        P.add("dve", lambda e: e.tensor_tensor(out=cs_t[:], in0=cs_t[:], in1=cwr[:, 3, :], op=ALU.add), r=["cs_t", "cwr"], w=["cs_t"])
                        for r_ in range(2):
                            P.add("dve", lambda e, r_=r_: e.tensor_tensor(out=sts[:, r_, :], in0=sts[:, r_, :], in1=cwr[:, r_, :], op=ALU.mult),
                                  r=["sts", "cwr"], w=["sts"])
                            P.add("dve", lambda e, r_=r_: e.tensor_tensor(out=cs_t[:], in0=cs_t[:], in1=sts[:, r_, :], op=ALU.add),
                                  r=["cs_t", "sts"], w=["cs_t"])
                        P.add("act", lambda e: e.activation(out=gs_t[:], in_=cs_t[:, 0:128], func=AF.Gelu_apprx_tanh), r=["cs_t"], w=["gs_t"])
                        P.add("dve", lambda e, p=p: e.tensor_tensor(out=act_s[:, p * 128:(p + 1) * 128], in0=gs_t[:], in1=cs_t[:, 128:256], op=ALU.mult),
                              r=["gs_t", "cs_t"], w=["act_s"])
                        side_ops, P.rec = P.rec, None
                        P.merge_sched(main_ops, prev_side)
                        prev_side = side_ops
                if G == 0:
                    P.merge_sched(prev_side)
                if G < 3:
                    pres_next = [rms_rstd([(h_all[:, 1 + 4 * (G + 1) + tb, :], [("h", 2 + 4 * (G + 1) + tb)])], D, "f", slot=tb) for tb in range(4)]
                ACT_ALL = [("actT", p) for p in range(22)]
                P.rec = []
                for tb in range(4):
                    dbk = ((0, 6), (1, 7)) if tb % 2 == 0 else ((0, 2), (1, 3))
                    for half, bi in dbk:
                        def fdn(e, tb=tb, half=half, bi=bi):
                            for p in range(22):
                                ins = e.matmul(B[bi][:], lhsT=actT[:, p, tb * 128:(tb + 1) * 128], rhs=W_down[:, p, half * 512:(half + 1) * 512],
                                               start=(p == 0), stop=(p == 21))
                            return ins
                        P.add("pe", fdn, r=ACT_ALL + WDN, w=["B%d" % bi])
                    blk = 4 * G + tb
                    hblk = h_all[:, 1 + blk, :]
                    hk = ("h", 2 + blk)
                    rstd, rk_ = rms_rstd([(B[bi_][:], ["B%d" % bi_]) for (_, bi_) in dbk], D, "y", slot=4 + tb % 2)
                    for half, bi in dbk:
                        sl = slice(half * 512, (half + 1) * 512)
                        P.add("dve", lambda e, bi=bi, sl=sl, rstd=rstd: e.scalar_tensor_tensor(out=B[bi][:], in0=B[bi][:], scalar=rstd, in1=gof[:, sl],
                                                                                   op0=ALU.mult, op1=ALU.mult),
                              r=["B%d" % bi, rk_, "gof"], w=["B%d" % bi])
                        P.add("dve", lambda e, bi=bi, sl=sl, hblk=hblk: e.tensor_tensor(out=hblk[:, sl], in0=B[bi][:], in1=hblk[:, sl], op=ALU.add),
                              r=["B%d" % bi, hk], w=[hk])
                    dma("sp", y.ap()[blk * 128:(blk + 1) * 128, :], hblk, [hk], [("y", blk)], "y%d" % (blk % 4))
                s1_ops, P.rec = P.rec, []
                if G < 3:
                    for tb in range(4):
                        ffn_norm_T(h_all[:, 1 + 4 * (G + 1) + tb, :], [("h", 2 + 4 * (G + 1) + tb)], h2T[:, :, tb * 128:(tb + 1) * 128],
                                   ("h2T", tb), pre=pres_next[tb])
                s2_ops, P.rec = P.rec, None
                P.merge_sched(s1_ops, s2_ops)
                if G == 0:
                    def trs(e):
                        for p in range(22):
                            ins = e.transpose(out=B[0][:, p * NS:(p + 1) * NS], in_=act_s[:, p * 128:(p + 1) * 128], identity=identb[0:NS, 0:NS])
                        return ins
                    P.add("pe", trs, r=["act_s", "identb"], w=["B0"])
                    P.add("act", lambda e: e.activation(out=actTs[:].rearrange("p a n -> p (a n)"), in_=B[0][:, 0:22 * NS], func=AF.Copy),
                          r=["B0"], w=["actTs"])
                    for half, bi in ((0, 5), (1, 6)):
                        def fds(e, half=half, bi=bi):
                            for p in range(22):
                                ins = e.matmul(B[bi][0:NS, :], lhsT=actTs[:, p, :], rhs=W_down[:, p, half * 512:(half + 1) * 512],
                                               start=(p == 0), stop=(p == 21))
                            return ins
                        P.add("pe", fds, r=["actTs"] + WDN, w=["B%d" % bi])
                    rstd, rk_ = rms_rstd([(B[5][0:NS, :], ["B5"]), (B[6][0:NS, :], ["B6"])], D, "ys", NS, slot=6)
                    for half, bi in ((0, 5), (1, 6)):
                        sl = slice(half * 512, (half + 1) * 512)
                        P.add("dve", lambda e, bi=bi, sl=sl, rstd=rstd: e.scalar_tensor_tensor(out=B[bi][0:NS, :], in0=B[bi][0:NS, :], scalar=rstd, in1=gof[0:NS, sl],
                                                                                   op0=ALU.mult, op1=ALU.mult), r=["B%d" % bi, rk_, "gof"], w=["B%d" % bi])
                        P.add("dve", lambda e, bi=bi, sl=sl: e.tensor_tensor(out=h_s[0:NS, sl], in0=B[bi][0:NS, :], in1=h_s[0:NS, sl], op=ALU.add),
                              r=["B%d" % bi, "h_s"], w=["h_s"])
                    dma("sp", ys.ap(), h_s[0:NS, :], ["h_s"], ["ys"], "ys")
            TL = [("tails", ft) for ft in range(NFT)]
            P.add("pe", lambda e: e.transpose(out=B[7][0:88, 0:128], in_=tails[:].rearrange("p f t -> p (f t)"), identity=identf[:]),
                  r=TL + ["identf"], w=["B7"])
            P.add("dve", lambda e: e.tensor_copy(out=tpo[:], in_=B[7][0:88, 0:128]), ["B7"], ["tpo"])
            dma("sp", cp.ap(), tpo[:], ["tpo"], ["cp"], "cp")
            P.emit()
    return nc


def _t5_bucket_np(n):
    n = np.maximum(n, 0)
    max_exact = 16
    nf = np.maximum(n, 1).astype(np.float32)
    large = max_exact + (np.log(nf / max_exact) / np.log(128 / max_exact) * (32 - max_exact)).astype(np.int32)
    large = np.minimum(large, 31)
    return np.where(n < max_exact, n, large)


_NC_CACHE = {}


def _prep(x_prompt, x_sample, cache_win_k, cache_win_v, state_ffn_conv, rel_bias,
          w_in, b_in, attn_sinks, gmlp_ln_g, gmlp_ln_b, gmlp_w_s, gmlp_b_s,
          g_attn_out, g_gmlp_out, w_out, g_pre_mix, g_post_mix, g_pre_ffn, g_post_ffn,
          w_up, ffn_conv_w, ffn_conv_b, w_down):
    f32 = np.float32
    A = lambda a: np.ascontiguousarray(np.asarray(a, dtype=f32))
    x_prompt, x_sample = A(x_prompt), A(x_sample)
    qperm = np.array([(half * 4 + t) * 64 + d for t in range(4) for half in range(2) for d in range(64)])
    in_perm = np.concatenate([qperm, np.arange(512, 1792)])
    w_in_p = A(np.asarray(w_in)[0][:, in_perm])
    b_in_p = A(np.asarray(b_in)[0][in_perm][None, :])
    up_perm = np.array([(ft % 2) * DFF + (ft // 2) * 128 + i for ft in range(NFT) for i in range(128)])
    w_up_p = A(np.asarray(w_up)[0][:, up_perm])
    cw4 = np.concatenate([np.asarray(ffn_conv_w)[0], np.asarray(ffn_conv_b)[0][None, :]], axis=0)[:, up_perm]
    cwr = A(cw4)
    cwl = A(cw4.reshape(4, NFT, 128).transpose(1, 0, 2))
    hperm = np.array([(hp % 2) * 4 + hp // 2 for hp in range(8)])
    sinks = A(np.asarray(attn_sinks)[0][None, :])
    sinks_p = A(np.asarray(attn_sinks)[0][hperm][None, :])
    g_ao = A(np.concatenate([np.asarray(g_attn_out)[0], np.asarray(g_gmlp_out)[0]])[None, :])
    ident = np.eye(128, dtype=f32)
    dist = np.arange(383) - 127
    bk = _t5_bucket_np(dist)
    ok = (dist >= 0) & (dist <= 128)
    oh = np.zeros((32, 384), f32)
    oh[bk[ok], np.nonzero(ok)[0]] = 1.0
    mv = np.zeros((8, 384), f32)
    mv[:, :383][:, ~ok] = NEG
    mv[:, 383] = NEG
    oh2 = np.zeros((32, 129), f32)
    oh2[_t5_bucket_np(128 - np.arange(129)), np.arange(129)] = 1.0
    trilT = np.triu(np.ones((128, 128), f32))

    ck = A(cache_win_k)[0].reshape(128, 128, 128)
    cv = A(cache_win_v)[0].reshape(128, 128, 128)
    st = A(state_ffn_conv)[0]
    stp = np.ascontiguousarray(st[:, :, up_perm])

    shared = dict(rel_bias=A(rel_bias), w_in=w_in_p, b_in=b_in_p, sinks=sinks, sinks_p=sinks_p,
                  ln_g=A(gmlp_ln_g), ln_b=A(gmlp_ln_b), w_s=A(gmlp_w_s)[0], b_s=A(gmlp_b_s)[0], g_ao=g_ao,
                  w_out=A(w_out)[0], g_pre_mix=A(g_pre_mix), g_post_mix=A(g_post_mix), g_pre_ffn=A(g_pre_ffn),
                  g_post_ffn=A(g_post_ffn), w_up=w_up_p, cwl=cwl, cwr=cwr, w_down=A(w_down)[0],
                  ident=ident, oh=oh, oh2=oh2, mv=mv, trilT=trilT)
    in_maps = []
    for c in range(NCORES):
        b, part = c // 4, c % 4
        s0 = part * TOK
        xh = np.zeros((NBLK * 128, D), f32)
        lo = s0 - 256
        if lo >= 0:
            xh[:] = x_prompt[b, lo:s0 + TOK]
        else:
            xh[256:] = x_prompt[b, 0:TOK]
        m = dict(shared)
        m.update(xh=xh, xs=np.ascontiguousarray(x_sample[c * NS:(c + 1) * NS, 0, :]),
                 ck=np.ascontiguousarray(ck[c * NS:(c + 1) * NS]), cv=np.ascontiguousarray(cv[c * NS:(c + 1) * NS]),
                 st=np.ascontiguousarray(st[c * NS:(c + 1) * NS]), stp=np.ascontiguousarray(stp[c * NS:(c + 1) * NS]),
                 flag=np.full((128, 1), 0.0 if part == 0 else 1.0, f32),
                 maskc=np.full((128, 1), NEG if part == 0 else 0.0, f32))
        in_maps.append(m)

    return in_maps, up_perm


def _assemble(R, up_perm):
    f32 = np.float32

    y_prompt = np.stack([np.concatenate([R[b * 4 + p]["y"] for p in range(4)], axis=0) for b in range(2)])
    y_sample = np.concatenate([R[c]["ys"] for c in range(NCORES)], axis=0)[:, None, :]
    wk_p = np.stack([R[b * 4 + 3]["wkv"][:, 0:128].reshape(128, 2, 64) for b in range(2)])[None]
    wv_p = np.stack([R[b * 4 + 3]["wkv"][:, 128:256].reshape(128, 2, 64) for b in range(2)])[None]
    wk_s = np.concatenate([R[c]["wks"] for c in range(NCORES)], axis=0).reshape(1, 128, 128, 2, 64)
    wv_s = np.concatenate([R[c]["wvs"] for c in range(NCORES)], axis=0).reshape(1, 128, 128, 2, 64)
    gv_s = np.concatenate([R[c]["gv"] for c in range(NCORES)], axis=0)[None, :, None, :]
    inv = np.argsort(up_perm)
    cps = []
    for b in range(2):
        t = R[b * 4 + 3]["cp"].reshape(NFT, 2, 128).transpose(1, 0, 2).reshape(2, F2)
        cps.append(t[:, inv])
    c_p = np.stack(cps)[None]
    c_s = np.concatenate([R[c]["cs"] for c in range(NCORES)], axis=0)[None]
    outs = (y_prompt, y_sample, wk_p, wv_p, wk_s, wv_s, gv_s, c_p, c_s)
    return tuple(np.ascontiguousarray(o.astype(f32)) for o in outs)


def kernel(**inputs):
    in_maps, up_perm = _prep(**inputs)
    if "nc" not in _NC_CACHE:
        _NC_CACHE["nc"] = build_program()
    nc = _NC_CACHE["nc"]
    res = run_bass_kernel_spmd(nc, in_maps, core_ids=list(range(NCORES)))
    return _assemble(res.results, up_perm)
```

```python
from contextlib import ExitStack
import numpy as np
import concourse.bass as bass
import concourse.mybir as mybir
from concourse.bass_utils import run_bass_kernel_spmd

F32 = mybir.dt.float32
BF16 = mybir.dt.bfloat16
AF = mybir.ActivationFunctionType
ALU = mybir.AluOpType
AX = mybir.AxisListType

NCORES = 8
D = 1024
NBLK = 18
TOK = 2048
NS = 16
DFF = 2816
F2 = 5632
NFT = 44
EPS = 1e-6
NEG = -30000.0
NCB = 5
NUX = 3
MAXPH = 3
MAXJ = 99
DBG = 0
NWS = 4


class Prog:
    def __init__(self, nc, es):
        self.nc, self.es = nc, es
        self.ops = []
        self.lastw, self.readers = {}, {}
        self.sems, self.count = {}, {}
        self.waited = {e: {} for e in ("pe", "act", "dve", "pool", "sp")}
        self.rec = None

    def record(self, f):
        self.rec = []
        f()
        ops, self.rec = self.rec, None
        return ops

    def merge(self, a, b):
        ia = ib = 0
        while ia < len(a) or ib < len(b):
            if ib >= len(b) or (ia < len(a) and ia * len(b) <= ib * len(a)):
                self.add(*a[ia]); ia += 1
            else:
                self.add(*b[ib]); ib += 1

    @staticmethod
    def _estimate(op):
        eng, fn = op[0], op[1]
        rec = []

        class _Stub:
            def __getattr__(self, name):
                def call(*a, **kw):
                    out = kw.get("out", a[0] if a else None)
                    n = 1
                    try:
                        for d in out.shape[1:]:
                            n *= int(d)
                    except Exception:
                        n = 512
                    rec.append((name, n))
                    return self
                return call
        try:
            fn(_Stub())
        except Exception:
            return 1.0
        t = 0.0
        for name, n in rec:
            if eng == "pe":
                t += 0.03 + n / 1200.0
            elif name == "dma_start":
                t += 2.0
            else:
                t += 0.12 + n / 1000.0
        return max(t, 0.05)

    def merge_sched(self, *streams):
        streams = [st for st in streams if st]
        isbank = lambda k: isinstance(k, str) and len(k) == 2 and k[0] == "B" and k[1].isdigit()
        deps, est = [], []
        for st in streams:
            lastw, readers, dl = {}, {}, []
            for i, (eng, fn, r, w, dma) in enumerate(st):
                w2 = list(w) + [k for k in r if isbank(k)]
                d = set()
                for k in r:
                    if k in lastw:
                        d.add(lastw[k])
                for k in w2:
                    if k in lastw:
                        d.add(lastw[k])
                    d.update(readers.get(k, ()))
                for k in r:
                    readers.setdefault(k, []).append(i)
                for k in w2:
                    lastw[k] = i
                    readers[k] = []
                dl.append(d)
            deps.append(dl)
            est.append([self._estimate(op) for op in st])
        free = {}
        fin = [dict() for _ in streams]
        ptr = [0] * len(streams)
        while True:
            best = None
            for s_, st in enumerate(streams):
                i = ptr[s_]
                if i >= len(st):
                    continue
                eng = st[i][0]
                ready = max([fin[s_][d] for d in deps[s_][i]] + [0.0])
                start = max(ready, free.get(eng, 0.0))
                key = (start, i / len(st))
                if best is None or key < best[0]:
                    best = (key, s_, i, eng, start)
            if best is None:
                break
            _, s_, i, eng, start = best
            end = start + est[s_][i] + 0.15
            fin[s_][i] = end
            free[eng] = end
            ptr[s_] += 1
            self.add(*streams[s_][i])

    def sem(self, name):
        if name not in self.sems:
            self.sems[name] = self.es.enter_context(self.nc.semaphore(name))
            self.count[name] = 0
        return self.sems[name]

    def add(self, eng, fn, r=(), w=(), dma=None):
        if self.rec is not None:
            self.rec.append((eng, fn, tuple(r), tuple(w), dma))
            return
        w = list(w) + [k for k in r if isinstance(k, str) and len(k) == 2 and k[0] == "B" and k[1].isdigit()]
        deps = set()
        for k in r:
            if k in self.lastw:
                deps.add(self.lastw[k])
        for k in w:
            if k in self.lastw:
                deps.add(self.lastw[k])
            for t in self.readers.get(k, ()):
                deps.add(t)
        deps = set((sn, self.count[sn]) if sn.startswith("d_") else (sn, v) for (sn, v) in deps)
        if dma is None:
            sname, inc = "c_" + eng, 1
        else:
            sname, inc = "d_" + dma, 16
        self.sem(sname)
        self.count[sname] += inc
        tok = (sname, self.count[sname])
        for k in r:
            self.readers.setdefault(k, []).append(tok)
        for k in w:
            self.lastw[k] = tok
            self.readers[k] = []
        self.ops.append((eng, fn, deps, tok, inc))

    def emit(self):
        with self.nc.Block() as block:
            for eng, deco in (("pe", block.tensor), ("act", block.scalar), ("dve", block.vector),
                              ("pool", block.gpsimd), ("sp", block.sync)):
                ops = [o for o in self.ops if o[0] == eng]

                def body(e, ops=ops, eng=eng):
                    wt = self.waited[eng]
                    for (_, fn, deps, tok, inc) in ops:
                        need = {}
                        for (s, v) in deps:
                            need[s] = max(need.get(s, 0), v)
                        for s, v in need.items():
                            if wt.get(s, 0) < v:
                                e.wait_ge(self.sems[s], v)
                                wt[s] = v
                        ins = fn(e)
                        ins.then_inc(self.sems[tok[0]], inc)
                    if eng == "sp":
                        for s, v in self.count.items():
                            if s.startswith("d_") and wt.get(s, 0) < v:
                                e.wait_ge(self.sems[s], v)
                                wt[s] = v

                deco(body)
        self.ops = []


def build_program():
    nc = bass.Bass("TRN2", target_bir_lowering=False)

    def din(name, shape):
        return nc.dram_tensor(name, list(shape), F32, kind="ExternalInput")

    def dout(name, shape):
        return nc.dram_tensor(name, list(shape), F32, kind="ExternalOutput")

    xh = din("xh", (NBLK * 128, D))
    xs = din("xs", (NS, D))
    ck = din("ck", (NS, 128, 128))
    cv = din("cv", (NS, 128, 128))
    stt = din("st", (NS, 2, F2))
    stp = din("stp", (NS, 2, F2))
    flag = din("flag", (128, 1))
    maskc = din("maskc", (128, 1))
    rel_bias = din("rel_bias", (32, 8))
    w_in = din("w_in", (D, 1792))
    b_in = din("b_in", (1, 1792))
    sinks = din("sinks", (1, 8))
    sinks_p = din("sinks_p", (128, 1))
    ln_g = din("ln_g", (1, 512))
    ln_b = din("ln_b", (1, 512))
    w_s = din("w_s", (8, 128, 128))
    b_s = din("b_s", (8, 128))
    g_ao = din("g_ao", (1, 1024))
    w_out = din("w_out", (D, D))
    g_pre_mix = din("g_pre_mix", (1, D))
    g_post_mix = din("g_post_mix", (1, D))
    g_pre_ffn = din("g_pre_ffn", (1, D))
    g_post_ffn = din("g_post_ffn", (1, D))
    w_up = din("w_up", (D, F2))
    cwl_d = din("cwl", (NFT, 4, 128))
    cwr_d = din("cwr", (4, F2))
    w_down = din("w_down", (DFF, D))
    ident = din("ident", (128, 128))
    oh = din("oh", (32, 384))
    oh2 = din("oh2", (32, 129))
    mv = din("mv", (8, 384))
    trilT = din("trilT", (128, 128))

    y = dout("y", (TOK, D))
    ys = dout("ys", (NS, D))
    wkv = dout("wkv", (128, 256))
    wks = dout("wks", (NS, 128, 128))
    wvs = dout("wvs", (NS, 128, 128))
    gv = dout("gv", (NS, 512))
    cp = dout("cp", (88, 128))
    cs = dout("cs", (NS, 2, F2))

    scr = nc.dram_tensor("scr", [8, 128, 384], F32)
    scr2 = nc.dram_tensor("scr2", [8, 129], F32)
    qscr = nc.dram_tensor("qscr", [NS, 512], F32)
    kvscr = nc.dram_tensor("kvscr", [NS, 256], F32)
    kvscr2 = nc.dram_tensor("kvscr2", [128, 128], F32)
    ascr = nc.dram_tensor("ascr", [128, 64], F32)

    es = ExitStack()
    with es:
        P = Prog(nc, es)

        def sb(stack, name, shape, dt=F32):
            return stack.enter_context(nc.sbuf_tensor(name, list(shape), dt))

        def dma(eng, out, in_, r, w, key, slow=False):
            if slow:
                P.add(eng, lambda e: e.dma_start(out=out, in_=in_, allow_slow_non_contiguous=True), r, w, dma=key)
            else:
                P.add(eng, lambda e: e.dma_start(out=out, in_=in_), r, w, dma=key)

        h_all = sb(es, "h_all", (128, 17, D))
        h_s = sb(es, "h_s", (128, D))
        identb = sb(es, "identb", (128, 128), BF16)
        banks = [es.enter_context(nc.psum_tensor("bank%d" % i, [128, 1024], BF16)) for i in range(2)]
        for i in range(2, 8):
            banks.append(es.enter_context(nc.psum_tensor("bank%d" % i, [128, 512], F32)))
        B = banks
        junks = [sb(es, "junk0", (128, D), BF16)]
        junk = junks[0]
        ssA = sb(es, "ssA", (128, 32))
        ssB = sb(es, "ssB", (128, 8))
        onesr = sb(es, "onesr", (1, 128), BF16)
        epsb = sb(es, "epsb", (128, 1))

        def rms_rstd(srcs, n, tag, np_=128, slot=0, jk=0):
            st_ = ssA
            o = 4 * slot
            kk = lambda nm: ("st", slot, nm)
            for i, (ap, rk) in enumerate(srcs):
                c = st_[0:np_, o + i:o + i + 1]
                jv = junks[jk][0:np_, 0:ap.shape[-1]] if len(ap.shape) == 2 else junks[jk][0:np_, 0:512]
                P.add("act", lambda e, ap=ap, c=c, jv=jv: e.activation(out=jv, in_=ap, func=AF.Square, accum_out=c),
                      r=rk, w=[kk("ss%d" % i), "junk%d" % jk])
            if len(srcs) == 2:
                P.add("dve", lambda e: e.tensor_tensor(out=st_[0:np_, o:o + 1], in0=st_[0:np_, o:o + 1], in1=st_[0:np_, o + 1:o + 2], op=ALU.add),
                      r=[kk("ss0"), kk("ss1")], w=[kk("ss0")])
            P.add("act", lambda e: e.activation(out=st_[0:np_, o + 2:o + 3], in_=st_[0:np_, o:o + 1], func=AF.Ln, scale=1.0 / n, bias=epsb[0:np_, :]),
                  r=[kk("ss0"), "epsb"], w=[kk("std")])
            P.add("act", lambda e: e.activation(out=st_[0:np_, o + 3:o + 4], in_=st_[0:np_, o + 2:o + 3], func=AF.Exp, scale=-0.5),
                  r=[kk("std")], w=[kk("rstd")])
            return st_[0:np_, o + 3:o + 4], kk("rstd")

        sAB = ExitStack()
        with sAB:
            W_in = sb(sAB, "W_in", (128, 8, 1792), BF16)
            W_out = sb(sAB, "W_out", (128, 8, D), BF16)
            junks.append(sb(sAB, "junk1", (128, D), BF16))
            gpo = sb(sAB, "gpo", (128, D))
            lng = sb(sAB, "lng", (128, 512))
            lnb = sb(sAB, "lnb", (128, 512))
            bqk = sb(sAB, "bqk", (128, 5))
            bq8 = sb(sAB, "bq8", (128, 4))
            gT = sb(sAB, "gT", (128, 16))
            brow = sb(sAB, "brow", (1, 1792), BF16)

            class Cx:
                pass
            CX = []
            for t in range(2):
                cx = Cx()
                cx.t = t
                cx.xn = sb(sAB, "xn%d" % t, (128, D), BF16)
                cx.xnT = sb(sAB, "xnT%d" % t, (128, 8, 128), BF16)
                cx.ug = sb(sAB, "ug%d" % t, (128, 512))
                cx.vgg = sb(sAB, "vgg%d" % t, (128, 512))
                cx.vnb = sb(sAB, "vnb%d" % t, (128, 512), BF16)
                cx.t1 = sb(sAB, "t1%d" % t, (128, 512))
                cx.gm = sb(sAB, "gm%d" % t, (128, 512))
                cx.attn = sb(sAB, "attn%d" % t, (128, 512))
                cx.amix = sb(sAB, "amix%d" % t, (128, D), BF16)
                cx.aT = sb(sAB, "aT%d" % t, (128, 8, 128), BF16)
                cx.lnst = sb(sAB, "lnst%d" % t, (128, 8))
                cx.den = sb(sAB, "den%d" % t, (128, 8))
                cx.T = B[t]
                cx.Tk = "B%d" % t
                cx.a, cx.b, cx.c = (B[2 + 3 * t], B[3 + 3 * t], B[4 + 3 * t])
                cx.ak, cx.bk, cx.ck = ("B%d" % (2 + 3 * t), "B%d" % (3 + 3 * t), "B%d" % (4 + 3 * t))
                cx.k = (lambda nm, t=t: (nm, "cx", t))
                CX.append(cx)
            vn = CX[1].vgg

            dma("pool", identb[:], ident.ap(), [], ["identb"], "cstp")
            w_in_v = w_in.ap().rearrange("(kc p) n -> p kc n", p=128)
            for i in range(4):
                dma("pool", W_in[:, 2 * i:2 * i + 2, :], w_in_v[:, 2 * i:2 * i + 2, :], [], ["W_in%d" % i], "W_in")
            WIN = ["W_in%d" % i for i in range(4)]
            w_out_v = w_out.ap().rearrange("(kc p) n -> p kc n", p=128)
            for i in range(2):
                dma("pool", W_out[:, 4 * i:4 * i + 4, :], w_out_v[:, 4 * i:4 * i + 4, :], [], ["W_out%d" % i], "W_out")
            WOUT = ["W_out0", "W_out1"]
            dma("pool", brow[:], b_in.ap(), [], ["brow"], "cstp")
            dma("sp", h_s[0:NS, :], xs.ap(), [], ["h_s"], "h_s")
            dma("sp", gT[:, 0:8], g_pre_mix.ap()[0, :].rearrange("(c p) -> p c", p=128), [], ["gT"], "cst", slow=True)
            dma("sp", gT[:, 8:16], g_ao.ap()[0, :].rearrange("(c p) -> p c", p=128), [], ["gT"], "cst", slow=True)
            dma("sp", bqk[:], b_in.ap()[0, 0:640].rearrange("(t p) -> p t", p=128), [], ["bqk"], "cst", slow=True)
            dma("sp", gpo[:], g_post_mix.ap().partition_broadcast(128), [], ["gpo"], "cst")
            dma("sp", lng[:], ln_g.ap().partition_broadcast(128), [], ["lng"], "cst")
            dma("sp", lnb[:], ln_b.ap().partition_broadcast(128), [], ["lnb"], "cst")
            rb = sb(sAB, "rb", (32, 8))
            wsT = sb(sAB, "wsT", (128, 8, 128), BF16)
            bs_t = sb(sAB, "bs_t", (128, 8))
            esink = sb(sAB, "esink", (128, 8))
            biasT = sb(sAB, "biasT", (128, 2, 1024), BF16)
            bias1 = sb(sAB, "bias1", (128, 1024), BF16)
            ohs = sb(sAB, "ohs", (32, 384))
            mkc = sb(sAB, "mkc", (128, 1))
            oh2s = sb(sAB, "oh2s", (32, 129))
            Gs2 = sb(sAB, "Gs2", (8, 129))
            dma("sp", rb[:], rel_bias.ap(), [], ["rb"], "cst")
            dma("sp", oh2s[:], oh2.ap(), [], ["oh2s"], "cst")
            P.add("pe", lambda e: e.matmul(B[6][0:8, 0:129], lhsT=rb[:], rhs=oh2s[:], start=True, stop=True), ["rb", "oh2s"], ["B6"])
            P.add("dve", lambda e: e.tensor_copy(out=Gs2[:], in_=B[6][0:8, 0:129]), ["B6"], ["Gs2"])
            dma("sp", scr2.ap(), Gs2[:], ["Gs2"], ["scr2"], "scr2")
            P.add("dve", lambda e: e.memset(onesr[:], 1.0), [], ["onesr"])
            P.add("dve", lambda e: e.memset(epsb[:], EPS), [], ["epsb"])
            P.add("dve", lambda e: e.tensor_scalar(out=bq8[:], in0=bqk[:, 0:4], scalar1=0.125, scalar2=None, op0=ALU.mult),
                  ["bqk"], ["bq8"])
            for kc in range(8):
                P.add("dve", lambda e, kc=kc: e.tensor_scalar(out=W_in[:, kc, :], in0=W_in[:, kc, :], scalar1=gT[:, kc:kc + 1], scalar2=None, op0=ALU.mult),
                      ["gT", "W_in%d" % (kc // 2)], ["W_in%d" % (kc // 2)])
                P.add("dve", lambda e, kc=kc: e.tensor_scalar(out=W_out[:, kc, :], in0=W_out[:, kc, :], scalar1=gT[:, 8 + kc:9 + kc], scalar2=None, op0=ALU.mult),
                      ["gT", "W_out%d" % (kc // 4)], ["W_out%d" % (kc // 4)])

            def norm_T(cx, xap, xkeys, np_=128):
                rstd, rk_ = rms_rstd([(xap, xkeys)], D, "x", np_, slot=4 * cx.t, jk=cx.t)
                P.add("dve", lambda e: e.tensor_scalar(out=cx.xn[0:np_, :], in0=xap, scalar1=rstd, scalar2=None, op0=ALU.mult),
                      r=list(xkeys) + [rk_], w=[cx.k("xn")])

                def tr(e):
                    for c in range(8):
                        ins = e.transpose(out=cx.T[:, c * 128:c * 128 + np_], in_=cx.xn[0:np_, c * 128:(c + 1) * 128],
                                          identity=identb[0:np_, 0:np_])
                    return ins
                P.add("pe", tr, r=[cx.k("xn"), "identb"], w=[cx.Tk])
                Tv = cx.T[:].rearrange("p (c t) -> p c t", c=8)
                if np_ == 128:
                    P.add("dve", lambda e: e.tensor_copy(out=cx.xnT[:].rearrange("p c t -> p (c t)"), in_=cx.T[:, 0:1024]),
                          r=[cx.Tk], w=[cx.k("xnT")])
                else:
                    P.add("act", lambda e: e.activation(out=cx.xnT[:, :, 0:np_], in_=Tv[:, :, 0:np_], func=AF.Copy),
                          r=[cx.Tk], w=[cx.k("xnT")])

            def mm_tok(cx, bank, bkey, c0, c1, np_=128, o0=0):
                n = c1 - c0

                def f(e):
                    for kc in range(8):
                        e.matmul(bank[0:np_, o0:o0 + n], lhsT=cx.xnT[:, kc, 0:np_], rhs=W_in[:, kc, c0:c1], start=(kc == 0), stop=False)
                    return e.matmul(bank[0:np_, o0:o0 + n], lhsT=onesr[0:1, 0:np_], rhs=brow[0:1, c0:c1], start=False, stop=True)
                P.add("pe", f, r=[cx.k("xnT"), "brow", "onesr"] + WIN, w=[bkey])

            def gelu_u(cx, bank, bkey, np_=128):
                P.add("act", lambda e: e.activation(out=cx.ug[0:np_, :], in_=bank[0:np_, :], func=AF.Gelu), r=[bkey], w=[cx.k("ug")])

            def gelu_ln(cx, bank, bkey, np_=128, want_f32=False):
                vgg, lnst = cx.vgg, cx.lnst
                kv_, kl = cx.k("vgg"), (lambda i: cx.k("ln%d" % i))
                P.add("act", lambda e: e.activation(out=vgg[0:np_, :], in_=bank[0:np_, :], func=AF.Gelu), r=[bkey], w=[kv_])
                P.add("dve", lambda e: e.reduce_sum(out=lnst[0:np_, 0:1], in_=vgg[0:np_, :], axis=AX.X), r=[kv_], w=[kl(0)])
                P.add("dve", lambda e: e.tensor_scalar(out=lnst[0:np_, 1:2], in0=lnst[0:np_, 0:1], scalar1=-1.0 / 512, scalar2=None, op0=ALU.mult),
                      r=[kl(0)], w=[kl(1)])
                P.add("act", lambda e: e.activation(out=junks[cx.t][0:np_, 0:512], in_=vgg[0:np_, :], func=AF.Square, bias=lnst[0:np_, 1:2],
                                                    accum_out=lnst[0:np_, 2:3]), r=[kv_, kl(1)], w=[kl(2), "junk%d" % cx.t])
                P.add("act", lambda e: e.activation(out=lnst[0:np_, 3:4], in_=lnst[0:np_, 2:3], func=AF.Ln, scale=1.0 / 512, bias=epsb[0:np_, :]),
                      r=[kl(2), "epsb"], w=[kl(3)])
                P.add("act", lambda e: e.activation(out=lnst[0:np_, 4:5], in_=lnst[0:np_, 3:4], func=AF.Exp, scale=-0.5), r=[kl(3)], w=[kl(4)])
                P.add("dve", lambda e: e.tensor_scalar(out=vgg[0:np_, :], in0=vgg[0:np_, :], scalar1=lnst[0:np_, 1:2], scalar2=lnst[0:np_, 4:5],
                                                       op0=ALU.add, op1=ALU.mult), r=[kv_, kl(1), kl(4)], w=[kv_])
                P.add("dve", lambda e: e.tensor_tensor(out=vgg[0:np_, :], in0=vgg[0:np_, :], in1=lng[0:np_, :], op=ALU.mult),
                      r=[kv_, "lng"], w=[kv_])
                if want_f32:
                    P.add("dve", lambda e: e.tensor_tensor(out=vn[0:np_, :], in0=vgg[0:np_, :], in1=lnb[0:np_, :], op=ALU.add),
                          r=[kv_, "lnb"], w=["vn"])
                else:
                    P.add("dve", lambda e: e.tensor_tensor(out=cx.vnb[0:np_, :], in0=vgg[0:np_, :], in1=lnb[0:np_, :], op=ALU.add),
                          r=[kv_, "lnb"], w=[cx.k("vnb")])

            def out_proj(cx, xap, xkeys, hap, hkeys, np_=128, akeys=None):
                akeys = [cx.k("attn")] if akeys is None else list(akeys)
                attn, gm, amix, aT, t1 = cx.attn, cx.gm, cx.amix, cx.aT, cx.t1
                rstd, rk_ = rms_rstd([(attn[0:np_, :], akeys)], 512, "a", np_, slot=4 * cx.t + 1, jk=cx.t)
                P.add("dve", lambda e, rstd=rstd: e.tensor_scalar(out=amix[0:np_, 0:512], in0=attn[0:np_, :], scalar1=rstd, scalar2=None, op0=ALU.mult),
                      r=akeys + [rk_], w=[cx.k("amixa")])
                rstd, rk_ = rms_rstd([(gm[0:np_, :], [cx.k("gm")])], 512, "g", np_, slot=4 * cx.t + 2, jk=cx.t)
                P.add("dve", lambda e, rstd=rstd: e.tensor_scalar(out=amix[0:np_, 512:1024], in0=gm[0:np_, :], scalar1=rstd, scalar2=None, op0=ALU.mult),
                      r=[cx.k("gm"), rk_], w=[cx.k("amixg")])

                def tr(e):
                    for c in range(8):
                        ins = e.transpose(out=cx.T[:, c * 128:c * 128 + np_], in_=amix[0:np_, c * 128:(c + 1) * 128],
                                          identity=identb[0:np_, 0:np_])
                    return ins
                P.add("pe", tr, r=[cx.k("amixa"), cx.k("amixg"), "identb"], w=[cx.Tk])
                Tv = cx.T[:].rearrange("p (c t) -> p c t", c=8)
                if np_ == 128:
                    P.add("dve", lambda e: e.tensor_copy(out=aT[:].rearrange("p c t -> p (c t)"), in_=cx.T[:, 0:1024]), r=[cx.Tk], w=[cx.k("aT")])
                else:
                    P.add("act", lambda e: e.activation(out=aT[:, :, 0:np_], in_=Tv[:, :, 0:np_], func=AF.Copy), r=[cx.Tk], w=[cx.k("aT")])
                ob = ((0, cx.a, cx.ak), (1, cx.b, cx.bk))
                for half, bank, bkey in ob:
                    def f(e, half=half, bank=bank):
                        for kc in range(8):
                            ins = e.matmul(bank[0:np_, :], lhsT=aT[:, kc, 0:np_], rhs=W_out[:, kc, half * 512:(half + 1) * 512],
                                           start=(kc == 0), stop=(kc == 7))
                        return ins
                    P.add("pe", f, r=[cx.k("aT")] + WOUT, w=[bkey])
                rstd, rk_ = rms_rstd([(cx.a[0:np_, :], [cx.ak]), (cx.b[0:np_, :], [cx.bk])], D, "o", np_, slot=4 * cx.t + 3, jk=cx.t)
                for half, bank, bkey in ob:
                    sl = slice(half * 512, (half + 1) * 512)
                    P.add("dve", lambda e, bank=bank, sl=sl, rstd=rstd: e.scalar_tensor_tensor(out=t1[0:np_, :], in0=bank[0:np_, :], scalar=rstd,
                                                                                in1=gpo[0:np_, sl], op0=ALU.mult, op1=ALU.mult),
                          r=[bkey, rk_, "gpo"], w=[cx.k("t1")])
                    P.add("dve", lambda e, sl=sl: e.tensor_tensor(out=hap[:, sl], in0=t1[0:np_, :], in1=xap[:, sl], op=ALU.add),
                          r=[cx.k("t1")] + list(xkeys), w=list(hkeys))

            sB = ExitStack()
            with sB:
                Ks = h_all[:, 0:8, :].rearrange("p b (c d) -> p (b c) d", d=64)
                Vs = h_all[:, 8:16, :].rearrange("p b (c d) -> p (b c) d", d=64)
                knew = sb(sB, "knew", (128, 64))
                wsl = sb(sB, "wsl", (128, 8, 128), BF16)
                trl = sb(sB, "trl", (128, 128))
                mvb = sb(sB, "mvb", (128, 384))
                gbt = [sb(sB, "gbt%d" % i, (128, 384)) for i in range(2)]
                vnew = sb(sB, "vnew", (128, 64))
                qn = sb(sB, "qn", (128, 64))
                prod = sb(sB, "prod", (128, 32, 64))
                lg = sb(sB, "lg", (128, 129))
                fsb = sb(sB, "fsb", (128, 129))
                esp = sb(sB, "esp", (128, 1))
                dsum = sb(sB, "dsum", (128, 4))
                osum = sb(sB, "osum", (128, 5, 64))
                onh = sb(sB, "onh", (128, 64))
                qs_t = sb(sB, "qs_t", (NS, 512))
                kvs = sb(sB, "kvs", (NS, 256))
                ws00 = sb(sB, "ws00", (NS, 8))
                bs0 = sb(sB, "bs0", (NS, 8))

                dma("sp", ws00[:], bass.AP(w_s, 0, [[0, NS], [128 * 128, 8]]), [], ["ws00"], "cst", slow=True)
                dma("sp", bs0[:], bass.AP(b_s, 0, [[0, NS], [128, 8]]), [], ["bs0"], "cst", slow=True)
                def kv_dmas(phases):
                  for T_, src_, nm in phases:
                      for hp in range(8):
                          g = hp % 2
                          ps = slice(hp, 128, 8)
                          head = (hp % 2) * 4 + hp // 2
                          if T_ is None:
                              dma("sp", fsb[ps, :], scr2.ap()[head:head + 1, :].partition_broadcast(16), ["scr2"], [("fsb", hp)], "fe")
                              if hp == 0:
                                  dma("sp", esp[:], sinks_p.ap(), [], [("esp", h_) for h_ in range(8)], "fe")
                          elif hp < 2:
                              dma("sp", T_[ps, :, :], src_.ap()[:, :, g * 64:(g + 1) * 64], [], [(nm, hp)], nm + "L")
                          else:
                              gs_ = slice(g, 128, 8)
                              dma("sp", T_[ps, :, :], T_[gs_, :, :], [(nm, g)], [(nm, hp)], nm + "R")
                kv_dmas(((Ks, ck, "Ks"), (None, None, None)))
                FSB = [("fsb", hp) for hp in range(8)]
                ESP = [("esp", hp) for hp in range(8)]

                c0 = CX[0]
                c1x = CX[1]
                norm_T(c0, h_s[0:NS, :], ["h_s"], NS)
                mm_tok(c0, B[2], "B2", 0, 512, NS)
                mm_tok(c0, B[3], "B3", 512, 768, NS)
                mm_tok(c0, B[4], "B4", 768, 1280, NS)
                mm_tok(c0, B[5], "B5", 1280, 1792, NS)
                P.add("act", lambda e: e.activation(out=qs_t[:], in_=B[2][0:NS, :], func=AF.Copy, scale=0.125), r=["B2"], w=["qs_t"])
                P.add("act", lambda e: e.activation(out=kvs[:], in_=B[3][0:NS, 0:256], func=AF.Copy), r=["B3"], w=["kvs"])
                dma("sp", qscr.ap(), qs_t[:], ["qs_t"], ["qscr"], "qscr")
                for t_ in range(2):
                    for g in range(2):
                        c_ = t_ * 128 + g * 64
                        dma("sp", bass.AP(kvscr2, g * 128 + t_ * 64, [[1024, NS], [256, 4], [1, 64]]),
                            kvs[:, c_:c_ + 64].unsqueeze(1).to_broadcast([NS, 4, 64]), ["kvs"], [("kvscr", t_, g)], "kvscr")
                dma("sp", wks.ap()[:, 127, :], kvs[:, 0:128], ["kvs"], ["wks2"], "wkvs")
                dma("sp", wvs.ap()[:, 127, :], kvs[:, 128:256], ["kvs"], ["wvs2"], "wkvs")
                gelu_u(c0, B[4], "B4", NS)
                gelu_ln(c0, B[5], "B5", NS, want_f32=True)
                t1s, gms, ugs = c0.t1, c0.gm, c0.ug
                P.add("dve", lambda e: e.tensor_tensor(out=t1s[0:NS, :].rearrange("p (h d) -> p h d", h=8), in0=vn[0:NS, :].rearrange("p (h d) -> p h d", h=8),
                                                       in1=ws00[:].unsqueeze(2).to_broadcast([NS, 8, 64]), op=ALU.mult), r=["vn", "ws00"], w=[c0.k("t1")])
                P.add("dve", lambda e: e.tensor_tensor(out=t1s[0:NS, :].rearrange("p (h d) -> p h d", h=8), in0=t1s[0:NS, :].rearrange("p (h d) -> p h d", h=8),
                                                       in1=bs0[:].unsqueeze(2).to_broadcast([NS, 8, 64]), op=ALU.add), r=[c0.k("t1"), "bs0"], w=[c0.k("t1")])
                P.add("dve", lambda e: e.tensor_tensor(out=gms[0:NS, :], in0=t1s[0:NS, :], in1=ugs[0:NS, :], op=ALU.mult),
                      r=[c0.k("t1"), c0.k("ug")], w=[c0.k("gm")])
                dma("sp", qn[:], qscr.ap().rearrange("n (h d) -> (n h) d", h=8), ["qscr"], [("qn", hp) for hp in range(8)], "qkn")
                dma("sp", knew[:], kvscr2.ap()[:, 0:64], [("kvscr", 0, 0), ("kvscr", 0, 1)], [("knew", hp) for hp in range(8)], "qkn")
                dma("sp", vnew[:], kvscr2.ap()[:, 64:128], [("kvscr", 1, 0), ("kvscr", 1, 1)], [("vnew", hp) for hp in range(8)], "qkn")
                kv_dmas(((Vs, cv, "Vs"),))
                for g in range(2):
                    gs_ = slice(g, 128, 8)
                    dma("sp", wks.ap()[:, 0:127, g * 64:(g + 1) * 64], Ks[gs_, 1:128, :], [("Ks", g)], [("wks", g)], "wkvs")
                    dma("sp", wvs.ap()[:, 0:127, g * 64:(g + 1) * 64], Vs[gs_, 1:128, :], [("Vs", g)], [("wvs", g)], "wkvs")
                P.add("act", lambda e: e.activation(out=esp[:], in_=esp[:], func=AF.Exp), ESP, ESP)
                dma("sp", gv.ap(), vn[0:NS, :], ["vn"], ["gv"], "gv")
                KS = [("Ks", hp) for hp in range(8)]
                VS = [("Vs", hp) for hp in range(8)]
                QN = [("qn", hp) for hp in range(8)]
                KN = [("knew", hp) for hp in range(8)]
                VN = [("vnew", hp) for hp in range(8)]
                for ch in range(4):
                    cs_ = slice(ch * 32, (ch + 1) * 32)
                    P.add("dve", lambda e, cs_=cs_: e.tensor_tensor(out=prod[:], in0=Ks[:, cs_, :], in1=qn[:].unsqueeze(1).to_broadcast([128, 32, 64]),
                                                                    op=ALU.mult), r=KS + QN, w=["prod"])
                    P.add("dve", lambda e, cs_=cs_: e.reduce_sum(out=lg[:, cs_], in_=prod[:], axis=AX.X), r=["prod"], w=["lg"])
                P.add("dve", lambda e: e.tensor_tensor(out=prod[:, 0, :], in0=knew[:], in1=qn[:], op=ALU.mult), r=KN + QN, w=["prod"])
                P.add("dve", lambda e: e.reduce_sum(out=lg[:, 128:129], in_=prod[:, 0, :], axis=AX.X), r=["prod"], w=["lg"])
                P.add("dve", lambda e: e.tensor_tensor(out=lg[:], in0=lg[:], in1=fsb[:], op=ALU.add), r=["lg"] + FSB, w=["lg"])
                P.add("act", lambda e: e.activation(out=lg[:], in_=lg[:], func=AF.Exp, accum_out=dsum[:, 0:1]), r=["lg"], w=["lg", "dsum"])
                P.add("dve", lambda e: e.tensor_tensor(out=dsum[:, 1:2], in0=dsum[:, 0:1], in1=esp[:], op=ALU.add), r=["dsum"] + ESP, w=["dsum1"])
                P.add("dve", lambda e: e.reciprocal(out=dsum[:, 2:3], in_=dsum[:, 1:2]), r=["dsum1"], w=["dsum2"])
                for ch in range(4):
                    cs_ = slice(ch * 32, (ch + 1) * 32)
                    P.add("dve", lambda e, cs_=cs_: e.tensor_tensor(out=prod[:], in0=Vs[:, cs_, :], in1=lg[:, cs_].unsqueeze(2).to_broadcast([128, 32, 64]),
                                                                    op=ALU.mult), r=VS + ["lg"], w=["prod"])
                    P.add("dve", lambda e, ch=ch: e.reduce_sum(out=osum[:, ch, :], in_=prod[:].rearrange("p c d -> p d c"), axis=AX.X),
                          r=["prod"], w=[("osum", ch)])
                P.add("dve", lambda e: e.tensor_scalar(out=osum[:, 4, :], in0=vnew[:], scalar1=lg[:, 128:129], scalar2=None, op0=ALU.mult),
                      r=VN + ["lg"], w=[("osum", 4)])
                P.add("dve", lambda e: e.reduce_sum(out=onh[:], in_=osum[:].rearrange("p k d -> p d k"), axis=AX.X),
                      r=[("osum", k) for k in range(5)], w=["onh"])
                P.add("dve", lambda e: e.tensor_scalar(out=onh[:], in0=onh[:], scalar1=dsum[:, 2:3], scalar2=None, op0=ALU.mult),
                      r=["onh", "dsum2"], w=["onh"])
                P.add("dve", lambda e: e.memset(dsum[:, 3:4], 0.0), r=KS + VS + ["onh"], w=[("h", j) for j in range(1, 17)])
                for g in range(2):
                    dma("sp", bass.AP(ascr, g * 256, [[512, NS], [64, 4], [1, 64]]), onh[g:128:2, :], ["onh"], [("ascr", g)], "ascr")
                dma("sp", c0.attn[0:NS, :], ascr.ap().rearrange("(n h) d -> n (h d)", h=8), [("ascr", 0), ("ascr", 1)], [("attn_s", 0)], "attn_s")
                AK = [("attn_s", 0)]
                dma("sp", ohs[:], oh.ap(), [], ["ohs"], "cst2")
                dma("sp", mkc[:], maskc.ap(), [], ["mkc"], "cst2")
                dma("sp", mvb[:], mv.ap()[0:1, :].partition_broadcast(128), [], ["mvb"], "cst2")
                for h in range(8):
                    gt_ = gbt[h % 2]
                    gk_ = ("gbt", h % 2)
                    P.add("pe", lambda e, h=h: e.matmul(B[7][:, 0:384], lhsT=rb[:, h:h + 1].to_broadcast([32, 128]), rhs=ohs[:], start=True, stop=True),
                          ["rb", "ohs"], ["B7"])
                    P.add("dve", lambda e, gt_=gt_: e.tensor_tensor(out=gt_[:], in0=B[7][:, 0:384], in1=mvb[:], op=ALU.add), ["B7", "mvb"], [gk_])
                    dma("sp", scr.ap()[h], gt_[:], [gk_], [("scr", h)], "scr")
                for kb, off in ((0, 255), (1, 127)):
                    src = bass.AP(scr, off, [[383, 128], [128 * 384, 8], [1, 128]])
                    dma("pool", biasT[:, kb, :].rearrange("p (h q) -> p h q", h=8), src, [("scr", h) for h in range(8)], [("biasT", kb)], "biasT")
                P.add("dve", lambda e: e.tensor_scalar(out=bias1[:], in0=biasT[:, 0, :], scalar1=mkc[:, 0:1], scalar2=None, op0=ALU.add),
                      [("biasT", 0), "mkc"], ["bias1"])
                out_proj(c0, h_s[0:NS, :], ["h_s"], h_s[0:NS, :], ["h_s"], NS, akeys=AK)
                dma("sp", h_all[:, 16, :], xh.ap()[0:128, :], [], [("h", 17)], "x0")
                dma("sp", esink[:], sinks.ap().partition_broadcast(128), [], ["esink"], "cst2")
                dma("sp", trl[:], trilT.ap(), [], ["trl"], "cst2")
                dma("sp", bs_t[:], b_s.ap().rearrange("h t -> t h"), [], ["bs_t"], "cst2", slow=True)
                dma("pool", wsl[:], w_s.ap().rearrange("h t s -> t h s"), [], ["wsl"], "wsl")
                P.add("act", lambda e: e.activation(out=esink[:], in_=esink[:], func=AF.Exp), ["esink"], ["esink"])

                def trw(e):
                    for h in range(8):
                        ins = e.transpose(out=B[0][:, h * 128:(h + 1) * 128], in_=wsl[:, h, :], identity=identb[:])
                    return ins
                P.add("pe", trw, ["wsl", "identb"], ["B0"])
                for h in range(8):
                    P.add("dve", lambda e, h=h: e.tensor_tensor(out=wsT[:, h, :], in0=B[0][:, h * 128:(h + 1) * 128], in1=trl[:], op=ALU.mult),
                          ["B0", "trl"], ["wsT"])
                P.emit()
            if MAXPH < 2:
                return nc

            sA = ExitStack()
            with sA:
                kT_all = sb(sA, "kT_all", (128, NBLK, 128), BF16)
                v_all = sb(sA, "v_all", (128, NBLK, 2, 72), BF16)
                qT = [sb(sA, "qT%d" % t, (128, 512), BF16) for t in range(2)]
                PT = [sb(sA, "PT%d" % t, (128, 2, 1024), BF16) for t in range(2)]
                kvo = sb(sA, "kvo", (128, 256))

                for j in range(1, NBLK - 1):
                    dma("sp", h_all[:, j - 1, :], xh.ap()[j * 128:(j + 1) * 128, :], [], [("h", j)], "x%d" % j)
                XLAST = [True]
                P.add("dve", lambda e: e.memset(v_all[:], 1.0), [], [("v", j) for j in range(NBLK)])
                if DBG == 1:
                    P.emit()
                    return nc
                if DBG == 3:
                    P.emit()
                    return nc
                if DBG == 4:
                    P.emit()
                    return nc

                def stageA(j):
                    cx = CX[j % 2]
                    xap = h_all[:, 16, :] if j == 0 else h_all[:, j - 1, :]
                    xkeys = [("h", 17)] if j == 0 else [("h", j)]
                    norm_T(cx, xap, xkeys)
                    if j >= 1:
                        def fq(e):
                            for t in range(4):
                                for kc in range(8):
                                    ins = e.matmul(cx.a[:, t * 128:(t + 1) * 128], lhsT=W_in[:, kc, t * 128:(t + 1) * 128], rhs=cx.xnT[:, kc, :],
                                                   start=(kc == 0), stop=(kc == 7))
                            return ins
                        P.add("pe", fq, r=[cx.k("xnT")] + WIN, w=[cx.ak])
                        P.add("dve", lambda e: e.scalar_tensor_tensor(out=qT[cx.t][:].rearrange("p (t q) -> p t q", t=4),
                                                                      in0=cx.a[:].rearrange("p (t q) -> p t q", t=4), scalar=0.125,
                                                                      in1=bq8[:].unsqueeze(2).to_broadcast([128, 4, 128]),
                                                                      op0=ALU.mult, op1=ALU.add), r=[cx.ak, "bq8"], w=[cx.k("qT")])

                    def fk(e):
                        for kc in range(8):
                            ins = e.matmul(cx.b[:, 0:128], lhsT=W_in[:, kc, 512:640], rhs=cx.xnT[:, kc, :], start=(kc == 0), stop=(kc == 7))
                        return ins
                    P.add("pe", fk, r=[cx.k("xnT")] + WIN, w=[cx.bk])
                    mm_tok(cx, cx.b, cx.bk, 512, 768, o0=128)
                    P.add("dve", lambda e: e.tensor_scalar(out=kT_all[:, j, :], in0=cx.b[:, 0:128], scalar1=bqk[:, 4:5], scalar2=None, op0=ALU.add),
                          r=[cx.bk, "bqk"], w=[("k", j)])
                    for g in range(2):
                        P.add("act", lambda e, g=g: e.activation(out=v_all[:, j, g, 0:64], in_=cx.b[:, 256 + g * 64:320 + g * 64],
                                                                 func=AF.Identity), r=[cx.bk], w=[("v", j)])
                    if j == NBLK - 1:
                        P.add("act", lambda e: e.activation(out=kvo[:], in_=cx.b[:, 128:384], func=AF.Copy), r=[cx.bk], w=["kvo"])
                        dma("sp", wkv.ap(), kvo[:], ["kvo"], ["wkv"], "wkv")
                    if j == 0:
                        dma("sp", h_all[:, 16, :], xh.ap()[17 * 128:18 * 128, :], [], [("h", 17)], "x17")
                        return
                    mm_tok(cx, cx.c, cx.ck, 768, 1280)
                    gelu_u(cx, cx.c, cx.ck)
                    mm_tok(cx, cx.c, cx.ck, 1280, 1792)
                    gelu_ln(cx, cx.c, cx.ck)

                def stageB(j):
                    cx = CX[j % 2]
                    PTt = PT[cx.t]
                    for g in range(2):
                        for kb in range(2):
                            jb = j - 1 + kb
                            bank, bkey = (cx.b, cx.bk) if kb == 0 else (cx.c, cx.ck)
                            bt = (bias1[:, g * 512:(g + 1) * 512] if (kb == 0 and j <= 2) else biasT[:, kb, g * 512:(g + 1) * 512])
                            btk = "bias1" if (kb == 0 and j <= 2) else ("biasT", kb)

                            def fs(e, g=g, jb=jb, bank=bank, bt=bt):
                                e.matmul(bank[:], lhsT=kT_all[g * 64:(g + 1) * 64, jb, :], rhs=qT[cx.t][g * 64:(g + 1) * 64, :], start=True, stop=False)
                                return e.matmul(bank[:], lhsT=identb[:], rhs=bt, start=False, stop=True)
                            P.add("pe", fs, r=[("k", jb), cx.k("qT"), "identb", btk], w=[bkey])
                            P.add("act", lambda e, g=g, kb=kb, bank=bank: e.activation(out=PTt[:, kb, g * 512:(g + 1) * 512], in_=bank[:], func=AF.Exp),
                                  r=[bkey], w=[cx.k(("PT", kb, g))])
                    for half, bank, bkey in ((0, cx.a, cx.ak), (1, cx.b, cx.bk)):
                        def fpv(e, half=half, bank=bank):
                            for hh in range(4):
                                h = half * 4 + hh
                                for kb in range(2):
                                    ins = e.matmul(bank[:, hh * 128:hh * 128 + 65], lhsT=PTt[:, kb, h * 128:(h + 1) * 128],
                                                   rhs=v_all[:, j - 1 + kb, half, 0:65], start=(kb == 0), stop=(kb == 1))
                            return ins
                        P.add("pe", fpv, r=[cx.k(("PT", 0, half)), cx.k(("PT", 1, half)), ("v", j - 1), ("v", j)], w=[bkey])
                        bv = bank[:].rearrange("p (h c) -> p h c", h=4)
                        hs = slice(half * 4, half * 4 + 4)
                        den = cx.den
                        dk = cx.k(("den", half))
                        P.add("dve", lambda e, bv=bv, hs=hs, den=den: e.tensor_tensor(out=den[:, hs].unsqueeze(2), in0=bv[:, :, 64:65],
                                                                                      in1=esink[:, hs].unsqueeze(2), op=ALU.add),
                              r=[bkey, "esink"], w=[dk])
                        P.add("dve", lambda e, hs=hs, den=den: e.reciprocal(out=den[:, hs], in_=den[:, hs]), r=[dk], w=[dk])
                        P.add("dve", lambda e, bv=bv, hs=hs, half=half, den=den: e.tensor_tensor(
                            out=cx.attn[:, half * 256:(half + 1) * 256].rearrange("p (h d) -> p h d", h=4), in0=bv[:, :, 0:64],
                            in1=den[:, hs].unsqueeze(2).to_broadcast([128, 4, 64]), op=ALU.mult),
                            r=[bkey, dk], w=[cx.k("attn")])

                    def fg(e):
                        for h in range(8):
                            ins = e.matmul(cx.c[:, h * 64:(h + 1) * 64], lhsT=wsT[:, h, :], rhs=cx.vnb[:, h * 64:(h + 1) * 64], start=True, stop=True)
                        return ins
                    P.add("pe", fg, r=["wsT", cx.k("vnb")], w=[cx.ck])
                    P.add("dve", lambda e: e.tensor_tensor(out=cx.t1[:].rearrange("p (h d) -> p h d", h=8), in0=cx.c[:].rearrange("p (h d) -> p h d", h=8),
                                                           in1=bs_t[:].unsqueeze(2).to_broadcast([128, 8, 64]), op=ALU.add),
                          r=[cx.ck, "bs_t"], w=[cx.k("t1")])
                    P.add("dve", lambda e: e.tensor_tensor(out=cx.gm[:], in0=cx.t1[:], in1=cx.ug[:], op=ALU.mult),
                          r=[cx.k("t1"), cx.k("ug")], w=[cx.k("gm")])
                    out_proj(cx, h_all[:, j - 1, :], [("h", j)], h_all[:, j - 1, :], [("h", j)])

                NB = min(NBLK, MAXJ)
                stageA(0)
                if NB > 1:
                    stageA(1)
                for j in range(1, NB):
                    opsB = P.record(lambda: stageB(j))
                    opsA = P.record(lambda: stageA(j + 1)) if j + 1 < NB else []
                    P.merge_sched(opsB, opsA)
                P.emit()
        if MAXPH < 3:
            return nc

        sC = ExitStack()
        with sC:
            W_down = sb(sC, "W_down", (128, 22, D), BF16)
            Wup = [sb(sC, "Wup%d" % i, (128, 8, 256), BF16) for i in range(NWS)]
            gpf = sb(sC, "gpf", (128, D))
            gof = sb(sC, "gof", (128, D))
            identf = sb(sC, "identf", (128, 128))
            cwl = sb(sC, "cwl_s", (NFT, 4, 128))
            cwT = sb(sC, "cwT", (128, 4, NFT))
            flg = sb(sC, "flg", (128, 1))
            h2 = sb(sC, "h2", (128, D), BF16)
            h2T = sb(sC, "h2T", (128, 8, 512), BF16)
            h2Th = sb(sC, "h2Th", (128, 8, 2), BF16)
            h2Ts = sb(sC, "h2Ts", (128, 8, NS), BF16)
            cbuf = [sb(sC, "cbuf%d" % i, (128, 512)) for i in range(NCB)]
            actT = sb(sC, "actT", (128, 22, 512), BF16)
            tails = sb(sC, "tails", (128, NFT, 2))
            upx = [sb(sC, "upx%d" % i, (128, 514)) for i in range(NUX)]
            tpo = sb(sC, "tpo", (88, 128))
            cwr = sb(sC, "cwr_s", (NS, 4, 256))
            ups_b = [h_all[0:NS, 0, 0:256], h_all[0:NS, 0, 256:512]]
            sts = h_all[0:NS, 0, 512:1024].rearrange("p (r c) -> p r c", r=2)
            cs_t = sb(sC, "cs_t", (NS, 256))
            gs_t = sb(sC, "gs_t", (NS, 128))
            act_s = sb(sC, "act_s", (NS, DFF), BF16)
            actTs = sb(sC, "actTs", (128, 22, NS), BF16)

            w_down_v = w_down.ap().rearrange("(kc p) n -> p kc n", p=128)
            dma("sp", gpf[:], g_pre_ffn.ap().partition_broadcast(128), [], ["gpf"], "cst3")
            dma("sp", gof[:], g_post_ffn.ap().partition_broadcast(128), [], ["gof"], "cst3")
            dma("sp", identf[:], ident.ap(), [], ["identf"], "cst3")
            dma("sp", cwl[:], cwl_d.ap(), [], ["cwl"], "cst3")
            dma("sp", flg[:], flag.ap(), [], ["flg"], "cst3")
            dma("sp", cs.ap()[:, 0, :], stt.ap()[:, 1, :], [], ["cs0"], "cs0")
            w_up_v = w_up.ap().rearrange("(kc p) n -> p kc n", p=128)
            wup_seq = [(G, p) for G in range(4) for p in range(22)]

            def load_wup(i):
                G, p = wup_seq[i]
                s = i % NWS
                dma("pool", Wup[s][:], w_up_v[:, :, p * 256:(p + 1) * 256], [], [("Wup", s)], "Wup%d" % s)
            for i in range(NWS - 1):
                load_wup(i)
            for i in range(2):
                dma("pool", W_down[:, 11 * i:11 * i + 11, :], w_down_v[:, 11 * i:11 * i + 11, :], [], ["W_down%d" % i], "W_down")
            WDN = ["W_down0", "W_down1"]

            def trc(e):
                for r_ in range(4):
                    ins = e.transpose(out=B[7][:, r_ * NFT:(r_ + 1) * NFT], in_=cwl[:, r_, :], identity=identf[0:NFT, 0:NFT])
                return ins
            P.add("pe", trc, ["cwl", "identf"], ["B7"])
            P.add("dve", lambda e: e.tensor_copy(out=cwT[:].rearrange("p r f -> p (r f)"), in_=B[7][:, 0:4 * NFT]), ["B7"], ["cwT"])

            def ffn_norm_T(xap, xkeys, dst, dkey, np_=128, c0=0, slot=0, pre=None):
                rstd, rk_ = pre if pre is not None else rms_rstd([(xap, xkeys)], D, "f", np_, slot=slot)
                P.add("dve", lambda e: e.scalar_tensor_tensor(out=h2[0:np_, :], in0=xap, scalar=rstd, in1=gpf[0:np_, :],
                                                              op0=ALU.mult, op1=ALU.mult), r=list(xkeys) + [rk_, "gpf"], w=["h2"])

                def tr(e):
                    for c in range(8):
                        ins = e.transpose(out=B[0][:, c * 128:c * 128 + np_], in_=h2[0:np_, c * 128:(c + 1) * 128], identity=identb[0:np_, 0:np_])
                    return ins
                P.add("pe", tr, r=["h2", "identb"], w=["B0"])
                B0v = B[0][:].rearrange("p (c t) -> p c t", c=8)
                P.add("act", lambda e: e.activation(out=dst, in_=B0v[:, :, c0:np_], func=AF.Copy), r=["B0"], w=[dkey])

            ffn_norm_T(h_all[:, 0, :], [("h", 1)], h2Th[:], "h2Th", c0=126)
            ffn_norm_T(h_s[0:NS, :], ["h_s"], h2Ts[:], "h2Ts", NS)
            P.add("dve", lambda e: e.memset(ssB[:, 0:1], 0.0), r=[("h", 1)], w=[("ups", 0), ("ups", 1), "sts"])

            wi = 0
            cslot = 0
            for G in range(4):
                par, ppar = G % 2, (G + 1) % 2
                if G == 0:
                    pres = [rms_rstd([(h_all[:, 1 + tb, :], [("h", 2 + tb)])], D, "f", slot=tb) for tb in range(4)]
                else:
                    pres = pres_next
                if G == 0:
                    for tb in range(4):
                        ffn_norm_T(h_all[:, 1 + tb, :], [("h", 2 + tb)], h2T[:, :, tb * 128:(tb + 1) * 128], ("h2T", tb), pre=pres[tb])
                H2T = [("h2T", tb) for tb in range(4)]
                prev_side = []
                for p in range(22):
                    s = wi % NWS
                    if G == 0:
                        P.rec = []
                    if wi + NWS - 1 < len(wup_seq):
                        load_wup(wi + NWS - 1)
                    wi += 1
                    cts = []
                    ups = ups_b[p % 2]
                    upk = ("ups", p % 2)
                    if G == 0:
                        def fb7(e, s=s):
                            for t2 in range(2):
                                for kc in range(8):
                                    e.matmul(B[7][:, 2 * t2:2 * t2 + 2], lhsT=Wup[s][:, kc, t2 * 128:(t2 + 1) * 128], rhs=h2Th[:, kc, :],
                                             start=(kc == 0), stop=(kc == 7))
                            for kc in range(8):
                                ins = e.matmul(B[7][0:NS, 256:512], lhsT=h2Ts[:, kc, :], rhs=Wup[s][:, kc, :], start=(kc == 0), stop=(kc == 7))
                            return ins
                        P.add("pe", fb7, r=[("Wup", s), "h2Ts", "h2Th"], w=["B7"])
                        P.add("act", lambda e, ups=ups: e.activation(out=ups[:], in_=B[7][0:NS, 256:512], func=AF.Copy), r=["B7"], w=[upk])
                        P.add("act", lambda e, p=p: e.activation(out=tails[:, 2 * p:2 * p + 2, :], in_=B[7][:, 0:4].rearrange("q (a b) -> q a b", a=2),
                                                                 func=AF.Copy, scale=flg[:, 0:1]),
                              r=["B7", "flg"], w=[("tails", 2 * p), ("tails", 2 * p + 1)])
                    for t2 in range(2):
                        ft = p * 2 + t2
                        bi = 2 + (cslot % 4)

                        def fup(e, s=s, t2=t2, bi=bi):
                            for kc in range(8):
                                ins = e.matmul(B[bi][:], lhsT=Wup[s][:, kc, t2 * 128:(t2 + 1) * 128], rhs=h2T[:, kc, :], start=(kc == 0), stop=(kc == 7))
                            return ins
                        P.add("pe", fup, r=[("Wup", s)] + H2T, w=["B%d" % bi])
                        c = cbuf[cslot % NCB]
                        ck_ = ("cbuf", cslot % NCB)
                        ux = upx[cslot % NUX]
                        uk = ("upx", cslot % NUX)
                        uhk = ("upxh", cslot % NUX)
                        cslot += 1
                        cts.append((c, ck_))
                        w0, w1, w2, bb = (cwT[:, r_, ft:ft + 1] for r_ in range(4))
                        P.add("act", lambda e, c=c, bi=bi, w2=w2, bb=bb: e.activation(out=c[:], in_=B[bi][:], func=AF.Identity, scale=w2, bias=bb),
                              r=["B%d" % bi, "cwT"], w=[ck_])
                        P.add("act", lambda e, ux=ux, bi=bi: e.activation(out=ux[:, 2:514], in_=B[bi][:], func=AF.Copy),
                              r=["B%d" % bi], w=[uk])
                        P.add("pool", lambda e, ux=ux, ft=ft: e.tensor_copy(out=ux[:, 0:2], in_=tails[:, ft, :]), r=[("tails", ft)], w=[uhk])
                        P.add("act", lambda e, bi=bi, ft=ft: e.activation(out=tails[:, ft, :], in_=B[bi][:, 510:512], func=AF.Copy),
                              r=["B%d" % bi], w=[("tails", ft)])
                        P.add("dve", lambda e, c=c, ux=ux, w1=w1: e.scalar_tensor_tensor(out=c[:], in0=ux[:, 1:513], scalar=w1, in1=c[:],
                                                                                         op0=ALU.mult, op1=ALU.add), r=[uk, uhk, ck_, "cwT"], w=[ck_])
                        P.add("dve", lambda e, c=c, ux=ux, w0=w0: e.scalar_tensor_tensor(out=c[:], in0=ux[:, 0:512], scalar=w0, in1=c[:],
                                                                                         op0=ALU.mult, op1=ALU.add), r=[uk, uhk, ck_, "cwT"], w=[ck_])
                    (cg, cgk), (cv_, cvk) = cts
                    P.add("act", lambda e, cg=cg: e.activation(out=cg[:], in_=cg[:], func=AF.Gelu_apprx_tanh), r=[cgk], w=[cgk])
                    P.add("dve", lambda e, cg=cg, cv_=cv_, p=p: e.tensor_tensor(out=actT[:, p, :], in0=cg[:], in1=cv_[:], op=ALU.mult),
                          r=[cgk, cvk], w=[("actT", p)])
                    if G == 0:
                        main_ops, P.rec = P.rec, []
                        for t2 in range(2):
                            ft = p * 2 + t2
                            ocol = (ft % 2) * DFF + (ft // 2) * 128
                            dma("sp", cs.ap()[:, 1, ocol:ocol + 128], ups[:, t2 * 128:(t2 + 1) * 128], [upk], [("cs1", ft)], "cs1_%d" % t2)
                        c0_ = p * 256
                        dma("sp", cwr[:], bass.AP(cwr_d, c0_, [[0, NS], [F2, 4], [1, 256]]), [], ["cwr"], "cwr")
                        dma("sp", sts[:], stp.ap()[:, :, c0_:c0_ + 256], [], ["sts"], "sts")
                        P.add("dve", lambda e, ups=ups: e.tensor_tensor(out=cs_t[:], in0=ups[:], in1=cwr[:, 2, :], op=ALU.mult), r=[upk, "cwr"], w=["cs_t"])
                        P.add("dve", lambda e: e.tensor_tensor(out=cs_t[:], in0=cs_t[:], in1=cwr[:, 3, :], op=ALU.add), r=["cs_t", "cwr"], w=["cs_t"])
                        for r_ in range(2):
                            P.add("dve", lambda e, r_=r_: e.tensor_tensor(out=sts[:, r_, :], in0=sts[:, r_, :], in1=cwr[:, r_, :], op=ALU.mult),
                                  r=["sts", "cwr"], w=["sts"])
                            P.add("dve", lambda e, r_=r_: e.tensor_tensor(out=cs_t[:], in0=cs_t[:], in1=sts[:, r_, :], op=ALU.add),
                                  r=["cs_t", "sts"], w=["cs_t"])
                        P.add("act", lambda e: e.activation(out=gs_t[:], in_=cs_t[:, 0:128], func=AF.Gelu_apprx_tanh), r=["cs_t"], w=["gs_t"])
                        P.add("dve", lambda e, p=p: e.tensor_tensor(out=act_s[:, p * 128:(p + 1) * 128], in0=gs_t[:], in1=cs_t[:, 128:256], op=ALU.mult),
                              r=["gs_t", "cs_t"], w=["act_s"])
                        side_ops, P.rec = P.rec, None
                        P.merge_sched(main_ops, prev_side)
                        prev_side = side_ops
                if G == 0:
                    P.merge_sched(prev_side)
                if G < 3:
                    pres_next = [rms_rstd([(h_all[:, 1 + 4 * (G + 1) + tb, :], [("h", 2 + 4 * (G + 1) + tb)])], D, "f", slot=tb) for tb in range(4)]
                ACT_ALL = [("actT", p) for p in range(22)]
                P.rec = []
                for tb in range(4):
                    dbk = ((0, 6), (1, 7)) if tb % 2 == 0 else ((0, 2), (1, 3))
                    for half, bi in dbk:
                        def fdn(e, tb=tb, half=half, bi=bi):
                            for p in range(22):
                                ins = e.matmul(B[bi][:], lhsT=actT[:, p, tb * 128:(tb + 1) * 128], rhs=W_down[:, p, half * 512:(half + 1) * 512],
                                               start=(p == 0), stop=(p == 21))
                            return ins
                        P.add("pe", fdn, r=ACT_ALL + WDN, w=["B%d" % bi])
                    blk = 4 * G + tb
                    hblk = h_all[:, 1 + blk, :]
                    hk = ("h", 2 + blk)
                    rstd, rk_ = rms_rstd([(B[bi_][:], ["B%d" % bi_]) for (_, bi_) in dbk], D, "y", slot=4 + tb % 2)
                    for half, bi in dbk:
                        sl = slice(half * 512, (half + 1) * 512)
                        P.add("dve", lambda e, bi=bi, sl=sl, rstd=rstd: e.scalar_tensor_tensor(out=B[bi][:], in0=B[bi][:], scalar=rstd, in1=gof[:, sl],
                                                                                   op0=ALU.mult, op1=ALU.mult),
                              r=["B%d" % bi, rk_, "gof"], w=["B%d" % bi])
                        P.add("dve", lambda e, bi=bi, sl=sl, hblk=hblk: e.tensor_tensor(out=hblk[:, sl], in0=B[bi][:], in1=hblk[:, sl], op=ALU.add),
                              r=["B%d" % bi, hk], w=[hk])
                    dma("sp", y.ap()[blk * 128:(blk + 1) * 128, :], hblk, [hk], [("y", blk)], "y%d" % (blk % 4))
                s1_ops, P.rec = P.rec, []
                if G < 3:
                    for tb in range(4):
                        ffn_norm_T(h_all[:, 1 + 4 * (G + 1) + tb, :], [("h", 2 + 4 * (G + 1) + tb)], h2T[:, :, tb * 128:(tb + 1) * 128],
                                   ("h2T", tb), pre=pres_next[tb])
                s2_ops, P.rec = P.rec, None
                P.merge_sched(s1_ops, s2_ops)
                if G == 0:
                    def trs(e):
                        for p in range(22):
                            ins = e.transpose(out=B[0][:, p * NS:(p + 1) * NS], in_=act_s[:, p * 128:(p + 1) * 128], identity=identb[0:NS, 0:NS])
                        return ins
                    P.add("pe", trs, r=["act_s", "identb"], w=["B0"])
                    P.add("act", lambda e: e.activation(out=actTs[:].rearrange("p a n -> p (a n)"), in_=B[0][:, 0:22 * NS], func=AF.Copy),
                          r=["B0"], w=["actTs"])
                    for half, bi in ((0, 5), (1, 6)):
                        def fds(e, half=half, bi=bi):
                            for p in range(22):
                                ins = e.matmul(B[bi][0:NS, :], lhsT=actTs[:, p, :], rhs=W_down[:, p, half * 512:(half + 1) * 512],
                                               start=(p == 0), stop=(p == 21))
                            return ins
                        P.add("pe", fds, r=["actTs"] + WDN, w=["B%d" % bi])
                    rstd, rk_ = rms_rstd([(B[5][0:NS, :], ["B5"]), (B[6][0:NS, :], ["B6"])], D, "ys", NS, slot=6)
                    for half, bi in ((0, 5), (1, 6)):
                        sl = slice(half * 512, (half + 1) * 512)
                        P.add("dve", lambda e, bi=bi, sl=sl, rstd=rstd: e.scalar_tensor_tensor(out=B[bi][0:NS, :], in0=B[bi][0:NS, :], scalar=rstd, in1=gof[0:NS, sl],
                                                                                   op0=ALU.mult, op1=ALU.mult), r=["B%d" % bi, rk_, "gof"], w=["B%d" % bi])
                        P.add("dve", lambda e, bi=bi, sl=sl: e.tensor_tensor(out=h_s[0:NS, sl], in0=B[bi][0:NS, :], in1=h_s[0:NS, sl], op=ALU.add),
                              r=["B%d" % bi, "h_s"], w=["h_s"])
                    dma("sp", ys.ap(), h_s[0:NS, :], ["h_s"], ["ys"], "ys")
            TL = [("tails", ft) for ft in range(NFT)]
            P.add("pe", lambda e: e.transpose(out=B[7][0:88, 0:128], in_=tails[:].rearrange("p f t -> p (f t)"), identity=identf[:]),
                  r=TL + ["identf"], w=["B7"])
            P.add("dve", lambda e: e.tensor_copy(out=tpo[:], in_=B[7][0:88, 0:128]), ["B7"], ["tpo"])
            dma("sp", cp.ap(), tpo[:], ["tpo"], ["cp"], "cp")
            P.emit()
    return nc


def _t5_bucket_np(n):
    n = np.maximum(n, 0)
    max_exact = 16
    nf = np.maximum(n, 1).astype(np.float32)
    large = max_exact + (np.log(nf / max_exact) / np.log(128 / max_exact) * (32 - max_exact)).astype(np.int32)
    large = np.minimum(large, 31)
    return np.where(n < max_exact, n, large)


_NC_CACHE = {}


def _prep(x_prompt, x_sample, cache_win_k, cache_win_v, state_ffn_conv, rel_bias,
          w_in, b_in, attn_sinks, gmlp_ln_g, gmlp_ln_b, gmlp_w_s, gmlp_b_s,
          g_attn_out, g_gmlp_out, w_out, g_pre_mix, g_post_mix, g_pre_ffn, g_post_ffn,
          w_up, ffn_conv_w, ffn_conv_b, w_down):
    f32 = np.float32
    A = lambda a: np.ascontiguousarray(np.asarray(a, dtype=f32))
    x_prompt, x_sample = A(x_prompt), A(x_sample)
    qperm = np.array([(half * 4 + t) * 64 + d for t in range(4) for half in range(2) for d in range(64)])
    in_perm = np.concatenate([qperm, np.arange(512, 1792)])
    w_in_p = A(np.asarray(w_in)[0][:, in_perm])
    b_in_p = A(np.asarray(b_in)[0][in_perm][None, :])
    up_perm = np.array([(ft % 2) * DFF + (ft // 2) * 128 + i for ft in range(NFT) for i in range(128)])
    w_up_p = A(np.asarray(w_up)[0][:, up_perm])
    cw4 = np.concatenate([np.asarray(ffn_conv_w)[0], np.asarray(ffn_conv_b)[0][None, :]], axis=0)[:, up_perm]
    cwr = A(cw4)
    cwl = A(cw4.reshape(4, NFT, 128).transpose(1, 0, 2))
    hperm = np.array([(hp % 2) * 4 + hp // 2 for hp in range(8)])
    sinks = A(np.asarray(attn_sinks)[0][None, :])
    sinks_p = A(np.tile(np.asarray(attn_sinks)[0][hperm], NS)[:, None])
    g_ao = A(np.concatenate([np.asarray(g_attn_out)[0], np.asarray(g_gmlp_out)[0]])[None, :])
    ident = np.eye(128, dtype=f32)
    dist = np.arange(383) - 127
    bk = _t5_bucket_np(dist)
    ok = (dist >= 0) & (dist <= 128)
    oh = np.zeros((32, 384), f32)
    oh[bk[ok], np.nonzero(ok)[0]] = 1.0
    mv = np.zeros((8, 384), f32)
    mv[:, :383][:, ~ok] = NEG
    mv[:, 383] = NEG
    oh2 = np.zeros((32, 129), f32)
    oh2[_t5_bucket_np(128 - np.arange(129)), np.arange(129)] = 1.0
    trilT = np.triu(np.ones((128, 128), f32))

    ck = A(cache_win_k)[0].reshape(128, 128, 128)
    cv = A(cache_win_v)[0].reshape(128, 128, 128)
    st = A(state_ffn_conv)[0]
    stp = np.ascontiguousarray(st[:, :, up_perm])

    shared = dict(rel_bias=A(rel_bias), w_in=w_in_p, b_in=b_in_p, sinks=sinks, sinks_p=sinks_p,
                  ln_g=A(gmlp_ln_g), ln_b=A(gmlp_ln_b), w_s=A(gmlp_w_s)[0], b_s=A(gmlp_b_s)[0], g_ao=g_ao,
                  w_out=A(w_out)[0], g_pre_mix=A(g_pre_mix), g_post_mix=A(g_post_mix), g_pre_ffn=A(g_pre_ffn),
                  g_post_ffn=A(g_post_ffn), w_up=w_up_p, cwl=cwl, cwr=cwr, w_down=A(w_down)[0],
                  ident=ident, oh=oh, oh2=oh2, mv=mv, trilT=trilT)
    in_maps = []
    for c in range(NCORES):
        b, part = c // 4, c % 4
        s0 = part * TOK
        xh = np.zeros((NBLK * 128, D), f32)
        lo = s0 - 256
        if lo >= 0:
            xh[:] = x_prompt[b, lo:s0 + TOK]
        else:
            xh[256:] = x_prompt[b, 0:TOK]
        m = dict(shared)
        m.update(xh=xh, xs=np.ascontiguousarray(x_sample[c * NS:(c + 1) * NS, 0, :]),
                 ck=np.ascontiguousarray(ck[c * NS:(c + 1) * NS]), cv=np.ascontiguousarray(cv[c * NS:(c + 1) * NS]),
                 st=np.ascontiguousarray(st[c * NS:(c + 1) * NS]), stp=np.ascontiguousarray(stp[c * NS:(c + 1) * NS]),
                 flag=np.full((128, 1), 0.0 if part == 0 else 1.0, f32),
                 maskc=np.full((128, 1), NEG if part == 0 else 0.0, f32))
        in_maps.append(m)

    return in_maps, up_perm


def _assemble(R, up_perm):
    f32 = np.float32

    y_prompt = np.stack([np.concatenate([R[b * 4 + p]["y"] for p in range(4)], axis=0) for b in range(2)])
    y_sample = np.concatenate([R[c]["ys"] for c in range(NCORES)], axis=0)[:, None, :]
    wk_p = np.stack([R[b * 4 + 3]["wkv"][:, 0:128].reshape(128, 2, 64) for b in range(2)])[None]
    wv_p = np.stack([R[b * 4 + 3]["wkv"][:, 128:256].reshape(128, 2, 64) for b in range(2)])[None]
    wk_s = np.concatenate([R[c]["wks"] for c in range(NCORES)], axis=0).reshape(1, 128, 128, 2, 64)
    wv_s = np.concatenate([R[c]["wvs"] for c in range(NCORES)], axis=0).reshape(1, 128, 128, 2, 64)
    gv_s = np.concatenate([R[c]["gv"] for c in range(NCORES)], axis=0)[None, :, None, :]
    inv = np.argsort(up_perm)
    cps = []
    for b in range(2):
        t = R[b * 4 + 3]["cp"].reshape(NFT, 2, 128).transpose(1, 0, 2).reshape(2, F2)
        cps.append(t[:, inv])
    c_p = np.stack(cps)[None]
    c_s = np.concatenate([R[c]["cs"] for c in range(NCORES)], axis=0)[None]
    outs = (y_prompt, y_sample, wk_p, wv_p, wk_s, wv_s, gv_s, c_p, c_s)
    return tuple(np.ascontiguousarray(o.astype(f32)) for o in outs)


def kernel(**inputs):
    in_maps, up_perm = _prep(**inputs)
    if "nc" not in _NC_CACHE:
        _NC_CACHE["nc"] = build_program()
    nc = _NC_CACHE["nc"]
    res = run_bass_kernel_spmd(nc, in_maps, core_ids=list(range(NCORES)))
    return _assemble(res.results, up_perm)
```
